# Optimizing a Trainium2 kernel written in Bass

```python
import math
import jax, jax.numpy as jnp
from jax import lax
import numpy as np

D_MODEL = 1024
BATCH = 4
SEQ = 8192
DEPTH = 4
DEC_BATCH = 16
DEC_SEQ = 4096
PAST_LEN = 128

N_MIXERS = 4
HEAD_DIM = 64
N_HEADS = 16
DILATIONS = ((128, 1), (512, 4), (2048, 16))
N_DIL = 3
B_KV_HEADS = 4
B_GROUP = N_HEADS // B_KV_HEADS
B_HALF_WINDOW = 128
NUM_BUCKETS = 32
MAX_DISTANCE = 1024
SHORT_CONV = 3
D_INNER = 2 * D_MODEL
SSM_HEAD_DIM = 64
SSM_HEADS = D_INNER // SSM_HEAD_DIM
SSM_GROUPS = 4
HEADS_PER_GROUP = SSM_HEADS // SSM_GROUPS
D_STATE = 128
SSM_CONV = 4
CHUNK = 128
CONV_DIM = D_INNER + 2 * SSM_GROUPS * D_STATE
D_FF = 2816
FFN_CONV = 3
EPS = 1e-6
NEG_INF = -1e30

kernel_name = 'hybrid_bidir_encoder_interleaved'


def rmsnorm(x, g):
    xf = x.astype(jnp.float32)
    y = xf * lax.rsqrt(jnp.mean(xf * xf, axis=-1, keepdims=True) + EPS) * g.astype(jnp.float32)
    return y.astype(x.dtype)


def dwconv(x, w):
    k, c = w.shape
    left = k // 2
    return lax.conv_general_dilated(x, w[:, None, :].astype(x.dtype), window_strides=(1,),
                                    padding=[(left, k - 1 - left)],
                                    dimension_numbers=('NWC', 'WIO', 'NWC'),
                                    feature_group_count=c)


def t5_bucket(rel):
    half = NUM_BUCKETS // 2
    exact = half // 2
    n = np.abs(rel)
    log_ratio = np.log(np.maximum(n, 1) / exact) / math.log(MAX_DISTANCE / exact)
    large = np.minimum(exact + (log_ratio * (half - exact)).astype(np.int64), half - 1)
    return np.where(rel > 0, half, 0) + np.where(n < exact, n, large)


def banded_attention(q, k, v, rel_bias, half, stride, sink=None):
    n, l, hk, g, dh = q.shape
    blk = half
    nb = -(-l // blk)
    pad = nb * blk - l
    qb = jnp.pad(q, ((0, 0), (0, pad), (0, 0), (0, 0), (0, 0))).reshape(n, nb, blk, hk, g, dh)

    def windows(t):
        tp = jnp.pad(t, ((0, 0), (blk, blk + pad), (0, 0), (0, 0))).reshape(n, nb + 2, blk, hk, dh)
        return jnp.concatenate([tp[:, :-2], tp[:, 1:-1], tp[:, 2:]], axis=2)

    kw, vw = windows(k), windows(v)
    rel = np.arange(3 * blk)[None, :] - blk - np.arange(blk)[:, None]
    kpos = np.arange(nb)[:, None] * blk + np.arange(3 * blk)[None, :] - blk
    mask = (np.abs(rel) <= half)[None] & ((kpos >= 0) & (kpos < l))[:, None, :]
    bias = rel_bias[t5_bucket(rel * stride)].astype(jnp.float32)
    bias = bias.reshape(blk, 3 * blk, hk, g).transpose(2, 3, 0, 1)
    s = jnp.einsum('nbqhgd,nbkhd->nbhgqk', qb, kw).astype(jnp.float32) * (HEAD_DIM ** -0.5) + bias
    s = jnp.where(mask[None, :, None, None], s, NEG_INF)
    m = s.max(axis=-1)
    if sink is not None:
        sk = sink.astype(jnp.float32).reshape(hk, g)[None, None, :, :, None]
        m = jnp.maximum(m, sk)
    p = jnp.exp(s - m[..., None])
    denom = p.sum(axis=-1)
    if sink is not None:
        denom = denom + jnp.exp(sk - m)
    o = jnp.einsum('nbhgqk,nbkhd->nbqhgd', p / denom[..., None], vw.astype(jnp.float32))
    lse = (m + jnp.log(denom)).transpose(0, 1, 4, 2, 3)
    o = o.reshape(n, nb * blk, hk, g, dh)[:, :l].astype(q.dtype)
    lse = lse.reshape(n, nb * blk, hk, g)[:, :l]
    return o, lse


def mixer_dilated(h, w_qkv, w_o, rel_bias):
    bsz, s, _ = h.shape
    qkv = (h @ w_qkv).reshape(bsz, s, N_DIL, 3, N_HEADS, HEAD_DIM)
    outs, lses = [], []
    for gi, (window, r) in enumerate(DILATIONS):
        def to_sub(t):
            return t.reshape(bsz, s // r, r, N_HEADS, HEAD_DIM).transpose(0, 2, 1, 3, 4).reshape(bsz * r, s // r, N_HEADS, HEAD_DIM)
        q, k, v = (to_sub(qkv[:, :, gi, j]) for j in range(3))
        o, lse = banded_attention(q[:, :, :, None], k, v, rel_bias, window // (2 * r), r)
        outs.append(o.reshape(bsz, r, s // r, N_HEADS, HEAD_DIM).transpose(0, 2, 1, 3, 4).reshape(bsz, s, N_HEADS, HEAD_DIM))
        lses.append(lse.reshape(bsz, r, s // r, N_HEADS).transpose(0, 2, 1, 3).reshape(bsz, s, N_HEADS))
    alpha = jax.nn.softmax(jnp.stack(lses), axis=0)
    o = jnp.einsum('zbsh,zbshd->bshd', alpha, jnp.stack(outs).astype(jnp.float32))
    return o.reshape(bsz, s, N_HEADS * HEAD_DIM).astype(h.dtype) @ w_o


def mixer_window(h, w_qkv, sink, w_o, rel_bias):
    bsz, s, _ = h.shape
    qkv = h @ w_qkv
    qw, kvw = N_HEADS * HEAD_DIM, B_KV_HEADS * HEAD_DIM
    q = qkv[..., :qw].reshape(bsz, s, B_KV_HEADS, B_GROUP, HEAD_DIM)
    k = qkv[..., qw:qw + kvw].reshape(bsz, s, B_KV_HEADS, HEAD_DIM)
    v = qkv[..., qw + kvw:].reshape(bsz, s, B_KV_HEADS, HEAD_DIM)
    o, _ = banded_attention(q, k, v, rel_bias, B_HALF_WINDOW, 1, sink)
    return o.reshape(bsz, s, qw) @ w_o


def mixer_shortconv(h, w_in, conv_w, w_out):
    bg, cg, xin = jnp.split(h @ w_in, 3, axis=-1)
    return (bg * dwconv(cg * xin, conv_w)) @ w_out


def ssd_chunked(x, adt, bm, cm):
    b, s, g, e, p = x.shape
    n = bm.shape[-1]
    nc = s // CHUNK
    x = x.reshape(b, nc, CHUNK, g, e, p)
    adt = adt.reshape(b, nc, CHUNK, g, e)
    bm = bm.reshape(b, nc, CHUNK, g, n).astype(jnp.float32)
    cm = cm.reshape(b, nc, CHUNK, g, n).astype(jnp.float32)
    a_cs = jnp.cumsum(adt, axis=2)
    a_t = jnp.moveaxis(a_cs, 2, -1)
    seg = a_t[..., :, None] - a_t[..., None, :]
    causal = np.tril(np.ones((CHUNK, CHUNK), dtype=bool))
    lmat = jnp.exp(jnp.where(causal, seg, -jnp.inf))
    cb = jnp.einsum('bclgn,bcsgn->bcgls', cm, bm)
    y_diag = jnp.einsum('bcgels,bcsgep->bclgep', cb[:, :, :, None] * lmat, x)
    decay_states = jnp.exp(a_cs[:, :, -1:] - a_cs)
    states = jnp.einsum('bclgn,bclge,bclgep->bcgepn', bm, decay_states, x)
    chunk_decay = jnp.exp(a_cs[:, :, -1])

    def step(hc, inp):
        st, dec = inp
        return hc * dec[..., None, None] + st, hc

    h0 = jnp.zeros((b, g, e, p, n), jnp.float32)
    _, states_in = lax.scan(step, h0, (jnp.moveaxis(states, 1, 0), jnp.moveaxis(chunk_decay, 1, 0)))
    states_in = jnp.moveaxis(states_in, 0, 1)
    y_off = jnp.einsum('bclgn,bcgepn,bclge->bclgep', cm, states_in, jnp.exp(a_cs))
    return (y_diag + y_off).reshape(b, s, g, e, p)


def mixer_ssd(h, w_in, conv_w, conv_b, dt_bias, a_log, d_skip, norm_g, w_out):
    bsz, s, _ = h.shape
    proj = h @ w_in
    z = proj[..., :D_INNER]
    xbc = proj[..., D_INNER:D_INNER + CONV_DIM]
    dt_raw = proj[..., D_INNER + CONV_DIM:].reshape(bsz, s, 2, SSM_HEADS)
    xbc = jax.nn.silu(dwconv(xbc, conv_w) + conv_b)
    gn = SSM_GROUPS * D_STATE
    xf = xbc[..., :D_INNER].reshape(bsz, s, SSM_GROUPS, HEADS_PER_GROUP, SSM_HEAD_DIM).astype(jnp.float32)
    bm = xbc[..., D_INNER:D_INNER + gn].reshape(bsz, s, SSM_GROUPS, D_STATE)
    cm = xbc[..., D_INNER + gn:].reshape(bsz, s, SSM_GROUPS, D_STATE)
    dt = jax.nn.softplus(dt_raw.astype(jnp.float32) + dt_bias.astype(jnp.float32))
    a = -jnp.exp(a_log.astype(jnp.float32))
    adt = (dt * a).reshape(bsz, s, 2, SSM_GROUPS, HEADS_PER_GROUP)
    dtg = dt.reshape(bsz, s, 2, SSM_GROUPS, HEADS_PER_GROUP)
    y_f = ssd_chunked(xf * dtg[:, :, 0, :, :, None], adt[:, :, 0], bm, cm)
    y_b = jnp.flip(ssd_chunked(jnp.flip(xf * dtg[:, :, 1, :, :, None], 1), jnp.flip(adt[:, :, 1], 1),
                               jnp.flip(bm, 1), jnp.flip(cm, 1)), 1)
    y = y_f + y_b + xf * d_skip.astype(jnp.float32).reshape(SSM_GROUPS, HEADS_PER_GROUP)[..., None]
    y = y.reshape(bsz, s, D_INNER) * jax.nn.silu(z.astype(jnp.float32))
    return rmsnorm(y, norm_g).astype(h.dtype) @ w_out


def conv_ffn(h, w_up, conv_w, conv_b, w_down):
    a, u = jnp.split(h @ w_up, 2, axis=-1)
    a = dwconv(a, conv_w) + conv_b
    return (jax.nn.silu(a) * u) @ w_down


def trunk(x, rel_bias, a_w_qkv, a_w_o, b_w_qkv, b_sink, b_w_o, c_w_in, c_conv_w, c_w_out,
          d_w_in, d_conv_w, d_conv_b, d_dt_bias, d_a_log, d_skip, d_norm_g, d_w_out,
          ffn_w_up, ffn_conv_w, ffn_conv_b, ffn_w_down, norm_g, final_g):
    for i in range(DEPTH):
        kind, j = i % N_MIXERS, i // N_MIXERS
        h = rmsnorm(x, norm_g[i, 0])
        if kind == 0:
            mix = mixer_dilated(h, a_w_qkv[j], a_w_o[j], rel_bias)
        elif kind == 1:
            mix = mixer_window(h, b_w_qkv[j], b_sink[j], b_w_o[j], rel_bias)
        elif kind == 2:
            mix = mixer_shortconv(h, c_w_in[j], c_conv_w[j], c_w_out[j])
        else:
            mix = mixer_ssd(h, d_w_in[j], d_conv_w[j], d_conv_b[j], d_dt_bias[j], d_a_log[j],
                            d_skip[j], d_norm_g[j], d_w_out[j])
        x = x + mix
        x = x + conv_ffn(rmsnorm(x, norm_g[i, 1]), ffn_w_up[i], ffn_conv_w[i], ffn_conv_b[i], ffn_w_down[i])
    return rmsnorm(x, final_g)


def setup_inputs(seed: int = 0) -> dict:
    key = jax.random.key(seed)
    ks = jax.random.split(key, 26)
    n_a, n_b, n_c, n_d = (len(range(t, DEPTH, N_MIXERS)) for t in range(N_MIXERS))

    def normal(k, shape):
        return jax.random.normal(k, shape, jnp.float32)

    def w(k, shape, fan_in):
        return normal(k, shape) * (fan_in ** -0.5)

    aw = N_HEADS * HEAD_DIM
    dt0 = jnp.exp(jax.random.uniform(ks[14], (n_d, 2, SSM_HEADS), jnp.float32,
                                     minval=math.log(1e-3), maxval=math.log(1e-1)))
    return {
        'x_prompt': normal(ks[0], (BATCH, SEQ, D_MODEL)),
        'x_sample': normal(ks[1], (DEC_BATCH, DEC_SEQ, D_MODEL)),
        'rel_bias': 0.2 * normal(ks[2], (NUM_BUCKETS, N_HEADS)),
        'a_w_qkv': w(ks[3], (n_a, D_MODEL, N_DIL * 3 * aw), D_MODEL),
        'a_w_o': w(ks[4], (n_a, aw, D_MODEL), aw),
        'b_w_qkv': w(ks[5], (n_b, D_MODEL, (N_HEADS + 2 * B_KV_HEADS) * HEAD_DIM), D_MODEL),
        'b_sink': 0.5 * normal(ks[6], (n_b, N_HEADS)),
        'b_w_o': w(ks[7], (n_b, aw, D_MODEL), aw),
        'c_w_in': w(ks[8], (n_c, D_MODEL, 3 * D_MODEL), D_MODEL),
        'c_conv_w': w(ks[9], (n_c, SHORT_CONV, D_MODEL), SHORT_CONV),
        'c_w_out': w(ks[10], (n_c, D_MODEL, D_MODEL), D_MODEL),
        'd_w_in': w(ks[11], (n_d, D_MODEL, D_INNER + CONV_DIM + 2 * SSM_HEADS), D_MODEL),
        'd_conv_w': w(ks[12], (n_d, SSM_CONV, CONV_DIM), SSM_CONV),
        'd_conv_b': 0.02 * normal(ks[13], (n_d, CONV_DIM)),
        'd_dt_bias': dt0 + jnp.log(-jnp.expm1(-dt0)),
        'd_a_log': jnp.log(jax.random.uniform(ks[15], (n_d, 2, SSM_HEADS), jnp.float32, minval=1.0, maxval=16.0)),
        'd_skip': 1.0 + 0.1 * normal(ks[16], (n_d, SSM_HEADS)),
        'd_norm_g': 1.0 + 0.05 * normal(ks[17], (n_d, D_INNER)),
        'd_w_out': w(ks[18], (n_d, D_INNER, D_MODEL), D_INNER),
        'ffn_w_up': w(ks[19], (DEPTH, D_MODEL, 2 * D_FF), D_MODEL),
        'ffn_conv_w': w(ks[20], (DEPTH, FFN_CONV, D_FF), FFN_CONV),
        'ffn_conv_b': 0.02 * normal(ks[21], (DEPTH, D_FF)),
        'ffn_w_down': w(ks[22], (DEPTH, D_FF, D_MODEL), D_FF),
        'norm_g': 1.0 + 0.05 * normal(ks[23], (DEPTH, 2, D_MODEL)),
        'final_g': 1.0 + 0.05 * normal(ks[24], (D_MODEL,)),
    }


def reference(x_prompt, x_sample, rel_bias, a_w_qkv, a_w_o, b_w_qkv, b_sink, b_w_o, c_w_in, c_conv_w, c_w_out,
              d_w_in, d_conv_w, d_conv_b, d_dt_bias, d_a_log, d_skip, d_norm_g, d_w_out,
              ffn_w_up, ffn_conv_w, ffn_conv_b, ffn_w_down, norm_g, final_g):
    y_prompt = trunk(x_prompt, rel_bias, a_w_qkv, a_w_o, b_w_qkv, b_sink, b_w_o, c_w_in, c_conv_w, c_w_out,
                     d_w_in, d_conv_w, d_conv_b, d_dt_bias, d_a_log, d_skip, d_norm_g, d_w_out,
                     ffn_w_up, ffn_conv_w, ffn_conv_b, ffn_w_down, norm_g, final_g)
    y_sample = trunk(x_sample, rel_bias, a_w_qkv, a_w_o, b_w_qkv, b_sink, b_w_o, c_w_in, c_conv_w, c_w_out,
                     d_w_in, d_conv_w, d_conv_b, d_dt_bias, d_a_log, d_skip, d_norm_g, d_w_out,
                     ffn_w_up, ffn_conv_w, ffn_conv_b, ffn_w_down, norm_g, final_g)
    return (y_prompt, y_sample)
```

```python
import numpy as np
import ml_dtypes
import concourse.bass as bass
import concourse.mybir as mybir
from concourse.bass_utils import run_bass_kernel_spmd

F32 = mybir.dt.float32
BF16 = mybir.dt.bfloat16
AF = mybir.ActivationFunctionType
ALU = mybir.AluOpType

D = 1024
KC = 8
DFF = 2816
MC = 22
EPS = 1e-6
NSEG = 3
PAD = 2
DIN = 2048
import os
DBG_ONLY = os.environ.get("DBG_ONLY", "")


ATT_CFG = {"A": [(1, 64), (4, 64), (16, 64)], "B": [(1, 128)]}
ATT_CI = {"A": [0, 1, 2], "B": [3]}
ALL_CFG = [(1, 64), (4, 64), (16, 64), (1, 128)]


def t5_bucket_np(rel):
    half, exact = 16, 8
    n = np.abs(rel)
    log_ratio = np.log(np.maximum(n, 1) / exact) / np.log(1024 / exact)
    large = np.minimum(exact + (log_ratio * (half - exact)).astype(np.int64), half - 1)
    return np.where(rel > 0, half, 0) + np.where(n < exact, n, large)


def oh_struct():
    keys, mats = [], []
    for ci, (r, half) in enumerate(ALL_CFG):
        nparts = 2 * half // 128 + 1
        for j in range(nparts):
            rel = 128 * j - half + np.arange(128)[:, None] - np.arange(128)[None, :]
            valid = np.abs(rel) <= half
            bk = t5_bucket_np(rel * r)
            for b in range(32):
                mk = valid & (bk == b)
                if mk.any():
                    keys.append((ci, j, b))
                    mats.append(mk.astype(np.float32))
            keys.append((ci, j, 32))
            mats.append((~valid).astype(np.float32))
    return keys, np.stack(mats)


OH_KEYS, _OH_MATS = oh_struct()


class Sem:
    def __init__(self, h, i):
        self.h = h
        self.i = i


class Eng:
    def __init__(self, name, h, sem):
        self.name = name
        self.h = h
        self.sem = sem
        self.cnt = 0
        self.seen = {}


class Buf:
    def __init__(self, ap, name=""):
        self.ap = ap
        self.name = name
        self.w = None
        self.r = {}
        self.dsem = None
        self.dcnt = 0
        self.psum = False

    def __getitem__(self, idx):
        return self.ap[idx]


class K:
    def __init__(self):
        self.nc = bass.Bass("TRN2", target_bir_lowering=False)
        nc = self.nc
        self.nsem = 0
        self.pe = Eng("pe", nc.tensor, self.newsem("pe"))
        self.act = Eng("act", nc.scalar, self.newsem("act"))
        self.dve = Eng("dve", nc.vector, self.newsem("dve"))
        self.pool = Eng("pool", nc.gpsimd, self.newsem("pool"))
        self.sp = Eng("sp", nc.sync, self.newsem("sp"))
        self.out_tokens = []
        self.nalloc = 0
        self.dma_latest = {}

    def newsem(self, name):
        s = Sem(self.nc.alloc_semaphore(f"s_{name}_{self.nsem}"), self.nsem)
        self.nsem += 1
        return s

    def sb(self, name, shape, dt):
        self.nalloc += 1
        t = self.nc.alloc_sbuf_tensor(f"{name}_{self.nalloc}", list(shape), dt)
        return t

    def ps(self, name):
        self.nalloc += 1
        t = self.nc.alloc_psum_tensor(f"{name}_{self.nalloc}", [128, 512], F32)
        b = Buf(t[:], name)
        b.psum = True
        return b

    def _wait(self, eng, deps):
        for (sem, val) in deps:
            if sem is eng.sem and eng.name == "pe":
                continue
            if eng.seen.get(sem.i, 0) < val:
                eng.h.wait_ge(sem.h, val)
                eng.seen[sem.i] = val

    @staticmethod
    def _deps(reads, writes):
        deps = []
        for b in reads:
            if b.w is not None:
                deps.append(b.w)
            if b.psum:
                deps.extend(b.r.values())
        for b in writes:
            if b.w is not None:
                deps.append(b.w)
            deps.extend(b.r.values())
        return deps

    @staticmethod
    def _mark(tok, reads, writes):
        for b in reads:
            b.r[tok[0].i] = tok
        for b in writes:
            b.w = tok
            b.r = {}

    def op(self, eng, fn, reads=(), writes=(), signal=True):
        self._wait(eng, self._deps(reads, writes))
        ins = fn(eng.h)
        tok = (eng.sem, eng.cnt + 1)
        ins.then_inc(eng.sem.h, 1)
        eng.cnt += 1
        self._mark(tok, reads, writes)
        return tok

    def dma(self, q, out, in_, owner, reads=(), writes=(), is_output=False):
        if owner.dsem is None:
            owner.dsem = self.newsem("d" + owner.name)
        self._wait(q, [d for d in self._deps(reads, writes) if d[0] is not owner.dsem])
        ins = q.h.dma_start(out=out, in_=in_)
        ins.then_inc(owner.dsem.h, 16)
        owner.dcnt += 16
        tok = (owner.dsem, owner.dcnt)
        self.dma_latest[owner.dsem.i] = tok
        self._mark(tok, reads, writes)
        if is_output:
            self.out_tokens.append(tok)
        return tok

    def barrier(self):
        engs = [self.pe, self.act, self.dve, self.pool, self.sp]
        toks = [(e.sem, e.cnt) for e in engs if e.cnt] + list(self.dma_latest.values())
        for e in engs:
            self._wait(e, toks)

    def finish(self):
        self._wait(self.sp, self.out_tokens)
        for e in (self.pe, self.act, self.dve, self.pool):
            if e.cnt:
                self._wait(self.sp, [(e.sem, e.cnt)])


class Prog:
    def __init__(self, seg, layers=(0, 1, 2, 3), do_final=True):
        self.SEG = seg
        self.NT = NSEG * seg
        self.layers = layers
        self.do_final = do_final
        self.k = K()
        k = self.k
        nc = k.nc
        NT = self.NT
        self.din = {}

        def inp(name, shape):
            self.din[name] = nc.dram_tensor(name, list(shape), F32, kind="ExternalInput").ap()
            return self.din[name]

        self.xin = inp("xT", [D, NT + 2 * PAD])
        self.flags_d = inp("flags", [128, 8])
        inp("c_w_in", [1, D, 3 * D]); inp("c_w_out", [1, D, D])
        inp("ffn_w_up", [4, D, 2 * DFF]); inp("ffn_cv", [4, 128, 4, MC])
        inp("ffn_w_down", [4, DFF, D]); inp("normg_h", [128, 9, KC]); inp("c_cv", [128, 3, KC])
        inp("a_w_qkv", [1, D, 9 * D]); inp("a_w_o", [1, D, D]); inp("b_w_qkv", [1, D, 1536]); inp("b_w_o", [1, D, D])
        inp("relb_rep", [128, 512]); inp("sink_h", [128, 8]); inp("oh_mats", [len(OH_KEYS), 128, 128]); inp("flags2", [128, 12])
        self.QT_d = nc.dram_tensor("QT_d", [3, D, NT], BF16, kind="Internal")
        self.KT_d = nc.dram_tensor("KT_d", [3, D, NT], BF16, kind="Internal")
        self.V_d = nc.dram_tensor("V_d", [3, NT, D], BF16, kind="Internal")
        self.OT_d = nc.dram_tensor("OT_d", [D, NT], BF16, kind="Internal")
        nb5 = (NT + 511) // 512
        self.r_qt = [[[Buf(None, "rq") for _ in range(nb5)] for _ in range(KC)] for _ in range(3)]
        self.r_kt = [[[Buf(None, "rk") for _ in range(nb5)] for _ in range(KC)] for _ in range(3)]
        self.r_v = [[[Buf(None, "rv") for _ in range(NT // 128)] for _ in range(2)] for _ in range(3)]
        self.r_ot = [[Buf(None, "ro") for _ in range(NSEG)] for _ in range(KC)]
        inp("d_w_in", [1, D, 5184]); inp("d_w_out", [1, DIN, D]); inp("d_cv", [128, 5, 24]); inp("d_consts", [128, 5, 128])
        inp("d_dtb_rep", [128, 64]); inp("d_alog_rep", [128, 64]); inp("d_skip_rep", [128, 32]); inp("d_normg_rep", [128, DIN])
        self.X_d = nc.dram_tensor("X_d", [NT, 2560], BF16, kind="Internal")
        self.BT_d = nc.dram_tensor("BT_d", [512, NT], BF16, kind="Internal")
        self.CT_d = nc.dram_tensor("CT_d", [512, NT], BF16, kind="Internal")
        self.Z_d = nc.dram_tensor("Z_d", [NT, DIN], BF16, kind="Internal")
        self.DTA_d = nc.dram_tensor("DTA_d", [NT, 128], F32, kind="Internal")
        self.Y_d = nc.dram_tensor("Y_d", [NT, DIN], F32, kind="Internal")
        self.YT_d = nc.dram_tensor("YT_d", [DIN, NT], BF16, kind="Internal")
        self.yT = nc.dram_tensor("yT", [D, NT], F32, kind="ExternalOutput").ap()
        self.xA = nc.dram_tensor("xA", [D, NT + 2 * PAD], F32, kind="Internal").ap()
        self.xB = nc.dram_tensor("xB", [D, NT + 2 * PAD], F32, kind="Internal").ap()
        self.nreg = (NT + 2 * PAD + 511) // 512
        self.regs = {}
        for nm in ("xin", "xA", "xB", "yT"):
            self.regs[nm] = [[Buf(None, f"{nm}{n}_{i}") for i in range(self.nreg)] for n in range(KC)]
        self.aps = {"xin": self.xin, "xA": self.xA, "xB": self.xB, "yT": self.yT}

        self.ones = Buf(k.sb("ones", [128, 128], BF16)[:], "ones")
        self.flags = Buf(k.sb("flags", [128, 8], F32)[:], "flags")
        self.zer = Buf(k.sb("zer", [128, 8], F32)[:], "zer")
        self.xt = Buf(k.sb("xt", [128, KC, 512], F32)[:], "xt")
        self.hT = [Buf(None, f"hT{i}") for i in range(KC)]
        hT_t = k.sb("hT", [128, KC, 512], BF16)
        for i in range(KC):
            self.hT[i].ap = hT_t[:, i, :]
        self.hT_t = hT_t
        self.sq = [Buf(k.sb("sq", [128, 512], BF16)[:], f"sq{i}") for i in range(2)]
        self.rstd = Buf(k.sb("rstd", [128, 512], F32)[:], "rstd")
        self.gbuf = [Buf(k.sb("gb", [128, 512], F32)[:], f"gb{i}") for i in range(3)]
        self.xo = [Buf(k.sb("xo", [128, 512], F32)[:], f"xo{i}") for i in range(3)]
        self.normg = Buf(k.sb("normg", [128, 9, KC], F32)[:], "normg")
        self.p_ss = k.ps("pss")
        self.p_a = [k.ps("pa0"), k.ps("pa1")]
        self.p_u = [k.ps("pu0"), k.ps("pu1")]
        self.p_o = [k.ps("po0"), k.ps("po1")]
        self.p_x = k.ps("px")
        self.WCOLS = KC * 2 * DFF + MC * D
        self.warena = k.sb("warena", [128, self.WCOLS], BF16)
        self.wbuf = Buf(self.warena[:], "warena")
        self.hmid_t = k.sb("hmid", [128, MC, 512], BF16)
        self.hmid = [Buf(self.hmid_t[:, i, :], f"hm{i}") for i in range(MC)]
        self.cvec = Buf(k.sb("cvec", [128, 4, MC], F32)[:], "cvec")

        self.qs = [Buf(k.sb("qs", [128, 512], BF16)[:], f"qs{i}") for i in range(4)]
        self.xr = [Buf(k.sb("xr", [128, 512], F32)[:], f"xr{i}") for i in range(2)]
        self.flags2 = Buf(k.sb("flags2", [128, 12], F32)[:], "flags2")
        self.es = Buf(k.sb("es", [128, 8], F32)[:], "es")
        self.erb = Buf(k.sb("erb", [128, 512], F32)[:], "erb")
        A = self.warena
        SEGc = self.SEG
        Hm = 1024
        o = 0
        def carve(n):
            nonlocal o
            v = A[:, o:o + n]
            o += n
            return v
        self.a_KT = Buf(carve(SEGc + 2 * Hm), "aKT")
        self.a_QT = Buf(carve(2 * SEGc).rearrange("p (a t) -> p a t", a=2), "aQT")
        self.a_V = Buf(carve(48 * 256).rearrange("p (b c) -> p b c", c=256), "aV")
        self.a_tab = Buf(carve(3 * 8 * 2 * 256), "atab")
        self.a_accO = Buf(carve(2 * SEGc).bitcast(F32), "accO")
        self.a_accD = Buf(carve(2 * SEGc).bitcast(F32), "accD")
        self.a_OTn = Buf(carve(SEGc), "OTn")
        self.a_pexp = [Buf(carve(512).bitcast(F32), f"pexp{i}") for i in range(4)]
        self.a_PT = [Buf(carve(256), f"PT{i}") for i in range(4)]
        self.a_oh = Buf(carve(32 * 128).rearrange("p (b c) -> p b c", c=128), "aoh")
        self.a_ttmp = Buf(carve(256).bitcast(F32), "ttmp")
        self.a_onesz = Buf(carve(256), "onesz")
        self.a_ident = Buf(carve(128), "aident")
        assert o <= self.WCOLS, o
        self.rr = 0
        with nc.allow_non_contiguous_dma(reason="few tiny halo/pad column transfers"):
            self.build()

    def regions(self, nm, c0, c1, n=None):
        out = []
        for nn in (range(KC) if n is None else [n]):
            out.extend(self.regs[nm][nn][c0 // 512:(c1 - 1) // 512 + 1])
        return out

    def build(self):
        k = self.k
        nc = k.nc
        k.op(k.dve, lambda e: e.memset(self.ones[:], 1.0), writes=[self.ones])
        k.op(k.dve, lambda e: e.memset(self.zer[:], 0.0), writes=[self.zer])
        k.dma(k.sp, self.flags[:], self.flags_d, self.flags, writes=[self.flags])
        k.dma(k.sp, self.flags2[:], self.din["flags2"], self.flags2, writes=[self.flags2])
        k.dma(k.sp, self.erb[:], self.din["relb_rep"], self.erb, writes=[self.erb])
        k.op(k.act, lambda e: e.activation(out=self.erb[:], in_=self.erb[:], func=AF.Identity, scale=8.0), reads=[self.erb], writes=[self.erb])
        k.dma(k.sp, self.normg[:], self.din["normg_h"], self.normg, writes=[self.normg])
        NT = self.NT
        for nm in ("xA", "xB"):
            for col in (0, 1, NT + PAD, NT + PAD + 1):
                dst = self.aps[nm][:, col:col + 1].rearrange("(k p) o -> p k o", p=128)
                k.dma(k.sp, dst, self.zer[:, 0:8].rearrange("p (k o) -> p k o", o=1), self.zer,
                      reads=[self.zer], writes=self.regions(nm, col, col + 1))
        cur = "xin"
        names = ["xB", "xA"]
        nlay = len(self.layers)
        for li, layer in enumerate(self.layers):
            kind = layer % 4
            dst = names[0]
            if DBG_ONLY == "ffn":
                kind = -1
            if kind == 2:
                self.pass_conv(layer, cur, dst)
            elif kind in (0, 1):
                self.pass_attention("A" if kind == 0 else "B", layer, cur, dst)
            elif kind == 3:
                self.pass_ssd(layer, cur, dst)
            if kind in (0, 1, 2, 3):
                cur = dst
                names = names[::-1]
                dst = names[0]
            last = (li == nlay - 1) and self.do_final
            if DBG_ONLY == "mixer":
                self.pass_copy_final(cur)
                continue
            self.pass_ffn(layer, cur, "yT" if last else dst, final=last)
            cur = dst
            names = names[::-1]
        k.finish()

    def windows(self, hl=1, hr=None):
        if hr is None:
            hr = hl
        SEG = self.SEG
        res = []
        for s in range(NSEG):
            a = s * SEG
            b = a + SEG
            o0 = a
            while o0 < b:
                o1 = min(b, o0 + 512 - hl - hr)
                c0 = o0 - hl
                W = (o1 + hr) - c0
                fl = (2 * s) if o0 == a else None
                fr = (2 * s + 1) if o1 == b else None
                res.append((c0, W, o0, o1, fl, fr))
                o0 = o1
        return res

    def load_weights_cast(self, dst_ap_fn, src, nk, ncols, wbuf_off):
        k = self.k
        step = 2048
        for kk in range(nk):
            for c in range(0, ncols, step):
                cc = min(step, ncols - c)
                o = wbuf_off + kk * ncols + c
                k.dma(k.pool, self.warena[:, o:o + cc], src[kk * 128:(kk + 1) * 128, c:c + cc],
                      self.wbuf, writes=[self.wbuf])

    def front(self, src, c0, W, gidx, fl, fr, hl=1, hr=1):
        self.front_a(src, c0, W)
        self.front_b(W, gidx, fl, fr, hl, hr)

    def front_a(self, src, c0, W):
        k = self.k
        xt, hT = self.xt, self.hT
        srcap = self.aps[src][:, c0 + PAD:c0 + PAD + W].rearrange("(k p) w -> p k w", p=128)
        k.dma(k.sp, xt[:, :, 0:W], srcap, xt, reads=self.regions(src, c0 + PAD, c0 + PAD + W), writes=[xt])
        for kk in range(KC):
            sq = self.sq[kk % 2]
            k.op(k.act, lambda e, kk=kk, sq=sq: e.activation(out=sq[:, 0:W], in_=xt[:, kk, 0:W], func=AF.Square),
                 reads=[xt], writes=[sq])
            k.op(k.pe, lambda e, kk=kk, sq=sq: e.matmul(self.p_ss[:, 0:W], self.ones[:], sq[:, 0:W],
                                                       start=(kk == 0), stop=(kk == KC - 1)),
                 reads=[sq, self.ones], writes=[self.p_ss], signal=(kk == KC - 1))
        rstd = self.rstd
        k.op(k.act, lambda e: e.activation(out=rstd[:, 0:W], in_=self.p_ss[:, 0:W], func=AF.Sqrt,
                                           bias=EPS, scale=1.0 / D), reads=[self.p_ss], writes=[rstd])
        k.op(k.dve, lambda e: e.reciprocal(out=rstd[:, 0:W], in_=rstd[:, 0:W]), reads=[rstd], writes=[rstd])

    def front_b(self, W, gidx, fl, fr, hl=1, hr=1):
        k = self.k
        xt, hT, rstd = self.xt, self.hT, self.rstd
        for kk in range(KC):
            k.op(k.dve, lambda e, kk=kk: e.scalar_tensor_tensor(
                out=hT[kk][:, 0:W], in0=xt[:, kk, 0:W], scalar=self.normg[:, gidx, kk:kk + 1],
                in1=rstd[:, 0:W], op0=ALU.mult, op1=ALU.mult),
                reads=[xt, rstd, self.normg], writes=[hT[kk]])
        for (f, col, wd) in ((fl, 0, hl), (fr, W - hr, hr)):
            if f is not None and wd > 0:
                k.op(k.dve, lambda e, f=f, col=col, wd=wd: e.tensor_scalar(
                    out=self.hT_t[:, :, col:col + wd], in0=self.hT_t[:, :, col:col + wd],
                    scalar1=self.flags[:, f:f + 1], scalar2=None, op0=ALU.mult),
                    reads=self.hT + [self.flags], writes=self.hT)

    def back_out(self, dst, wo_off, nmk, act_bufs, W, o0, o1, final, h=1, xres=None):
        k = self.k
        Wo = W - 2 * h
        for n in range(KC):
            po = self.p_o[n % 2]
            for m in range(nmk):
                o = wo_off + m * D + n * 128
                k.op(k.pe, lambda e, m=m, o=o, po=po: e.matmul(po[:, 0:Wo], self.warena[:, o:o + 128],
                                                              act_bufs[m][:, h:W - h],
                                                              start=(m == 0), stop=(m == nmk - 1)),
                     reads=[act_bufs[m], self.wbuf], writes=[po], signal=(m == nmk - 1))
            if not final:
                xo = self.xo[self.rr % 3]
                self.rr += 1
                if xres is None:
                    k.op(k.dve, lambda e, n=n, po=po, xo=xo: e.tensor_tensor(
                        out=xo[:, 0:Wo], in0=po[:, 0:Wo], in1=self.xt[:, n, h:W - h], op=ALU.add),
                        reads=[po, self.xt], writes=[xo])
                else:
                    xr = self.xr[n % 2]
                    k.dma(k.sp, xr[:, 0:Wo], self.aps[xres][n * 128:(n + 1) * 128, o0 + PAD:o1 + PAD], xr,
                          reads=self.regions(xres, o0 + PAD, o1 + PAD, n), writes=[xr])
                    k.op(k.dve, lambda e, n=n, po=po, xo=xo, xr=xr: e.tensor_tensor(
                        out=xo[:, 0:Wo], in0=po[:, 0:Wo], in1=xr[:, 0:Wo], op=ALU.add),
                        reads=[po, xr], writes=[xo])
                k.dma(k.sp, self.aps[dst][n * 128:(n + 1) * 128, o0 + PAD:o1 + PAD], xo[:, 0:Wo], xo,
                      reads=[xo], writes=self.regions(dst, o0 + PAD, o1 + PAD, n))
            else:
                k.op(k.dve, lambda e, n=n, po=po: e.tensor_tensor(
                    out=self.xt[:, n, h:W - h], in0=po[:, 0:Wo], in1=self.xt[:, n, h:W - h], op=ALU.add),
                    reads=[po, self.xt], writes=[self.xt])
        if final:
            xt = self.xt
            for kk in range(KC):
                sq = self.sq[kk % 2]
                k.op(k.act, lambda e, kk=kk, sq=sq: e.activation(out=sq[:, 0:Wo], in_=xt[:, kk, h:W - h], func=AF.Square),
                     reads=[xt], writes=[sq])
                k.op(k.pe, lambda e, kk=kk, sq=sq: e.matmul(self.p_ss[:, 0:Wo], self.ones[:], sq[:, 0:Wo],
                                                           start=(kk == 0), stop=(kk == KC - 1)),
                     reads=[sq, self.ones], writes=[self.p_ss], signal=(kk == KC - 1))
            rstd = self.rstd
            k.op(k.act, lambda e: e.activation(out=rstd[:, 0:Wo], in_=self.p_ss[:, 0:Wo], func=AF.Sqrt,
                                               bias=EPS, scale=1.0 / D), reads=[self.p_ss], writes=[rstd])
            k.op(k.dve, lambda e: e.reciprocal(out=rstd[:, 0:Wo], in_=rstd[:, 0:Wo]), reads=[rstd], writes=[rstd])
            for kk in range(KC):
                xo = self.xo[self.rr % 3]
                self.rr += 1
                k.op(k.dve, lambda e, kk=kk, xo=xo: e.scalar_tensor_tensor(
                    out=xo[:, 0:Wo], in0=xt[:, kk, h:W - h], scalar=self.normg[:, 8, kk:kk + 1],
                    in1=rstd[:, 0:Wo], op0=ALU.mult, op1=ALU.mult),
                    reads=[xt, rstd, self.normg], writes=[xo])
                k.dma(k.sp, self.yT[kk * 128:(kk + 1) * 128, o0:o1], xo[:, 0:Wo], xo,
                      reads=[xo], writes=self.regions("yT", o0, o1, kk), is_output=True)

    def pass_copy_final(self, src):
        k = self.k
        for c in range(0, self.NT, 512):
            srcap = self.aps[src][:, c + PAD:c + PAD + 512].rearrange("(k p) w -> p k w", p=128)
            k.dma(k.sp, self.xt[:, :, 0:512], srcap, self.xt, reads=self.regions(src, c + PAD, c + PAD + 512), writes=[self.xt])
            k.dma(k.sp, self.yT[:, c:c + 512].rearrange("(k p) w -> p k w", p=128), self.xt[:, :, 0:512], self.xt,
                  reads=[self.xt], writes=self.regions("yT", c, c + 512), is_output=True)

    def pass_ffn(self, layer, src, dst, final=False):
        k = self.k
        up_off = 0
        dn_off = KC * 2 * DFF
        self.load_weights_cast(None, self.din["ffn_w_up"][layer], KC, 2 * DFF, up_off)
        self.load_weights_cast(None, self.din["ffn_w_down"][layer], MC, D, dn_off)
        cv = self.cvec
        k.dma(k.sp, cv[:], self.din["ffn_cv"][layer], cv, writes=[cv])
        wins = self.windows(1)
        pre = not final
        if pre:
            self.front(src, wins[0][0], wins[0][1], layer * 2 + 1, wins[0][4], wins[0][5])
        for wi, (c0, W, o0, o1, fl, fr) in enumerate(wins):
            if not pre:
                self.front(src, c0, W, layer * 2 + 1, fl, fr)
            Wo = W - 2
            PA = [self.p_a[0], self.p_a[1], self.p_o[0], self.p_x]
            PU = [self.p_u[0], self.p_u[1], self.p_o[1], self.p_ss]
            for m in range(MC):
                pa = PA[m % 4]
                pu = PU[m % 4]
                for (pp, coff) in ((pa, m * 128), (pu, DFF + m * 128)):
                    for kk in range(KC):
                        o = up_off + kk * 2 * DFF + coff
                        k.op(k.pe, lambda e, kk=kk, o=o, pp=pp: e.matmul(
                            pp[:, 0:W], self.warena[:, o:o + 128], self.hT[kk][:, 0:W],
                            start=(kk == 0), stop=(kk == KC - 1)),
                            reads=[self.hT[kk], self.wbuf], writes=[pp], signal=(kk == KC - 1))
                g = self.gbuf[m % 3]
                k.op(k.act, lambda e, m=m, pa=pa, g=g: e.activation(
                    out=g[:, 0:Wo], in_=pa[:, 1:W - 1], func=AF.Identity,
                    bias=cv[:, 3, m:m + 1], scale=cv[:, 1, m:m + 1]), reads=[pa, cv], writes=[g])
                k.op(k.dve, lambda e, m=m, pa=pa, g=g: e.scalar_tensor_tensor(
                    out=g[:, 0:Wo], in0=pa[:, 0:W - 2], scalar=cv[:, 0, m:m + 1], in1=g[:, 0:Wo],
                    op0=ALU.mult, op1=ALU.add), reads=[pa, cv, g], writes=[g])
                k.op(k.dve, lambda e, m=m, pa=pa, g=g: e.scalar_tensor_tensor(
                    out=g[:, 0:Wo], in0=pa[:, 2:W], scalar=cv[:, 2, m:m + 1], in1=g[:, 0:Wo],
                    op0=ALU.mult, op1=ALU.add), reads=[pa, cv, g], writes=[g])
                k.op(k.act, lambda e, g=g: e.activation(out=g[:, 0:Wo], in_=g[:, 0:Wo], func=AF.Silu),
                     reads=[g], writes=[g])
                k.op(k.dve, lambda e, m=m, pu=pu, g=g: e.tensor_tensor(
                    out=self.hmid[m][:, 1:W - 1], in0=g[:, 0:Wo], in1=pu[:, 1:W - 1], op=ALU.mult),
                    reads=[g, pu], writes=[self.hmid[m]])
            if pre and wi + 1 < len(wins):
                nw = wins[wi + 1]
                self.front(src, nw[0], nw[1], layer * 2 + 1, nw[4], nw[5])
            self.back_out(dst, dn_off, MC, self.hmid, W, o0, o1, final, xres=(src if pre else None))

    def pass_attention(self, kind, layer, src, dst):
        k = self.k
        cfgs = ATT_CFG[kind]
        j0 = layer // 4
        NT, SEG = self.NT, self.SEG
        gidx = layer * 2
        for gi in range(len(cfgs)):
            if kind == "A":
                self.load_weights_cast(None, self.din["a_w_qkv"][j0][:, gi * 3 * D:(gi + 1) * 3 * D], KC, 3 * D, 0)
            else:
                wq = self.din["b_w_qkv"][j0]
                for kk in range(KC):
                    base = kk * 3 * D
                    rows = slice(kk * 128, (kk + 1) * 128)
                    k.dma(k.pool, self.warena[:, base:base + D], wq[rows, 0:D], self.wbuf, writes=[self.wbuf])
                    for part, scol in ((1, D), (2, D + 256)):
                        for cd in range(4):
                            dstv = self.warena[:, base + part * D:base + (part + 1) * D].rearrange(
                                "p (g c e) -> p g c e", g=4, c=4)[:, :, cd, :]
                            srcv = wq[rows, scol:scol + 256].rearrange("p (g e) -> p g e", g=4)
                            k.dma(k.pool, dstv, srcv, self.wbuf, writes=[self.wbuf])
            wins0 = self.windows(0)
            self.front_a(src, wins0[0][0], wins0[0][1])
            for wi0, (c0, W, o0, o1, fl, fr) in enumerate(wins0):
                self.front_b(W, gidx, None, None, 0, 0)
                if wi0 + 1 < len(wins0):
                    self.front_a(src, wins0[wi0 + 1][0], wins0[wi0 + 1][1])
                cb = c0 // 512
                for m in range(16):
                    pp = self.p_a[m % 2]
                    for kk in range(KC):
                        o = kk * 3 * D + m * 128
                        k.op(k.pe, lambda e, kk=kk, o=o, pp=pp: e.matmul(
                            pp[:, 0:W], self.warena[:, o:o + 128], self.hT[kk][:, 0:W],
                            start=(kk == 0), stop=(kk == KC - 1)), reads=[self.hT[kk], self.wbuf], writes=[pp])
                    q = self.qs[m % 4]
                    if m % 2 == 0:
                        k.op(k.act, lambda e, pp=pp, q=q: e.activation(out=q[:, 0:W], in_=pp[:, 0:W], func=AF.Copy),
                             reads=[pp], writes=[q])
                    else:
                        k.op(k.dve, lambda e, pp=pp, q=q: e.tensor_copy(out=q[:, 0:W], in_=pp[:, 0:W]),
                             reads=[pp], writes=[q])
                    dt_, rg = (self.QT_d, self.r_qt) if m < 8 else (self.KT_d, self.r_kt)
                    mm = m % 8
                    k.dma(k.sp, dt_.ap()[gi, mm * 128:(mm + 1) * 128, c0:c0 + W], q[:, 0:W], q,
                          reads=[q], writes=[rg[gi][mm][cb]])
                for tb in range(W // 128):
                    for hf in range(2):
                        pv = self.p_u[(tb * 2 + hf) % 2]
                        for kk in range(KC):
                            o = kk * 3 * D + 2 * D + hf * 512
                            k.op(k.pe, lambda e, kk=kk, o=o, pv=pv, tb=tb: e.matmul(
                                pv[:, 0:512], self.hT[kk][:, tb * 128:(tb + 1) * 128], self.warena[:, o:o + 512],
                                start=(kk == 0), stop=(kk == KC - 1)), reads=[self.hT[kk], self.wbuf], writes=[pv])
                        q = self.qs[(tb * 2 + hf) % 4]
                        if hf == 0:
                            k.op(k.act, lambda e, pv=pv, q=q: e.activation(out=q[:, 0:512], in_=pv[:, 0:512], func=AF.Copy),
                                 reads=[pv], writes=[q])
                        else:
                            k.op(k.dve, lambda e, pv=pv, q=q: e.tensor_copy(out=q[:, 0:512], in_=pv[:, 0:512]),
                                 reads=[pv], writes=[q])
                        t0 = c0 + tb * 128
                        k.dma(k.sp, self.V_d.ap()[gi, t0:t0 + 128, hf * 512:(hf + 1) * 512], q[:, 0:512], q,
                              reads=[q], writes=[self.r_v[gi][hf][t0 // 128]])
        k.barrier()
        self.attn_setup(kind, j0)
        for s_ in range(NSEG):
            for hp in range(KC):
                for gi, (r, half) in enumerate(cfgs):
                    self.attn_unit(kind, s_, hp, gi, r, half, first=(gi == 0))
                self.attn_finish(s_, hp)
        k.barrier()
        self.load_weights_cast(None, self.din["a_w_o" if kind == "A" else "b_w_o"][j0], KC, D, 0)
        for (c0, W, o0, o1, fl, fr) in self.windows(0):
            srcap = self.aps[src][:, c0 + PAD:c0 + PAD + W].rearrange("(k p) w -> p k w", p=128)
            k.dma(k.sp, self.xt[:, :, 0:W], srcap, self.xt, reads=self.regions(src, c0 + PAD, c0 + PAD + W), writes=[self.xt])
            for m in range(KC):
                k.dma(k.sp, self.hmid[m][:, 0:W], self.OT_d.ap()[m * 128:(m + 1) * 128, c0:c0 + W], self.hmid[m],
                      reads=[self.r_ot[m][c0 // SEG]], writes=[self.hmid[m]])
            self.back_out(dst, 0, KC, self.hmid, W, o0, o1, False, h=0)

    def attn_setup(self, kind, j0):
        k = self.k
        cfgs = ATT_CFG[kind]
        onesz = self.a_onesz
        k.op(k.dve, lambda e: e.memset(onesz[:], 0.0), writes=[onesz])
        k.op(k.dve, lambda e: e.memset(onesz[:, 0:64], 1.0), writes=[onesz])
        k.op(k.dve, lambda e: e.memset(onesz[:, 192:256], 1.0), writes=[onesz])
        k.dma(k.pool, self.a_ident[:], self.din["d_consts"][:, 4, :], self.a_ident, writes=[self.a_ident])
        k.op(k.dve, lambda e: e.memset(self.a_QT[:], 0.0), writes=[self.a_QT])
        k.op(k.dve, lambda e: e.memset(self.a_V[:], 0.0), writes=[self.a_V])
        if kind == "B":
            k.dma(k.sp, self.es[:], self.din["sink_h"], self.es, writes=[self.es])
            k.op(k.act, lambda e: e.activation(out=self.es[:], in_=self.es[:], func=AF.Exp), reads=[self.es], writes=[self.es])
        else:
            k.op(k.dve, lambda e: e.memset(self.es[:], 0.0), writes=[self.es])
        oh, ttmp, tab = self.a_oh, self.a_ttmp, self.a_tab
        for gi, ci in enumerate(ATT_CI[kind]):
            r, half = ALL_CFG[ci]
            nparts = 2 * half // 128 + 1
            for j in range(nparts):
                idx = [i for i, kk in enumerate(OH_KEYS) if kk[0] == ci and kk[1] == j]
                bks = [OH_KEYS[i][2] for i in idx]
                nb = len(idx)
                k.dma(k.pool, oh[:, 0:nb, :], self.din["oh_mats"][idx[0]:idx[0] + nb].rearrange("b p c -> p b c"),
                      oh, writes=[oh])
                for h in range(16):
                    hp, hh = h // 2, h % 2
                    to = ((gi * 8 + hp) * nparts + j) * 256 + hh * 128
                    tslot = tab[:, to:to + 128]
                    for ii, b in enumerate(bks):
                        sc = self.erb[:, b * 16 + h:b * 16 + h + 1] if b < 32 else -30000.0
                        last = (ii == nb - 1)
                        outap = tslot if last else ttmp[:]
                        if ii == 0:
                            k.op(k.dve, lambda e, ii=ii, sc=sc, outap=outap: e.tensor_scalar(
                                out=outap, in0=oh[:, ii, :], scalar1=sc, scalar2=None, op0=ALU.mult),
                                reads=[oh, self.erb], writes=[tab if last else ttmp])
                        else:
                            k.op(k.dve, lambda e, ii=ii, sc=sc, outap=outap: e.scalar_tensor_tensor(
                                out=outap, in0=oh[:, ii, :], scalar=sc, in1=ttmp[:], op0=ALU.mult, op1=ALU.add),
                                reads=[oh, self.erb, ttmp], writes=[tab if last else ttmp])

    def attn_unit(self, kind, s_, hp, gi, r, half, first):
        k = self.k
        NT, SEG = self.NT, self.SEG
        H = half * r
        n = SEG // r
        nqb = n // 128
        nparts = 2 * half // 128 + 1
        nbk = nqb + nparts - 1
        KT, QT, V = self.a_KT, self.a_QT, self.a_V
        rows = slice(hp * 128, (hp + 1) * 128)
        base = s_ * SEG
        ktd, qtd = self.KT_d.ap()[gi], self.QT_d.ap()[gi]

        def kreg(a, b):
            return self.r_kt[gi][hp][a // 512:(b - 1) // 512 + 1]
        la = max(0, base - H)
        k.dma(k.sp, KT[:, 0:H], ktd[rows, la:la + H], KT, reads=kreg(la, la + H), writes=[KT])
        k.dma(k.sp, KT[:, H:H + SEG], ktd[rows, base:base + SEG], KT, reads=kreg(base, base + SEG), writes=[KT])
        ra = min(NT - H, base + SEG)
        k.dma(k.sp, KT[:, H + SEG:H + SEG + H], ktd[rows, ra:ra + H], KT, reads=kreg(ra, ra + H), writes=[KT])
        qreg = self.r_qt[gi][hp][base // 512:(base + SEG - 1) // 512 + 1]
        k.dma(k.sp, QT[0:64, 0, 0:SEG], qtd[hp * 128:hp * 128 + 64, base:base + SEG], QT, reads=qreg, writes=[QT])
        k.dma(k.sp, QT[64:128, 1, 0:SEG], qtd[hp * 128 + 64:hp * 128 + 128, base:base + SEG], QT, reads=qreg, writes=[QT])
        vt = self.V_d
        hf = hp // 4
        for rho in range(r):
            for m in range(nbk):
                tstart = base + rho + r * (128 * m - half)
                p0 = 0 if tstart >= 0 else (-tstart + r - 1) // r
                p1 = min(128, (NT - 1 - tstart) // r + 1)
                if p1 <= p0:
                    continue
                off = (gi * NT + tstart + r * p0) * D + hp * 128
                srcv = bass.AP(vt, off, [[r * D, p1 - p0], [64, 2], [1, 64]])
                dstv = V[p0:p1, rho * nbk + m, :].rearrange("p (a e) -> p a e", a=4)[:, 0:4:3, :]
                ta, tl = tstart + r * p0, tstart + r * (p1 - 1)
                k.dma(k.sp, dstv, srcv, V, reads=self.r_v[gi][hf][ta // 128:tl // 128 + 1], writes=[V])
        accO, accD = self.a_accO, self.a_accD
        parts = []
        for rho in range(r):
            for qb in range(nqb):
                q0 = rho + r * 128 * qb
                qsl = slice(q0, q0 + 127 * r + 1, r)
                bi = rho * nqb + qb
                for j in range(nparts):
                    m = qb + j
                    lo = 128 * m - half
                    fcol = None
                    if lo < 0:
                        fcol = s_ * 4 + (0 if lo == -64 else 1)
                    elif lo + 128 > n:
                        fcol = s_ * 4 + (2 if lo + 128 - n == 64 else 3)
                    parts.append(dict(qsl=qsl, j=j, kc0=H + rho + r * lo, fcol=fcol, vb=rho * nbk + m,
                                      pO=self.p_o[bi % 2], pD=self.p_u[bi % 2]))
        SB = [self.p_a[0], self.p_a[1], self.p_ss, self.p_x]
        NS, LAG = 4, 3

        def stageA(i):
            p = parts[i]
            pS, pexp, PT = SB[i % NS], self.a_pexp[i % NS], self.a_PT[i % NS]
            kc0, qsl, j, fcol = p["kc0"], p["qsl"], p["j"], p["fcol"]
            to = ((gi * 8 + hp) * nparts + j) * 256
            tabv = self.a_tab[:, to:to + 256]
            k.op(k.pe, lambda e: e.matmul(pS[:, 0:256], KT[:, kc0:kc0 + 127 * r + 1:r], QT[:, :, qsl], start=True, stop=False),
                 reads=[KT, QT], writes=[pS])
            k.op(k.pe, lambda e: e.matmul(pS[:, 0:256], self.a_ident[:], tabv, start=False, stop=True),
                 reads=[self.a_ident, self.a_tab], writes=[pS])
            if fcol is None:
                k.op(k.act, lambda e: e.activation(out=PT[:], in_=pS[:, 0:256], func=AF.Exp, scale=0.125),
                     reads=[pS], writes=[PT])
            else:
                k.op(k.act, lambda e: e.activation(out=pexp[:], in_=pS[:, 0:256], func=AF.Exp, scale=0.125),
                     reads=[pS], writes=[pexp])
                k.op(k.dve, lambda e: e.tensor_scalar(out=PT[:], in0=pexp[:], scalar1=self.flags2[:, fcol:fcol + 1],
                                                      scalar2=None, op0=ALU.mult), reads=[pexp, self.flags2], writes=[PT])

        def stageB(i):
            p = parts[i]
            PT = self.a_PT[i % NS]
            j, vb, pO, pD, qsl = p["j"], p["vb"], p["pO"], p["pD"], p["qsl"]
            for hh in range(2):
                k.op(k.pe, lambda e, hh=hh: e.matmul(
                    pO[:, 0:128], V[:, vb, hh * 128:(hh + 1) * 128], PT[:, hh * 128:(hh + 1) * 128],
                    start=(j == 0 and hh == 0), stop=(j == nparts - 1 and hh == 1)), reads=[V, PT], writes=[pO])
            for hh in range(2):
                k.op(k.pe, lambda e, hh=hh: e.matmul(
                    pD[:, 0:128], self.a_onesz[:, hh * 128:(hh + 1) * 128], PT[:, hh * 128:(hh + 1) * 128],
                    start=(j == 0 and hh == 0), stop=(j == nparts - 1 and hh == 1)), reads=[self.a_onesz, PT], writes=[pD])
            if j == nparts - 1:
                for (acc, pp) in ((accO, pO), (accD, pD)):
                    if first:
                        k.op(k.dve, lambda e, acc=acc, pp=pp: e.tensor_copy(out=acc[:, qsl], in_=pp[:, 0:128]),
                             reads=[pp], writes=[acc])
                    else:
                        k.op(k.dve, lambda e, acc=acc, pp=pp: e.tensor_tensor(
                            out=acc[:, qsl], in0=acc[:, qsl], in1=pp[:, 0:128], op=ALU.add),
                            reads=[pp, acc], writes=[acc])

        npt = len(parts)
        for i in range(npt + LAG):
            if i < npt:
                stageA(i)
            if i >= LAG:
                stageB(i - LAG)

    def attn_finish(self, s_, hp):
        k = self.k
        SEG = self.SEG
        accO, accD, OTn = self.a_accO, self.a_accD, self.a_OTn
        k.op(k.dve, lambda e: e.tensor_scalar(out=accD[:, 0:SEG], in0=accD[:, 0:SEG], scalar1=self.es[:, hp:hp + 1],
                                              scalar2=None, op0=ALU.add), reads=[accD, self.es], writes=[accD])
        k.op(k.dve, lambda e: e.reciprocal(out=accD[:, 0:SEG], in_=accD[:, 0:SEG]), reads=[accD], writes=[accD])
        k.op(k.dve, lambda e: e.tensor_tensor(out=OTn[:, 0:SEG], in0=accO[:, 0:SEG], in1=accD[:, 0:SEG], op=ALU.mult),
             reads=[accO, accD], writes=[OTn])
        k.dma(k.sp, self.OT_d.ap()[hp * 128:(hp + 1) * 128, s_ * SEG:(s_ + 1) * SEG], OTn[:, 0:SEG], OTn,
              reads=[OTn], writes=[self.r_ot[hp][s_]])


    def pass_ssd(self, layer, src, dst):
        k = self.k
        j0 = layer // 4
        NT, SEG = self.NT, self.SEG
        A = self.warena
        WIN = 5184
        self.load_weights_cast(None, self.din["d_w_in"][j0], KC, WIN, 0)
        o = KC * WIN
        def carve(n):
            nonlocal o
            v = A[:, o:o + n]
            o += n
            return v
        dcv = Buf(carve(2 * 5 * 24).bitcast(F32).rearrange("p (a b) -> p a b", a=5), "dcv")
        ident = Buf(carve(128), "ident")
        dtb = Buf(carve(128).bitcast(F32), "dtb")
        aneg = Buf(carve(128).bitcast(F32), "aneg")
        dta_s = [Buf(carve(256).bitcast(F32), f"dtas{i}") for i in range(2)]
        tst = [Buf(carve(512), f"tst{i}") for i in range(3)]
        assert o <= self.WCOLS
        k.dma(k.sp, dcv[:], self.din["d_cv"], dcv, writes=[dcv])
        k.dma(k.pool, ident[:], self.din["d_consts"][:, 4, :], ident, writes=[ident])
        k.dma(k.sp, dtb[:], self.din["d_dtb_rep"], dtb, writes=[dtb])
        k.dma(k.sp, aneg[:], self.din["d_alog_rep"], aneg, writes=[aneg])
        k.op(k.act, lambda e: e.activation(out=aneg[:], in_=aneg[:], func=AF.Exp), reads=[aneg], writes=[aneg])
        k.op(k.dve, lambda e: e.tensor_scalar(out=aneg[:], in0=aneg[:], scalar1=-1.0, scalar2=None, op0=ALU.mult),
             reads=[aneg], writes=[aneg])
        xs = [self.hmid[m] if m < 20 else self.hmid[20 + m % 2] for m in range(24)]
        pxb = self.p_x[:].bitcast(BF16)
        winsd = self.windows(2, 1)
        self.front_a(src, winsd[0][0], winsd[0][1])
        for wid, (c0, W, o0, o1, fl, fr) in enumerate(winsd):
            self.front_b(W, layer * 2, fl, fr, 2, 1)
            if wid + 1 < len(winsd):
                self.front_a(src, winsd[wid + 1][0], winsd[wid + 1][1])
            Wo = W - 3
            for m in range(24):
                pa = self.p_a[m % 2]
                for kk in range(KC):
                    oo = kk * WIN + DIN + m * 128
                    k.op(k.pe, lambda e, kk=kk, oo=oo, pa=pa: e.matmul(
                        pa[:, 0:W], A[:, oo:oo + 128], self.hT[kk][:, 0:W], start=(kk == 0), stop=(kk == KC - 1)),
                        reads=[self.hT[kk], self.wbuf], writes=[pa])
                g = self.gbuf[m % 3]
                k.op(k.act, lambda e, m=m, pa=pa, g=g: e.activation(
                    out=g[:, 0:Wo], in_=pa[:, 2:W - 1], func=AF.Identity, bias=dcv[:, 4, m:m + 1], scale=dcv[:, 2, m:m + 1]),
                    reads=[pa, dcv], writes=[g])
                for (tap, a0) in ((0, 0), (1, 1), (3, 3)):
                    k.op(k.dve, lambda e, m=m, pa=pa, g=g, tap=tap, a0=a0: e.scalar_tensor_tensor(
                        out=g[:, 0:Wo], in0=pa[:, a0:a0 + Wo], scalar=dcv[:, tap, m:m + 1], in1=g[:, 0:Wo],
                        op0=ALU.mult, op1=ALU.add), reads=[pa, dcv, g], writes=[g])
                k.op(k.act, lambda e, m=m, g=g: e.activation(out=xs[m][:, 0:Wo], in_=g[:, 0:Wo], func=AF.Silu),
                     reads=[g], writes=[xs[m]])
                if m >= 16:
                    dt_ = self.BT_d if m < 20 else self.CT_d
                    mm = (m - 16) % 4
                    k.dma(k.sp, dt_.ap()[mm * 128:(mm + 1) * 128, o0:o1], xs[m][:, 0:Wo], xs[m], reads=[xs[m]])
            for ib, a in enumerate(range(o0, o1, 128)):
                b = min(o1, a + 128)
                nb = b - a
                la, lb = a - o0, b - o0
                ha, hb = a - c0, b - c0
                for q4 in range(5):
                    for i in range(4):
                        m = q4 * 4 + i
                        k.op(k.pe, lambda e, m=m, i=i, la=la, lb=lb, nb=nb: e.transpose(
                            pxb[0:nb, i * 128:(i + 1) * 128], xs[m][:, la:lb], ident[:]),
                            reads=[xs[m], ident], writes=[self.p_x])
                    st = tst[q4 % 3]
                    k.op(k.act if q4 % 2 == 0 else k.dve,
                         (lambda e, st=st, nb=nb: e.activation(out=st[0:nb, :], in_=pxb[0:nb, 0:512], func=AF.Copy)) if q4 % 2 == 0
                         else (lambda e, st=st, nb=nb: e.tensor_copy(out=st[0:nb, :], in_=pxb[0:nb, 0:512])),
                         reads=[self.p_x], writes=[st])
                    k.dma(k.sp, self.X_d.ap()[a:b, q4 * 512:(q4 + 1) * 512], st[0:nb, :], st, reads=[st])
                for q4 in range(4):
                    pz = self.p_u[q4 % 2]
                    for kk in range(KC):
                        oo = kk * WIN + q4 * 512
                        k.op(k.pe, lambda e, kk=kk, oo=oo, pz=pz, ha=ha, hb=hb, nb=nb: e.matmul(
                            pz[0:nb, 0:512], self.hT[kk][:, ha:hb], A[:, oo:oo + 512], start=(kk == 0), stop=(kk == KC - 1)),
                            reads=[self.hT[kk], self.wbuf], writes=[pz])
                    st = tst[(q4 + 2) % 3]
                    k.op(k.act, lambda e, st=st, pz=pz, nb=nb: e.activation(out=st[0:nb, :], in_=pz[0:nb, 0:512], func=AF.Silu),
                         reads=[pz], writes=[st])
                    k.dma(k.sp, self.Z_d.ap()[a:b, q4 * 512:(q4 + 1) * 512], st[0:nb, :], st, reads=[st])
                pd = self.p_o[ib % 2]
                for kk in range(KC):
                    oo = kk * WIN + DIN + 3072
                    k.op(k.pe, lambda e, kk=kk, oo=oo, pd=pd, ha=ha, hb=hb, nb=nb: e.matmul(
                        pd[0:nb, 0:64], self.hT[kk][:, ha:hb], A[:, oo:oo + 64], start=(kk == 0), stop=(kk == KC - 1)),
                        reads=[self.hT[kk], self.wbuf], writes=[pd])
                ds = dta_s[ib % 2]
                k.op(k.dve, lambda e, ds=ds, pd=pd, nb=nb: e.tensor_tensor(out=ds[0:nb, 0:64], in0=pd[0:nb, 0:64], in1=dtb[0:nb, :], op=ALU.add),
                     reads=[pd, dtb], writes=[ds])
                k.op(k.act, lambda e, ds=ds, nb=nb: e.activation(out=ds[0:nb, 0:64], in_=ds[0:nb, 0:64], func=AF.Exp), reads=[ds], writes=[ds])
                k.op(k.act, lambda e, ds=ds, nb=nb: e.activation(out=ds[0:nb, 0:64], in_=ds[0:nb, 0:64], func=AF.Ln, bias=1.0), reads=[ds], writes=[ds])
                k.op(k.dve, lambda e, ds=ds, nb=nb: e.tensor_tensor(out=ds[0:nb, 64:128], in0=ds[0:nb, 0:64], in1=aneg[0:nb, :], op=ALU.mult),
                     reads=[ds, aneg], writes=[ds])
                k.dma(k.sp, self.DTA_d.ap()[a:b, :], ds[0:nb, :], ds, reads=[ds])
        k.barrier()
        o = 0
        cst = Buf(carve(4 * 256).bitcast(F32).rearrange("p (a b) -> p a b", a=4), "cst")
        ident = Buf(carve(128), "ident2")
        ones32 = Buf(carve(256).bitcast(F32), "ones32")
        skp = Buf(carve(64).bitcast(F32), "skp")
        ngr = Buf(carve(2 * DIN).bitcast(F32), "ngr")
        NB2 = 2
        xtokS = [Buf(carve(2560), f"xtok{i}") for i in range(NB2)]
        BTtS = [Buf(carve(512).rearrange("p (g t) -> p g t", g=4), f"BTt{i}") for i in range(NB2)]
        CTtS = [Buf(carve(512).rearrange("p (g t) -> p g t", g=4), f"CTt{i}") for i in range(NB2)]
        dtaS = [Buf(carve(256).bitcast(F32), f"dta{i}") for i in range(NB2)]
        ztS = [Buf(carve(DIN), f"zt{i}") for i in range(NB2)]
        yaccS = [Buf(carve(2 * DIN).bitcast(F32), f"yacc{i}") for i in range(NB2)]
        hst = [Buf(carve(2 * 512).bitcast(F32), f"hst{g}") for g in range(4)]
        hbf = [Buf(carve(512), f"hbf{g}") for g in range(4)]
        xdt = Buf(carve(DIN), "xdt")
        xsk = Buf(carve(DIN), "xsk")
        xdd = [Buf(carve(512), f"xdd{i}") for i in range(2)]
        NR = 3
        Ue = [Buf(carve(1024).bitcast(F32), f"Ue{i}") for i in range(NR)]
        Dm = [Buf(carve(1024).bitcast(F32), f"Dm{i}") for i in range(NR)]
        MT = [Buf(carve(512), f"MT{i}") for i in range(NR)]
        CBs = [Buf(carve(256).bitcast(F32), f"CB{g}") for g in range(4)]
        acsS = [Buf(carve(64).bitcast(F32), f"acs{i}") for i in range(2)]
        eaS = [Buf(carve(64).bitcast(F32), f"ea{i}") for i in range(2)]
        aclS = [Buf(carve(64).bitcast(F32), f"acl{i}") for i in range(2)]
        cdr = Buf(carve(64).bitcast(F32), "cdr")
        dec = Buf(carve(64).bitcast(F32), "dec")
        tmpg = [Buf(carve(1024).bitcast(F32), f"tmpg{i}") for i in range(2)]
        junk = Buf(carve(DIN), "junk")
        ssq = Buf(carve(4).bitcast(F32), "ssq")
        yn = Buf(carve(DIN), "yn")
        yst = [Buf(carve(512).rearrange("p (a t) -> p a t", a=4), f"yst{i}") for i in range(2)]
        assert o <= self.WCOLS, o
        k.dma(k.sp, cst[:], self.din["d_consts"][:, 0:4, :], cst, writes=[cst])
        k.dma(k.pool, ident[:], self.din["d_consts"][:, 4, :], ident, writes=[ident])
        k.op(k.dve, lambda e: e.memset(ones32[:], 1.0), writes=[ones32])
        k.dma(k.sp, skp[:], self.din["d_skip_rep"], skp, writes=[skp])
        k.dma(k.sp, ngr[:], self.din["d_normg_rep"], ngr, writes=[ngr])
        NCH = SEG // 128
        pyb = self.p_x[:].bitcast(BF16)

        def bc(ap2, n):
            return ap2.unsqueeze(2).broadcast_to([128, ap2.shape[1], n])

        def v3(ap2):
            return ap2.rearrange("p (e q) -> p e q", q=64)

        def q4v(ap2):
            return ap2.rearrange("p (a l) -> p a l", a=4)

        for d in range(2):
            order = list(range(NSEG * NCH))
            if d == 1:
                order = order[::-1]

            def emit_loads(ci):
                c = order[ci]
                t0 = c * 128
                sl = ci % NB2
                k.dma(k.sp, xtokS[sl][:], self.X_d.ap()[t0:t0 + 128, :], xtokS[sl], writes=[xtokS[sl]])
                k.dma(k.sp, BTtS[sl][:], self.BT_d.ap()[:, t0:t0 + 128].rearrange("(g n) t -> n g t", g=4), BTtS[sl], writes=[BTtS[sl]])
                k.dma(k.sp, CTtS[sl][:], self.CT_d.ap()[:, t0:t0 + 128].rearrange("(g n) t -> n g t", g=4), CTtS[sl], writes=[CTtS[sl]])
                k.dma(k.sp, dtaS[sl][:], self.DTA_d.ap()[t0:t0 + 128, :], dtaS[sl], writes=[dtaS[sl]])
                if d == 1:
                    k.dma(k.sp, yaccS[sl][:], self.Y_d.ap()[t0:t0 + 128, :], yaccS[sl], writes=[yaccS[sl]])
                    k.dma(k.sp, ztS[sl][:], self.Z_d.ap()[t0:t0 + 128, :], ztS[sl], writes=[ztS[sl]])

            emit_loads(0)
            for ci, c in enumerate(order):
                t0 = c * 128
                sg, cc = c // NCH, c % NCH
                sl = ci % NB2
                xtok, BTt, CTt, dta, zt, yacc = xtokS[sl], BTtS[sl], CTtS[sl], dtaS[sl], ztS[sl], yaccS[sl]
                acs, ea, acl = acsS[ci % 2], eaS[ci % 2], aclS[ci % 2]
                if ci + 1 < len(order):
                    emit_loads(ci + 1)
                start_seq = (cc == 0) if d == 0 else (cc == NCH - 1)
                if start_seq:
                    joined = (sg == 1) if d == 0 else (sg == 0)
                    for g in range(4):
                        if joined:
                            k.op(k.dve, lambda e, g=g: e.tensor_scalar(out=hst[g][:], in0=hst[g][:], scalar1=self.flags[:, 1:2],
                                                                      scalar2=None, op0=ALU.mult), reads=[hst[g], self.flags], writes=[hst[g]])
                        else:
                            k.op(k.dve, lambda e, g=g: e.memset(hst[g][:], 0.0), writes=[hst[g]])
                        k.op(k.act, lambda e, g=g: e.activation(out=hbf[g][:], in_=hst[g][:], func=AF.Copy), reads=[hst[g]], writes=[hbf[g]])
                adt = dta[:, 64 + d * 32:64 + (d + 1) * 32]
                dtv = dta[:, d * 32:(d + 1) * 32]
                tri = cst[:, d, :]
                msk = cst[:, 2 + d, :]
                last = 127 if d == 0 else 0
                pcs = self.p_ss
                k.op(k.pe, lambda e, tri=tri, adt=adt: e.matmul(pcs[:, 0:32], tri, adt, start=True, stop=True),
                     reads=[cst, dta], writes=[pcs])
                k.op(k.act, lambda e, acs=acs: e.activation(out=acs[:], in_=pcs[:, 0:32], func=AF.Copy), reads=[pcs], writes=[acs])
                k.op(k.act, lambda e, ea=ea: e.activation(out=ea[:], in_=pcs[:, 0:32], func=AF.Exp), reads=[pcs], writes=[ea])
                for g in range(4):
                    k.op(k.pe, lambda e, g=g, BTt=BTt, CTt=CTt: e.matmul(self.p_o[g % 2][:, 0:128], BTt[:, g, :], CTt[:, g, :], start=True, stop=True),
                         reads=[BTt, CTt], writes=[self.p_o[g % 2]])
                    k.op(k.act, lambda e, g=g: e.activation(out=CBs[g][:], in_=self.p_o[g % 2][:, 0:128], func=AF.Copy),
                         reads=[self.p_o[g % 2]], writes=[CBs[g]])

                def stA(qd):
                    e0 = qd * 4
                    ue, pac = Ue[qd % NR], self.p_a[qd % 2]
                    k.op(k.dve, lambda e: e.tensor_tensor(
                        out=q4v(ue[:]), in0=tri.unsqueeze(1).broadcast_to([128, 4, 128]),
                        in1=bc(adt[:, e0:e0 + 4], 128), op=ALU.mult), reads=[cst, dta], writes=[ue])
                    k.op(k.pe, lambda e: e.matmul(pac[:, 0:512], ones32[:], ue[:], start=True, stop=True),
                         reads=[ones32, ue], writes=[pac])

                def stB1(qd):
                    e0 = qd * 4
                    dm, pac = Dm[qd % NR], self.p_a[qd % 2]
                    for i in range(4):
                        k.op(k.dve, lambda e, i=i: e.scalar_tensor_tensor(
                            out=dm[:, i * 128:(i + 1) * 128], in0=pac[:, i * 128:(i + 1) * 128], scalar=acs[:, e0 + i:e0 + i + 1],
                            in1=msk, op0=ALU.subtract, op1=ALU.min), reads=[pac, acs, cst], writes=[dm])
                    pl = q4v(pac[:, 0:512])[:, :, last:last + 1]
                    k.op(k.act, lambda e: e.activation(out=dm[:], in_=dm[:], func=AF.Exp), reads=[dm], writes=[dm])
                    k.op(k.act, lambda e: e.activation(out=acl[:, e0:e0 + 4].unsqueeze(2), in_=pl, func=AF.Copy),
                         reads=[pac], writes=[acl])

                def stB2(qd):
                    g = qd // 2
                    e0 = qd * 4
                    dm, mt = Dm[qd % NR], MT[qd % NR]
                    pdg = self.p_u[g % 2]
                    k.op(k.dve, lambda e: e.tensor_tensor(
                        out=q4v(mt[:]), in0=q4v(dm[:]), in1=CBs[g][:].unsqueeze(1).broadcast_to([128, 4, 128]), op=ALU.mult),
                        reads=[dm, CBs[g]], writes=[mt])
                    for i in range(4):
                        eh = e0 + i
                        k.op(k.pe, lambda e, i=i, eh=eh: e.matmul(
                            pdg[:, (eh % 8) * 64:(eh % 8) * 64 + 64], mt[:, i * 128:(i + 1) * 128], xdt[:, eh * 64:(eh + 1) * 64],
                            start=True, stop=True), reads=[mt, xdt], writes=[pdg])

                def stC(g):
                    pdg = self.p_u[g % 2]
                    pof = self.p_o[g % 2]
                    k.op(k.pe, lambda e: e.matmul(pof[:, 0:512], CTt[:, g, :], hbf[g][:], start=True, stop=True),
                         reads=[CTt, hbf[g]], writes=[pof])
                    tg = tmpg[g % 2]
                    ysl = yacc[:, g * 512:(g + 1) * 512]
                    k.op(k.dve, lambda e: e.tensor_tensor(out=v3(tg[:]), in0=v3(pof[:, 0:512]),
                                                          in1=bc(ea[:, g * 8:(g + 1) * 8], 64), op=ALU.mult), reads=[pof, ea], writes=[tg])
                    if d == 0:
                        k.op(k.dve, lambda e: e.tensor_tensor(out=ysl, in0=tg[:], in1=pdg[:, 0:512], op=ALU.add),
                             reads=[tg, pdg], writes=[yacc])
                    else:
                        k.op(k.dve, lambda e: e.tensor_tensor(out=ysl, in0=ysl, in1=tg[:], op=ALU.add),
                             reads=[tg, yacc], writes=[yacc])
                        k.op(k.dve, lambda e: e.tensor_tensor(out=ysl, in0=ysl, in1=pdg[:, 0:512], op=ALU.add),
                             reads=[pdg, yacc], writes=[yacc])
                    g8 = slice(g * 8, (g + 1) * 8)
                    k.op(k.dve, lambda e: e.tensor_tensor(out=dec[:, g8], in0=acl[:, g8], in1=acs[:, g8], op=ALU.subtract),
                         reads=[acl, acs], writes=[dec])
                    k.op(k.act, lambda e: e.activation(out=dec[:, g8], in_=dec[:, g8], func=AF.Exp), reads=[dec], writes=[dec])
                    k.op(k.act, lambda e: e.activation(out=cdr[:, g8], in_=acl[:, g8], func=AF.Exp), reads=[acl], writes=[cdr])
                    xd = xdd[g % 2]
                    k.op(k.dve, lambda e: e.tensor_tensor(out=v3(xd[:]), in0=v3(xdt[:, g * 512:(g + 1) * 512]),
                                                          in1=bc(dec[:, g8], 64), op=ALU.mult), reads=[xdt, dec], writes=[xd])
                    pst = self.p_x
                    k.op(k.pe, lambda e: e.matmul(pst[:, 0:512], xtok[:, DIN + g * 128:DIN + (g + 1) * 128], xd[:],
                                                  start=True, stop=True), reads=[xtok, xd], writes=[pst])
                    k.op(k.dve, lambda e: e.tensor_tensor(out=v3(hst[g][:]), in0=v3(hst[g][:]), in1=bc(cdr[:, g8], 64), op=ALU.mult),
                         reads=[hst[g], cdr, hbf[g]], writes=[hst[g]])
                    k.op(k.dve, lambda e: e.tensor_tensor(out=hst[g][:], in0=hst[g][:], in1=pst[:, 0:512], op=ALU.add),
                         reads=[hst[g], pst], writes=[hst[g]])
                    k.op(k.act, lambda e: e.activation(out=hbf[g][:], in_=hst[g][:], func=AF.Copy), reads=[hst[g]], writes=[hbf[g]])

                stA(0)
                stA(1)
                k.op(k.dve, lambda e, dtv=dtv, xtok=xtok: e.tensor_tensor(out=v3(xdt[:]), in0=v3(xtok[:, 0:DIN]), in1=bc(dtv, 64), op=ALU.mult),
                     reads=[xtok, dta], writes=[xdt])
                for it in range(2, 8 + 3):
                    if it - 2 < 8:
                        stB1(it - 2)
                    if it < 8:
                        stA(it)
                    if 0 <= it - 3 < 8:
                        stB2(it - 3)
                        if (it - 3) % 2 == 1:
                            stC((it - 3) // 2)
                if d == 0:
                    k.dma(k.sp, self.Y_d.ap()[t0:t0 + 128, :], yacc[:], yacc, reads=[yacc])
                else:
                    k.op(k.dve, lambda e, xtok=xtok: e.tensor_tensor(out=v3(xsk[:]), in0=v3(xtok[:, 0:DIN]), in1=bc(skp[:], 64), op=ALU.mult),
                         reads=[xtok, skp], writes=[xsk])
                    k.op(k.dve, lambda e, yacc=yacc: e.tensor_tensor(out=yacc[:], in0=yacc[:], in1=xsk[:], op=ALU.add), reads=[yacc, xsk], writes=[yacc])
                    k.op(k.dve, lambda e, yacc=yacc, zt=zt: e.tensor_tensor(out=yacc[:], in0=yacc[:], in1=zt[:], op=ALU.mult), reads=[yacc, zt], writes=[yacc])
                    k.op(k.act, lambda e, yacc=yacc: e.activation(out=junk[:], in_=yacc[:], func=AF.Square, accum_out=ssq[:, 0:1]),
                         reads=[yacc], writes=[junk, ssq])
                    k.op(k.act, lambda e: e.activation(out=ssq[:, 0:1], in_=ssq[:, 0:1], func=AF.Sqrt, bias=EPS, scale=1.0 / DIN),
                         reads=[ssq], writes=[ssq])
                    k.op(k.dve, lambda e: e.reciprocal(out=ssq[:, 0:1], in_=ssq[:, 0:1]), reads=[ssq], writes=[ssq])
                    k.op(k.dve, lambda e, yacc=yacc: e.scalar_tensor_tensor(out=yn[:], in0=yacc[:], scalar=ssq[:, 0:1], in1=ngr[:],
                                                                            op0=ALU.mult, op1=ALU.mult), reads=[yacc, ssq, ngr], writes=[yn])
                    for q4 in range(4):
                        for i in range(4):
                            kk = q4 * 4 + i
                            k.op(k.pe, lambda e, kk=kk, i=i: e.transpose(pyb[:, i * 128:(i + 1) * 128], yn[:, kk * 128:(kk + 1) * 128], ident[:]),
                                 reads=[yn, ident], writes=[self.p_x])
                        st = yst[q4 % 2]
                        k.op(k.act, lambda e, st=st: e.activation(out=st[:].rearrange("p a t -> p (a t)"), in_=pyb[:, 0:512], func=AF.Copy),
                             reads=[self.p_x], writes=[st])
                        k.dma(k.sp, self.YT_d.ap()[q4 * 512:(q4 + 1) * 512, t0:t0 + 128].rearrange("(a p) t -> p a t", p=128),
                              st[:], st, reads=[st])
            k.barrier()
        self.load_weights_cast(None, self.din["d_w_out"][j0], 16, D, 0)
        for (c0, W, o0, o1, fl, fr) in self.windows(0):
            srcap = self.aps[src][:, c0 + PAD:c0 + PAD + W].rearrange("(k p) w -> p k w", p=128)
            k.dma(k.sp, self.xt[:, :, 0:W], srcap, self.xt, reads=self.regions(src, c0 + PAD, c0 + PAD + W), writes=[self.xt])
            for m in range(16):
                k.dma(k.sp, self.hmid[m][:, 0:W], self.YT_d.ap()[m * 128:(m + 1) * 128, c0:c0 + W], self.hmid[m],
                      writes=[self.hmid[m]])
            self.back_out(dst, 0, 16, self.hmid, W, o0, o1, False, h=0)

    def pass_conv(self, layer, src, dst):
        k = self.k
        j = layer // 4
        in_off = 0
        out_off = KC * 3 * D
        self.load_weights_cast(None, self.din["c_w_in"][j], KC, 3 * D, in_off)
        self.load_weights_cast(None, self.din["c_w_out"][j], KC, D, out_off)
        cv = self.cvec
        k.dma(k.sp, cv[:, 0:3, 0:KC], self.din["c_cv"], cv, writes=[cv])
        for (c0, W, o0, o1, fl, fr) in self.windows(1):
            self.front(src, c0, W, layer * 2 + 0, fl, fr)
            Wo = W - 2
            for m in range(KC):
                pb, pc, px = (self.p_x, self.p_ss)[m % 2], self.p_a[m % 2], self.p_u[m % 2]
                for (pp, coff) in ((pc, D + m * 128), (px, 2 * D + m * 128), (pb, m * 128)):
                    for kk in range(KC):
                        o = in_off + kk * 3 * D + coff
                        k.op(k.pe, lambda e, kk=kk, o=o, pp=pp: e.matmul(
                            pp[:, 0:W], self.warena[:, o:o + 128], self.hT[kk][:, 0:W],
                            start=(kk == 0), stop=(kk == KC - 1)),
                            reads=[self.hT[kk], self.wbuf], writes=[pp], signal=(kk == KC - 1))
                g = self.gbuf[m % 3]
                g2 = self.xo[m % 3]
                k.op(k.act, lambda e, pc=pc, g=g: e.activation(out=g[:, 0:W], in_=pc[:, 0:W], func=AF.Copy),
                     reads=[pc], writes=[g])
                k.op(k.dve, lambda e, px=px, g=g: e.tensor_tensor(out=g[:, 0:W], in0=g[:, 0:W], in1=px[:, 0:W],
                                                                  op=ALU.mult), reads=[g, px], writes=[g])
                k.op(k.dve, lambda e, m=m, g=g, g2=g2: e.tensor_scalar(
                    out=g2[:, 0:Wo], in0=g[:, 1:W - 1], scalar1=cv[:, 1, m:m + 1], scalar2=None, op0=ALU.mult),
                    reads=[g, cv], writes=[g2])
                k.op(k.dve, lambda e, m=m, g=g, g2=g2: e.scalar_tensor_tensor(
                    out=g2[:, 0:Wo], in0=g[:, 0:W - 2], scalar=cv[:, 0, m:m + 1], in1=g2[:, 0:Wo],
                    op0=ALU.mult, op1=ALU.add), reads=[g, cv, g2], writes=[g2])
                k.op(k.dve, lambda e, m=m, g=g, g2=g2: e.scalar_tensor_tensor(
                    out=g2[:, 0:Wo], in0=g[:, 2:W], scalar=cv[:, 2, m:m + 1], in1=g2[:, 0:Wo],
                    op0=ALU.mult, op1=ALU.add), reads=[g, cv, g2], writes=[g2])
                k.op(k.dve, lambda e, m=m, pb=pb, g2=g2: e.tensor_tensor(
                    out=self.hmid[m][:, 1:W - 1], in0=g2[:, 0:Wo], in1=pb[:, 1:W - 1], op=ALU.mult),
                    reads=[g2, pb], writes=[self.hmid[m]])
            self.back_out(dst, out_off, KC, self.hmid, W, o0, o1, False)


_PROG_CACHE = {}


def core_tokens(x_prompt, x_sample, c):
    if c < 4:
        return np.concatenate([x_prompt[c], x_sample[c]], axis=0), 1.0
    b = 4 + 3 * (c - 4)
    return np.concatenate([x_sample[b], x_sample[b + 1], x_sample[b + 2]], axis=0), 0.0


def host_layout(inputs):
    f = lambda a: np.asarray(a, np.float32)
    out = {}
    for nm in ("c_w_in", "c_w_out", "ffn_w_up", "ffn_w_down", "a_w_qkv", "a_w_o", "b_w_qkv", "b_w_o"):
        out[nm] = np.ascontiguousarray(f(inputs[nm]))
    ng = np.concatenate([f(inputs["norm_g"]).reshape(8, D), f(inputs["final_g"]).reshape(1, D)], 0)
    out["normg_h"] = np.ascontiguousarray(ng.reshape(9, KC, 128).transpose(2, 0, 1))
    cw = f(inputs["ffn_conv_w"]).reshape(4, 3, MC, 128)
    cb = f(inputs["ffn_conv_b"]).reshape(4, 1, MC, 128)
    out["ffn_cv"] = np.ascontiguousarray(np.concatenate([cw, cb], 1).transpose(0, 3, 1, 2))
    out["relb_rep"] = np.ascontiguousarray(np.broadcast_to(f(inputs["rel_bias"]).reshape(1, 512), (128, 512)))
    sk = f(inputs["b_sink"])[0].reshape(8, 2)
    out["sink_h"] = np.ascontiguousarray(np.repeat(sk.T, 64, axis=0))
    out["oh_mats"] = _OH_MATS
    for nm in ("d_w_in", "d_w_out"):
        out[nm] = np.ascontiguousarray(f(inputs[nm]))
    dw = f(inputs["d_conv_w"])[0].reshape(4, 24, 128)
    db = f(inputs["d_conv_b"])[0].reshape(1, 24, 128)
    out["d_cv"] = np.ascontiguousarray(np.concatenate([dw, db], 0).transpose(2, 0, 1))
    ii = np.arange(128)
    U = (ii[:, None] <= ii[None, :]).astype(np.float32)
    L = (ii[:, None] >= ii[None, :]).astype(np.float32)
    out["d_consts"] = np.ascontiguousarray(np.stack([U, L, (U - 1.0) * 1e4, (L - 1.0) * 1e4, np.eye(128, dtype=np.float32)], 1))
    rep = lambda v: np.ascontiguousarray(np.broadcast_to(v.reshape(1, -1), (128, v.size)))
    out["d_dtb_rep"] = rep(f(inputs["d_dt_bias"])[0])
    out["d_alog_rep"] = rep(f(inputs["d_a_log"])[0])
    out["d_skip_rep"] = rep(f(inputs["d_skip"])[0])
    out["d_normg_rep"] = rep(f(inputs["d_norm_g"])[0])
    out["c_cv"] = np.ascontiguousarray(f(inputs["c_conv_w"])[0].reshape(3, KC, 128).transpose(2, 0, 1))
    return out


def run(inputs, seg, layers=(0, 1, 2, 3), do_final=True):
    key = (seg, tuple(layers), do_final)
    prog = Prog(seg, layers, do_final)
    NT = 3 * seg
    xp = np.asarray(inputs["x_prompt"], np.float32)
    xs = np.asarray(inputs["x_sample"], np.float32)
    in_maps = []
    for c in range(8):
        tok, J = core_tokens(xp, xs, c)
        xT = np.zeros((D, NT + 2 * PAD), np.float32)
        xT[:, PAD:NT + PAD] = tok.T
        fl = np.zeros((128, 8), np.float32)
        fl[:, 1] = J
        fl[:, 2] = J
        f2 = np.ones((128, 12), np.float32)
        for sgi, (fL, fR) in enumerate(((0.0, J), (J, 0.0), (0.0, 0.0))):
            f2[0:64, sgi * 4 + 0] = fL
            f2[:, sgi * 4 + 1] = fL
            f2[64:128, sgi * 4 + 2] = fR
            f2[:, sgi * 4 + 3] = fR
        m = {"xT": xT, "flags": fl, "flags2": f2}
        m.update(host_layout(inputs))
        m = {kk: v for kk, v in m.items() if kk in prog.din}
        assert set(m) == set(prog.din), (set(prog.din) - set(m))
        in_maps.append(m)
    res = run_bass_kernel_spmd(prog.k.nc, in_maps, core_ids=list(range(8)))
    yp = np.zeros_like(xp)
    ys = np.zeros_like(xs)
    for c in range(8):
        y = np.asarray(res.results[c]["yT"]).T
        if c < 4:
            yp[c] = y[:2 * seg]
            ys[c] = y[2 * seg:]
        else:
            b = 4 + 3 * (c - 4)
            for i in range(3):
                ys[b + i] = y[i * seg:(i + 1) * seg]
    return yp, ys


def kernel(**inputs):
    return run(inputs, 4096)
```

```python
import numpy as np
import ml_dtypes
import concourse.bass as bass
import concourse.mybir as mybir
from concourse.bass_utils import run_bass_kernel_spmd

F32 = mybir.dt.float32
BF16 = mybir.dt.bfloat16
AF = mybir.ActivationFunctionType
ALU = mybir.AluOpType

D = 1024
KC = 8
DFF = 2816
MC = 22
EPS = 1e-6
NSEG = 3
PAD = 2
DIN = 2048
import os
DBG_ONLY = os.environ.get("DBG_ONLY", "")


ATT_CFG = {"A": [(1, 64), (4, 64), (16, 64)], "B": [(1, 128)]}
ATT_CI = {"A": [0, 1, 2], "B": [3]}
ALL_CFG = [(1, 64), (4, 64), (16, 64), (1, 128)]


def t5_bucket_np(rel):
    half, exact = 16, 8
    n = np.abs(rel)
    log_ratio = np.log(np.maximum(n, 1) / exact) / np.log(1024 / exact)
    large = np.minimum(exact + (log_ratio * (half - exact)).astype(np.int64), half - 1)
    return np.where(rel > 0, half, 0) + np.where(n < exact, n, large)


def oh_struct():
    keys, mats = [], []
    for ci, (r, half) in enumerate(ALL_CFG):
        nparts = 2 * half // 128 + 1
        for j in range(nparts):
            rel = 128 * j - half + np.arange(128)[:, None] - np.arange(128)[None, :]
            valid = np.abs(rel) <= half
            bk = t5_bucket_np(rel * r)
            for b in range(32):
                mk = valid & (bk == b)
                if mk.any():
                    keys.append((ci, j, b))
                    mats.append(mk.astype(np.float32))
            keys.append((ci, j, 32))
            mats.append((~valid).astype(np.float32))
    return keys, np.stack(mats)


OH_KEYS, _OH_MATS = oh_struct()


class Sem:
    def __init__(self, h, i):
        self.h = h
        self.i = i


class Eng:
    def __init__(self, name, h, sem):
        self.name = name
        self.h = h
        self.sem = sem
        self.cnt = 0
        self.seen = {}


class Buf:
    def __init__(self, ap, name=""):
        self.ap = ap
        self.name = name
        self.w = None
        self.r = {}
        self.dsem = None
        self.dcnt = 0
        self.psum = False

    def __getitem__(self, idx):
        return self.ap[idx]


class K:
    def __init__(self):
        self.nc = bass.Bass("TRN2", target_bir_lowering=False)
        nc = self.nc
        self.nsem = 0
        self.pe = Eng("pe", nc.tensor, self.newsem("pe"))
        self.act = Eng("act", nc.scalar, self.newsem("act"))
        self.dve = Eng("dve", nc.vector, self.newsem("dve"))
        self.pool = Eng("pool", nc.gpsimd, self.newsem("pool"))
        self.sp = Eng("sp", nc.sync, self.newsem("sp"))
        self.out_tokens = []
        self.nalloc = 0
        self.dma_latest = {}

    def newsem(self, name):
        s = Sem(self.nc.alloc_semaphore(f"s_{name}_{self.nsem}"), self.nsem)
        self.nsem += 1
        return s

    def sb(self, name, shape, dt):
        self.nalloc += 1
        t = self.nc.alloc_sbuf_tensor(f"{name}_{self.nalloc}", list(shape), dt)
        return t

    def ps(self, name):
        self.nalloc += 1
        t = self.nc.alloc_psum_tensor(f"{name}_{self.nalloc}", [128, 512], F32)
        b = Buf(t[:], name)
        b.psum = True
        return b

    def _wait(self, eng, deps):
        for (sem, val) in deps:
            if sem is eng.sem and eng.name == "pe":
                continue
            if eng.seen.get(sem.i, 0) < val:
                eng.h.wait_ge(sem.h, val)
                eng.seen[sem.i] = val

    @staticmethod
    def _deps(reads, writes):
        deps = []
        for b in reads:
            if b.w is not None:
                deps.append(b.w)
            if b.psum:
                deps.extend(b.r.values())
        for b in writes:
            if b.w is not None:
                deps.append(b.w)
            deps.extend(b.r.values())
        return deps

    @staticmethod
    def _mark(tok, reads, writes):
        for b in reads:
            b.r[tok[0].i] = tok
        for b in writes:
            b.w = tok
            b.r = {}

    def op(self, eng, fn, reads=(), writes=(), signal=True):
        self._wait(eng, self._deps(reads, writes))
        ins = fn(eng.h)
        tok = (eng.sem, eng.cnt + 1)
        ins.then_inc(eng.sem.h, 1)
        eng.cnt += 1
        self._mark(tok, reads, writes)
        return tok

    def dma(self, q, out, in_, owner, reads=(), writes=(), is_output=False):
        if owner.dsem is None:
            owner.dsem = self.newsem("d" + owner.name)
        self._wait(q, [d for d in self._deps(reads, writes) if d[0] is not owner.dsem])
        ins = q.h.dma_start(out=out, in_=in_)
        ins.then_inc(owner.dsem.h, 16)
        owner.dcnt += 16
        tok = (owner.dsem, owner.dcnt)
        self.dma_latest[owner.dsem.i] = tok
        self._mark(tok, reads, writes)
        if is_output:
            self.out_tokens.append(tok)
        return tok

    def barrier(self):
        engs = [self.pe, self.act, self.dve, self.pool, self.sp]
        toks = [(e.sem, e.cnt) for e in engs if e.cnt] + list(self.dma_latest.values())
        for e in engs:
            self._wait(e, toks)

    def finish(self):
        self._wait(self.sp, self.out_tokens)
        for e in (self.pe, self.act, self.dve, self.pool):
            if e.cnt:
                self._wait(self.sp, [(e.sem, e.cnt)])


class Prog:
    def __init__(self, seg, layers=(0, 1, 2, 3), do_final=True):
        self.SEG = seg
        self.NT = NSEG * seg
        self.layers = layers
        self.do_final = do_final
        self.k = K()
        k = self.k
        nc = k.nc
        NT = self.NT
        self.din = {}

        def inp(name, shape):
            self.din[name] = nc.dram_tensor(name, list(shape), F32, kind="ExternalInput").ap()
            return self.din[name]

        self.xin = inp("xT", [D, NT + 2 * PAD])
        self.flags_d = inp("flags", [128, 8])
        inp("c_w_in", [1, D, 3 * D]); inp("c_w_out", [1, D, D])
        inp("ffn_w_up", [4, D, 2 * DFF]); inp("ffn_cv", [4, 128, 4, MC])
        inp("ffn_w_down", [4, DFF, D]); inp("normg_h", [128, 9, KC]); inp("c_cv", [128, 3, KC])
        inp("a_w_qkv", [1, D, 9 * D]); inp("a_w_o", [1, D, D]); inp("b_w_qkv", [1, D, 1536]); inp("b_w_o", [1, D, D])
        inp("relb_rep", [128, 512]); inp("sink_h", [128, 8]); inp("oh_mats", [len(OH_KEYS), 128, 128]); inp("flags2", [128, 12])
        self.QT_d = nc.dram_tensor("QT_d", [3, D, NT], BF16, kind="Internal")
        self.KT_d = nc.dram_tensor("KT_d", [3, D, NT], BF16, kind="Internal")
        self.V_d = nc.dram_tensor("V_d", [3, NT, D], BF16, kind="Internal")
        self.OT_d = nc.dram_tensor("OT_d", [D, NT], BF16, kind="Internal")
        nb5 = (NT + 511) // 512
        self.r_qt = [[[Buf(None, "rq") for _ in range(nb5)] for _ in range(KC)] for _ in range(3)]
        self.r_kt = [[[Buf(None, "rk") for _ in range(nb5)] for _ in range(KC)] for _ in range(3)]
        self.r_v = [[[Buf(None, "rv") for _ in range(NT // 128)] for _ in range(2)] for _ in range(3)]
        self.r_ot = [[Buf(None, "ro") for _ in range(NSEG)] for _ in range(KC)]
        inp("d_w_in", [1, D, 5184]); inp("d_w_out", [1, DIN, D]); inp("d_cv", [128, 5, 24]); inp("d_consts", [128, 5, 128])
        inp("d_dtb_rep", [128, 64]); inp("d_alog_rep", [128, 64]); inp("d_skip_rep", [128, 32]); inp("d_normg_rep", [128, DIN])
        self.X_d = nc.dram_tensor("X_d", [NT, 2560], BF16, kind="Internal")
        self.BT_d = nc.dram_tensor("BT_d", [512, NT], BF16, kind="Internal")
        self.CT_d = nc.dram_tensor("CT_d", [512, NT], BF16, kind="Internal")
        self.Z_d = nc.dram_tensor("Z_d", [NT, DIN], BF16, kind="Internal")
        self.DTA_d = nc.dram_tensor("DTA_d", [NT, 128], F32, kind="Internal")
        self.Y_d = nc.dram_tensor("Y_d", [NT, DIN], F32, kind="Internal")
        self.YT_d = nc.dram_tensor("YT_d", [DIN, NT], BF16, kind="Internal")
        self.yT = nc.dram_tensor("yT", [D, NT], F32, kind="ExternalOutput").ap()
        self.xA = nc.dram_tensor("xA", [D, NT + 2 * PAD], F32, kind="Internal").ap()
        self.xB = nc.dram_tensor("xB", [D, NT + 2 * PAD], F32, kind="Internal").ap()
        self.nreg = (NT + 2 * PAD + 511) // 512
        self.regs = {}
        for nm in ("xin", "xA", "xB", "yT"):
            self.regs[nm] = [[Buf(None, f"{nm}{n}_{i}") for i in range(self.nreg)] for n in range(KC)]
        self.aps = {"xin": self.xin, "xA": self.xA, "xB": self.xB, "yT": self.yT}

        self.ones = Buf(k.sb("ones", [128, 128], BF16)[:], "ones")
        self.flags = Buf(k.sb("flags", [128, 8], F32)[:], "flags")
        self.zer = Buf(k.sb("zer", [128, 8], F32)[:], "zer")
        self.xt = Buf(k.sb("xt", [128, KC, 512], F32)[:], "xt")
        self.hT = [Buf(None, f"hT{i}") for i in range(KC)]
        hT_t = k.sb("hT", [128, KC, 512], BF16)
        for i in range(KC):
            self.hT[i].ap = hT_t[:, i, :]
        self.hT_t = hT_t
        self.sq = [Buf(k.sb("sq", [128, 512], BF16)[:], f"sq{i}") for i in range(2)]
        self.rstd = Buf(k.sb("rstd", [128, 512], F32)[:], "rstd")
        self.gbuf = [Buf(k.sb("gb", [128, 512], F32)[:], f"gb{i}") for i in range(3)]
        self.xo = [Buf(k.sb("xo", [128, 512], F32)[:], f"xo{i}") for i in range(3)]
        self.normg = Buf(k.sb("normg", [128, 9, KC], F32)[:], "normg")
        self.p_ss = k.ps("pss")
        self.p_a = [k.ps("pa0"), k.ps("pa1")]
        self.p_u = [k.ps("pu0"), k.ps("pu1")]
        self.p_o = [k.ps("po0"), k.ps("po1")]
        self.p_x = k.ps("px")
        self.WCOLS = KC * 2 * DFF + MC * D
        self.warena = k.sb("warena", [128, self.WCOLS], BF16)
        self.wbuf = Buf(self.warena[:], "warena")
        self.hmid_t = k.sb("hmid", [128, MC, 512], BF16)
        self.hmid = [Buf(self.hmid_t[:, i, :], f"hm{i}") for i in range(MC)]
        self.cvec = Buf(k.sb("cvec", [128, 4, MC], F32)[:], "cvec")

        self.qs = [Buf(k.sb("qs", [128, 512], BF16)[:], f"qs{i}") for i in range(4)]
        self.xr = [Buf(k.sb("xr", [128, 512], F32)[:], f"xr{i}") for i in range(2)]
        self.flags2 = Buf(k.sb("flags2", [128, 12], F32)[:], "flags2")
        self.es = Buf(k.sb("es", [128, 8], F32)[:], "es")
        self.erb = Buf(k.sb("erb", [128, 512], F32)[:], "erb")
        A = self.warena
        SEGc = self.SEG
        Hm = 1024
        o = 0
        def carve(n):
            nonlocal o
            v = A[:, o:o + n]
            o += n
            return v
        self.a_KT = Buf(carve(SEGc + 2 * Hm), "aKT")
        self.a_QT = Buf(carve(2 * SEGc).rearrange("p (a t) -> p a t", a=2), "aQT")
        self.a_V = Buf(carve(48 * 256).rearrange("p (b c) -> p b c", c=256), "aV")
        self.a_Vb = [Buf(self.a_V.ap, f"aVb{i}") for i in range(12)]
        self.a_tab = Buf(carve(3 * 8 * 2 * 256), "atab")
        self.a_accO = Buf(carve(2 * SEGc).bitcast(F32), "accO")
        self.a_accD = Buf(carve(2 * SEGc).bitcast(F32), "accD")
        self.a_OTn = Buf(carve(SEGc), "OTn")
        self.a_pexp = [Buf(carve(512).bitcast(F32), f"pexp{i}") for i in range(4)]
        self.a_PT = [Buf(carve(256), f"PT{i}") for i in range(4)]
        self.a_oh = Buf(carve(32 * 128).rearrange("p (b c) -> p b c", c=128), "aoh")
        self.a_ttmp = Buf(carve(256).bitcast(F32), "ttmp")
        self.a_onesz = Buf(carve(256), "onesz")
        self.a_ident = Buf(carve(128), "aident")
        assert o <= self.WCOLS, o
        self.rr = 0
        with nc.allow_non_contiguous_dma(reason="few tiny halo/pad column transfers"):
            self.build()

    def regions(self, nm, c0, c1, n=None):
        out = []
        for nn in (range(KC) if n is None else [n]):
            out.extend(self.regs[nm][nn][c0 // 512:(c1 - 1) // 512 + 1])
        return out

    def build(self):
        k = self.k
        nc = k.nc
        k.op(k.dve, lambda e: e.memset(self.ones[:], 1.0), writes=[self.ones])
        k.op(k.dve, lambda e: e.memset(self.zer[:], 0.0), writes=[self.zer])
        k.dma(k.sp, self.flags[:], self.flags_d, self.flags, writes=[self.flags])
        k.dma(k.sp, self.flags2[:], self.din["flags2"], self.flags2, writes=[self.flags2])
        k.dma(k.sp, self.erb[:], self.din["relb_rep"], self.erb, writes=[self.erb])
        k.op(k.act, lambda e: e.activation(out=self.erb[:], in_=self.erb[:], func=AF.Identity, scale=8.0), reads=[self.erb], writes=[self.erb])
        k.dma(k.sp, self.normg[:], self.din["normg_h"], self.normg, writes=[self.normg])
        NT = self.NT
        for nm in ("xA", "xB"):
            for col in (0, 1, NT + PAD, NT + PAD + 1):
                dst = self.aps[nm][:, col:col + 1].rearrange("(k p) o -> p k o", p=128)
                k.dma(k.sp, dst, self.zer[:, 0:8].rearrange("p (k o) -> p k o", o=1), self.zer,
                      reads=[self.zer], writes=self.regions(nm, col, col + 1))
        cur = "xin"
        names = ["xB", "xA"]
        nlay = len(self.layers)
        for li, layer in enumerate(self.layers):
            kind = layer % 4
            dst = names[0]
            if DBG_ONLY == "ffn":
                kind = -1
            if kind == 2:
                self.pass_conv(layer, cur, dst)
            elif kind in (0, 1):
                self.pass_attention("A" if kind == 0 else "B", layer, cur, dst)
            elif kind == 3:
                self.pass_ssd(layer, cur, dst)
            if kind in (0, 1, 2, 3):
                cur = dst
                names = names[::-1]
                dst = names[0]
            last = (li == nlay - 1) and self.do_final
            if DBG_ONLY == "mixer":
                self.pass_copy_final(cur)
                continue
            self.pass_ffn(layer, cur, "yT" if last else dst, final=last)
            cur = dst
            names = names[::-1]
        k.finish()

    def windows(self, hl=1, hr=None):
        if hr is None:
            hr = hl
        SEG = self.SEG
        res = []
        for s in range(NSEG):
            a = s * SEG
            b = a + SEG
            o0 = a
            while o0 < b:
                o1 = min(b, o0 + 512 - hl - hr)
                c0 = o0 - hl
                W = (o1 + hr) - c0
                fl = (2 * s) if o0 == a else None
                fr = (2 * s + 1) if o1 == b else None
                res.append((c0, W, o0, o1, fl, fr))
                o0 = o1
        return res

    def load_weights_cast(self, dst_ap_fn, src, nk, ncols, wbuf_off):
        k = self.k
        step = 2048
        for kk in range(nk):
            for c in range(0, ncols, step):
                cc = min(step, ncols - c)
                o = wbuf_off + kk * ncols + c
                k.dma(k.pool, self.warena[:, o:o + cc], src[kk * 128:(kk + 1) * 128, c:c + cc],
                      self.wbuf, writes=[self.wbuf])

    def front(self, src, c0, W, gidx, fl, fr, hl=1, hr=1):
        self.front_a(src, c0, W)
        self.front_b(W, gidx, fl, fr, hl, hr)

    def front_a(self, src, c0, W):
        k = self.k
        xt, hT = self.xt, self.hT
        srcap = self.aps[src][:, c0 + PAD:c0 + PAD + W].rearrange("(k p) w -> p k w", p=128)
        k.dma(k.sp, xt[:, :, 0:W], srcap, xt, reads=self.regions(src, c0 + PAD, c0 + PAD + W), writes=[xt])
        for kk in range(KC):
            sq = self.sq[kk % 2]
            k.op(k.act, lambda e, kk=kk, sq=sq: e.activation(out=sq[:, 0:W], in_=xt[:, kk, 0:W], func=AF.Square),
                 reads=[xt], writes=[sq])
            k.op(k.pe, lambda e, kk=kk, sq=sq: e.matmul(self.p_ss[:, 0:W], self.ones[:], sq[:, 0:W],
                                                       start=(kk == 0), stop=(kk == KC - 1)),
                 reads=[sq, self.ones], writes=[self.p_ss], signal=(kk == KC - 1))
        rstd = self.rstd
        k.op(k.act, lambda e: e.activation(out=rstd[:, 0:W], in_=self.p_ss[:, 0:W], func=AF.Sqrt,
                                           bias=EPS, scale=1.0 / D), reads=[self.p_ss], writes=[rstd])
        k.op(k.dve, lambda e: e.reciprocal(out=rstd[:, 0:W], in_=rstd[:, 0:W]), reads=[rstd], writes=[rstd])

    def front_b(self, W, gidx, fl, fr, hl=1, hr=1):
        k = self.k
        xt, hT, rstd = self.xt, self.hT, self.rstd
        for kk in range(KC):
            k.op(k.dve, lambda e, kk=kk: e.scalar_tensor_tensor(
                out=hT[kk][:, 0:W], in0=xt[:, kk, 0:W], scalar=self.normg[:, gidx, kk:kk + 1],
                in1=rstd[:, 0:W], op0=ALU.mult, op1=ALU.mult),
                reads=[xt, rstd, self.normg], writes=[hT[kk]])
        for (f, col, wd) in ((fl, 0, hl), (fr, W - hr, hr)):
            if f is not None and wd > 0:
                k.op(k.dve, lambda e, f=f, col=col, wd=wd: e.tensor_scalar(
                    out=self.hT_t[:, :, col:col + wd], in0=self.hT_t[:, :, col:col + wd],
                    scalar1=self.flags[:, f:f + 1], scalar2=None, op0=ALU.mult),
                    reads=self.hT + [self.flags], writes=self.hT)

    def back_out(self, dst, wo_off, nmk, act_bufs, W, o0, o1, final, h=1, xres=None):
        k = self.k
        Wo = W - 2 * h
        for n in range(KC):
            po = self.p_o[n % 2]
            for m in range(nmk):
                o = wo_off + m * D + n * 128
                k.op(k.pe, lambda e, m=m, o=o, po=po: e.matmul(po[:, 0:Wo], self.warena[:, o:o + 128],
                                                              act_bufs[m][:, h:W - h],
                                                              start=(m == 0), stop=(m == nmk - 1)),
                     reads=[act_bufs[m], self.wbuf], writes=[po], signal=(m == nmk - 1))
            if not final:
                xo = self.xo[self.rr % 3]
                self.rr += 1
                if xres is None:
                    k.op(k.dve, lambda e, n=n, po=po, xo=xo: e.tensor_tensor(
                        out=xo[:, 0:Wo], in0=po[:, 0:Wo], in1=self.xt[:, n, h:W - h], op=ALU.add),
                        reads=[po, self.xt], writes=[xo])
                else:
                    xr = self.xr[n % 2]
                    k.dma(k.sp, xr[:, 0:Wo], self.aps[xres][n * 128:(n + 1) * 128, o0 + PAD:o1 + PAD], xr,
                          reads=self.regions(xres, o0 + PAD, o1 + PAD, n), writes=[xr])
                    k.op(k.dve, lambda e, n=n, po=po, xo=xo, xr=xr: e.tensor_tensor(
                        out=xo[:, 0:Wo], in0=po[:, 0:Wo], in1=xr[:, 0:Wo], op=ALU.add),
                        reads=[po, xr], writes=[xo])
                k.dma(k.sp, self.aps[dst][n * 128:(n + 1) * 128, o0 + PAD:o1 + PAD], xo[:, 0:Wo], xo,
                      reads=[xo], writes=self.regions(dst, o0 + PAD, o1 + PAD, n))
            else:
                k.op(k.dve, lambda e, n=n, po=po: e.tensor_tensor(
                    out=self.xt[:, n, h:W - h], in0=po[:, 0:Wo], in1=self.xt[:, n, h:W - h], op=ALU.add),
                    reads=[po, self.xt], writes=[self.xt])
        if final:
            xt = self.xt
            for kk in range(KC):
                sq = self.sq[kk % 2]
                k.op(k.act, lambda e, kk=kk, sq=sq: e.activation(out=sq[:, 0:Wo], in_=xt[:, kk, h:W - h], func=AF.Square),
                     reads=[xt], writes=[sq])
                k.op(k.pe, lambda e, kk=kk, sq=sq: e.matmul(self.p_ss[:, 0:Wo], self.ones[:], sq[:, 0:Wo],
                                                           start=(kk == 0), stop=(kk == KC - 1)),
                     reads=[sq, self.ones], writes=[self.p_ss], signal=(kk == KC - 1))
            rstd = self.rstd
            k.op(k.act, lambda e: e.activation(out=rstd[:, 0:Wo], in_=self.p_ss[:, 0:Wo], func=AF.Sqrt,
                                               bias=EPS, scale=1.0 / D), reads=[self.p_ss], writes=[rstd])
            k.op(k.dve, lambda e: e.reciprocal(out=rstd[:, 0:Wo], in_=rstd[:, 0:Wo]), reads=[rstd], writes=[rstd])
            for kk in range(KC):
                xo = self.xo[self.rr % 3]
                self.rr += 1
                k.op(k.dve, lambda e, kk=kk, xo=xo: e.scalar_tensor_tensor(
                    out=xo[:, 0:Wo], in0=xt[:, kk, h:W - h], scalar=self.normg[:, 8, kk:kk + 1],
                    in1=rstd[:, 0:Wo], op0=ALU.mult, op1=ALU.mult),
                    reads=[xt, rstd, self.normg], writes=[xo])
                k.dma(k.sp, self.yT[kk * 128:(kk + 1) * 128, o0:o1], xo[:, 0:Wo], xo,
                      reads=[xo], writes=self.regions("yT", o0, o1, kk), is_output=True)

    def pass_copy_final(self, src):
        k = self.k
        for c in range(0, self.NT, 512):
            srcap = self.aps[src][:, c + PAD:c + PAD + 512].rearrange("(k p) w -> p k w", p=128)
            k.dma(k.sp, self.xt[:, :, 0:512], srcap, self.xt, reads=self.regions(src, c + PAD, c + PAD + 512), writes=[self.xt])
            k.dma(k.sp, self.yT[:, c:c + 512].rearrange("(k p) w -> p k w", p=128), self.xt[:, :, 0:512], self.xt,
                  reads=[self.xt], writes=self.regions("yT", c, c + 512), is_output=True)

    def pass_ffn(self, layer, src, dst, final=False):
        k = self.k
        up_off = 0
        dn_off = KC * 2 * DFF
        self.load_weights_cast(None, self.din["ffn_w_up"][layer], KC, 2 * DFF, up_off)
        self.load_weights_cast(None, self.din["ffn_w_down"][layer], MC, D, dn_off)
        cv = self.cvec
        k.dma(k.sp, cv[:], self.din["ffn_cv"][layer], cv, writes=[cv])
        wins = self.windows(1)
        pre = not final
        if pre:
            self.front(src, wins[0][0], wins[0][1], layer * 2 + 1, wins[0][4], wins[0][5])
        for wi, (c0, W, o0, o1, fl, fr) in enumerate(wins):
            if not pre:
                self.front(src, c0, W, layer * 2 + 1, fl, fr)
            Wo = W - 2
            PA = [self.p_a[0], self.p_a[1], self.p_o[0], self.p_x]
            PU = [self.p_u[0], self.p_u[1], self.p_o[1], self.p_ss]
            for m in range(MC):
                pa = PA[m % 4]
                pu = PU[m % 4]
                for (pp, coff) in ((pa, m * 128), (pu, DFF + m * 128)):
                    for kk in range(KC):
                        o = up_off + kk * 2 * DFF + coff
                        k.op(k.pe, lambda e, kk=kk, o=o, pp=pp: e.matmul(
                            pp[:, 0:W], self.warena[:, o:o + 128], self.hT[kk][:, 0:W],
                            start=(kk == 0), stop=(kk == KC - 1)),
                            reads=[self.hT[kk], self.wbuf], writes=[pp], signal=(kk == KC - 1))
                g = self.gbuf[m % 3]
                k.op(k.act, lambda e, m=m, pa=pa, g=g: e.activation(
                    out=g[:, 0:Wo], in_=pa[:, 1:W - 1], func=AF.Identity,
                    bias=cv[:, 3, m:m + 1], scale=cv[:, 1, m:m + 1]), reads=[pa, cv], writes=[g])
                k.op(k.dve, lambda e, m=m, pa=pa, g=g: e.scalar_tensor_tensor(
                    out=g[:, 0:Wo], in0=pa[:, 0:W - 2], scalar=cv[:, 0, m:m + 1], in1=g[:, 0:Wo],
                    op0=ALU.mult, op1=ALU.add), reads=[pa, cv, g], writes=[g])
                k.op(k.dve, lambda e, m=m, pa=pa, g=g: e.scalar_tensor_tensor(
                    out=g[:, 0:Wo], in0=pa[:, 2:W], scalar=cv[:, 2, m:m + 1], in1=g[:, 0:Wo],
                    op0=ALU.mult, op1=ALU.add), reads=[pa, cv, g], writes=[g])
                k.op(k.act, lambda e, g=g: e.activation(out=g[:, 0:Wo], in_=g[:, 0:Wo], func=AF.Silu),
                     reads=[g], writes=[g])
                k.op(k.dve, lambda e, m=m, pu=pu, g=g: e.tensor_tensor(
                    out=self.hmid[m][:, 1:W - 1], in0=g[:, 0:Wo], in1=pu[:, 1:W - 1], op=ALU.mult),
                    reads=[g, pu], writes=[self.hmid[m]])
            if pre and wi + 1 < len(wins):
                nw = wins[wi + 1]
                self.front(src, nw[0], nw[1], layer * 2 + 1, nw[4], nw[5])
            self.back_out(dst, dn_off, MC, self.hmid, W, o0, o1, final, xres=(src if pre else None))

    def pass_attention(self, kind, layer, src, dst):
        k = self.k
        cfgs = ATT_CFG[kind]
        j0 = layer // 4
        NT, SEG = self.NT, self.SEG
        gidx = layer * 2
        for gi in range(len(cfgs)):
            if kind == "A":
                self.load_weights_cast(None, self.din["a_w_qkv"][j0][:, gi * 3 * D:(gi + 1) * 3 * D], KC, 3 * D, 0)
            else:
                wq = self.din["b_w_qkv"][j0]
                for kk in range(KC):
                    base = kk * 3 * D
                    rows = slice(kk * 128, (kk + 1) * 128)
                    k.dma(k.pool, self.warena[:, base:base + D], wq[rows, 0:D], self.wbuf, writes=[self.wbuf])
                    for part, scol in ((1, D), (2, D + 256)):
                        for cd in range(4):
                            dstv = self.warena[:, base + part * D:base + (part + 1) * D].rearrange(
                                "p (g c e) -> p g c e", g=4, c=4)[:, :, cd, :]
                            srcv = wq[rows, scol:scol + 256].rearrange("p (g e) -> p g e", g=4)
                            k.dma(k.pool, dstv, srcv, self.wbuf, writes=[self.wbuf])
            wins0 = self.windows(0)
            self.front_a(src, wins0[0][0], wins0[0][1])
            for wi0, (c0, W, o0, o1, fl, fr) in enumerate(wins0):
                self.front_b(W, gidx, None, None, 0, 0)
                if wi0 + 1 < len(wins0):
                    self.front_a(src, wins0[wi0 + 1][0], wins0[wi0 + 1][1])
                cb = c0 // 512
                for m in range(16):
                    pp = self.p_a[m % 2]
                    for kk in range(KC):
                        o = kk * 3 * D + m * 128
                        k.op(k.pe, lambda e, kk=kk, o=o, pp=pp: e.matmul(
                            pp[:, 0:W], self.warena[:, o:o + 128], self.hT[kk][:, 0:W],
                            start=(kk == 0), stop=(kk == KC - 1)), reads=[self.hT[kk], self.wbuf], writes=[pp])
                    q = self.qs[m % 4]
                    if m % 2 == 0:
                        k.op(k.act, lambda e, pp=pp, q=q: e.activation(out=q[:, 0:W], in_=pp[:, 0:W], func=AF.Copy),
                             reads=[pp], writes=[q])
                    else:
                        k.op(k.dve, lambda e, pp=pp, q=q: e.tensor_copy(out=q[:, 0:W], in_=pp[:, 0:W]),
                             reads=[pp], writes=[q])
                    dt_, rg = (self.QT_d, self.r_qt) if m < 8 else (self.KT_d, self.r_kt)
                    mm = m % 8
                    k.dma(k.sp, dt_.ap()[gi, mm * 128:(mm + 1) * 128, c0:c0 + W], q[:, 0:W], q,
                          reads=[q], writes=[rg[gi][mm][cb]])
                for tb in range(W // 128):
                    for hf in range(2):
                        pv = self.p_u[(tb * 2 + hf) % 2]
                        for kk in range(KC):
                            o = kk * 3 * D + 2 * D + hf * 512
                            k.op(k.pe, lambda e, kk=kk, o=o, pv=pv, tb=tb: e.matmul(
                                pv[:, 0:512], self.hT[kk][:, tb * 128:(tb + 1) * 128], self.warena[:, o:o + 512],
                                start=(kk == 0), stop=(kk == KC - 1)), reads=[self.hT[kk], self.wbuf], writes=[pv])
                        q = self.qs[(tb * 2 + hf) % 4]
                        if hf == 0:
                            k.op(k.act, lambda e, pv=pv, q=q: e.activation(out=q[:, 0:512], in_=pv[:, 0:512], func=AF.Copy),
                                 reads=[pv], writes=[q])
                        else:
                            k.op(k.dve, lambda e, pv=pv, q=q: e.tensor_copy(out=q[:, 0:512], in_=pv[:, 0:512]),
                                 reads=[pv], writes=[q])
                        t0 = c0 + tb * 128
                        k.dma(k.sp, self.V_d.ap()[gi, t0:t0 + 128, hf * 512:(hf + 1) * 512], q[:, 0:512], q,
                              reads=[q], writes=[self.r_v[gi][hf][t0 // 128]])
        k.barrier()
        self.attn_setup(kind, j0)
        for s_ in range(NSEG):
            for hp in range(KC):
                for gi, (r, half) in enumerate(cfgs):
                    self.attn_unit(kind, s_, hp, gi, r, half, first=(gi == 0))
                self.attn_finish(s_, hp)
        k.barrier()
        self.load_weights_cast(None, self.din["a_w_o" if kind == "A" else "b_w_o"][j0], KC, D, 0)
        for (c0, W, o0, o1, fl, fr) in self.windows(0):
            srcap = self.aps[src][:, c0 + PAD:c0 + PAD + W].rearrange("(k p) w -> p k w", p=128)
            k.dma(k.sp, self.xt[:, :, 0:W], srcap, self.xt, reads=self.regions(src, c0 + PAD, c0 + PAD + W), writes=[self.xt])
            for m in range(KC):
                k.dma(k.sp, self.hmid[m][:, 0:W], self.OT_d.ap()[m * 128:(m + 1) * 128, c0:c0 + W], self.hmid[m],
                      reads=[self.r_ot[m][c0 // SEG]], writes=[self.hmid[m]])
            self.back_out(dst, 0, KC, self.hmid, W, o0, o1, False, h=0)

    def attn_setup(self, kind, j0):
        k = self.k
        cfgs = ATT_CFG[kind]
        onesz = self.a_onesz
        k.op(k.dve, lambda e: e.memset(onesz[:], 0.0), writes=[onesz])
        k.op(k.dve, lambda e: e.memset(onesz[:, 0:64], 1.0), writes=[onesz])
        k.op(k.dve, lambda e: e.memset(onesz[:, 192:256], 1.0), writes=[onesz])
        k.dma(k.pool, self.a_ident[:], self.din["d_consts"][:, 4, :], self.a_ident, writes=[self.a_ident])
        k.op(k.dve, lambda e: e.memset(self.a_QT[:], 0.0), writes=[self.a_QT])
        k.op(k.dve, lambda e: e.memset(self.a_V[:], 0.0), writes=list(self.a_Vb))
        if kind == "B":
            k.dma(k.sp, self.es[:], self.din["sink_h"], self.es, writes=[self.es])
            k.op(k.act, lambda e: e.activation(out=self.es[:], in_=self.es[:], func=AF.Exp), reads=[self.es], writes=[self.es])
        else:
            k.op(k.dve, lambda e: e.memset(self.es[:], 0.0), writes=[self.es])
        oh, ttmp, tab = self.a_oh, self.a_ttmp, self.a_tab
        for gi, ci in enumerate(ATT_CI[kind]):
            r, half = ALL_CFG[ci]
            nparts = 2 * half // 128 + 1
            for j in range(nparts):
                idx = [i for i, kk in enumerate(OH_KEYS) if kk[0] == ci and kk[1] == j]
                bks = [OH_KEYS[i][2] for i in idx]
                nb = len(idx)
                k.dma(k.pool, oh[:, 0:nb, :], self.din["oh_mats"][idx[0]:idx[0] + nb].rearrange("b p c -> p b c"),
                      oh, writes=[oh])
                for h in range(16):
                    hp, hh = h // 2, h % 2
                    to = ((gi * 8 + hp) * nparts + j) * 256 + hh * 128
                    tslot = tab[:, to:to + 128]
                    for ii, b in enumerate(bks):
                        sc = self.erb[:, b * 16 + h:b * 16 + h + 1] if b < 32 else -30000.0
                        last = (ii == nb - 1)
                        outap = tslot if last else ttmp[:]
                        if ii == 0:
                            k.op(k.dve, lambda e, ii=ii, sc=sc, outap=outap: e.tensor_scalar(
                                out=outap, in0=oh[:, ii, :], scalar1=sc, scalar2=None, op0=ALU.mult),
                                reads=[oh, self.erb], writes=[tab if last else ttmp])
                        else:
                            k.op(k.dve, lambda e, ii=ii, sc=sc, outap=outap: e.scalar_tensor_tensor(
                                out=outap, in0=oh[:, ii, :], scalar=sc, in1=ttmp[:], op0=ALU.mult, op1=ALU.add),
                                reads=[oh, self.erb, ttmp], writes=[tab if last else ttmp])

    def attn_unit(self, kind, s_, hp, gi, r, half, first):
        k = self.k
        NT, SEG = self.NT, self.SEG
        H = half * r
        n = SEG // r
        nqb = n // 128
        nparts = 2 * half // 128 + 1
        nbk = nqb + nparts - 1
        KT, QT, V = self.a_KT, self.a_QT, self.a_V
        rows = slice(hp * 128, (hp + 1) * 128)
        base = s_ * SEG
        ktd, qtd = self.KT_d.ap()[gi], self.QT_d.ap()[gi]

        def kreg(a, b):
            return self.r_kt[gi][hp][a // 512:(b - 1) // 512 + 1]
        vt = self.V_d
        hf = hp // 4
        for rho in range(r):
            for m in range(nbk):
                tstart = base + rho + r * (128 * m - half)
                p0 = 0 if tstart >= 0 else (-tstart + r - 1) // r
                p1 = min(128, (NT - 1 - tstart) // r + 1)
                if p1 <= p0:
                    continue
                off = (gi * NT + tstart + r * p0) * D + hp * 128
                srcv = bass.AP(vt, off, [[r * D, p1 - p0], [64, 2], [1, 64]])
                dstv = V[p0:p1, rho * nbk + m, :].rearrange("p (a e) -> p a e", a=4)[:, 0:4:3, :]
                ta, tl = tstart + r * p0, tstart + r * (p1 - 1)
                vg = self.a_Vb[(rho * nbk + m) // 4]
                k.dma(k.sp, dstv, srcv, vg, reads=self.r_v[gi][hf][ta // 128:tl // 128 + 1], writes=[vg])
        la = max(0, base - H)
        k.dma(k.sp, KT[:, 0:H], ktd[rows, la:la + H], KT, reads=kreg(la, la + H), writes=[KT])
        k.dma(k.sp, KT[:, H:H + SEG], ktd[rows, base:base + SEG], KT, reads=kreg(base, base + SEG), writes=[KT])
        ra = min(NT - H, base + SEG)
        k.dma(k.sp, KT[:, H + SEG:H + SEG + H], ktd[rows, ra:ra + H], KT, reads=kreg(ra, ra + H), writes=[KT])
        qreg = self.r_qt[gi][hp][base // 512:(base + SEG - 1) // 512 + 1]
        k.dma(k.sp, QT[0:64, 0, 0:SEG], qtd[hp * 128:hp * 128 + 64, base:base + SEG], QT, reads=qreg, writes=[QT])
        k.dma(k.sp, QT[64:128, 1, 0:SEG], qtd[hp * 128 + 64:hp * 128 + 128, base:base + SEG], QT, reads=qreg, writes=[QT])
        accO, accD = self.a_accO, self.a_accD
        parts = []
        for rho in range(r):
            for qb in range(nqb):
                q0 = rho + r * 128 * qb
                qsl = slice(q0, q0 + 127 * r + 1, r)
                bi = rho * nqb + qb
                for j in range(nparts):
                    m = qb + j
                    lo = 128 * m - half
                    fcol = None
                    if lo < 0:
                        fcol = s_ * 4 + (0 if lo == -64 else 1)
                    elif lo + 128 > n:
                        fcol = s_ * 4 + (2 if lo + 128 - n == 64 else 3)
                    parts.append(dict(qsl=qsl, j=j, kc0=H + rho + r * lo, fcol=fcol, vb=rho * nbk + m,
                                      pO=self.p_o[bi % 2], pD=self.p_u[bi % 2]))
        SB = [self.p_a[0], self.p_a[1], self.p_ss, self.p_x]
        NS, LAG = 4, 3

        def stageA(i):
            p = parts[i]
            pS, pexp, PT = SB[i % NS], self.a_pexp[i % NS], self.a_PT[i % NS]
            kc0, qsl, j, fcol = p["kc0"], p["qsl"], p["j"], p["fcol"]
            to = ((gi * 8 + hp) * nparts + j) * 256
            tabv = self.a_tab[:, to:to + 256]
            k.op(k.pe, lambda e: e.matmul(pS[:, 0:256], KT[:, kc0:kc0 + 127 * r + 1:r], QT[:, :, qsl], start=True, stop=False),
                 reads=[KT, QT], writes=[pS])
            k.op(k.pe, lambda e: e.matmul(pS[:, 0:256], self.a_ident[:], tabv, start=False, stop=True),
                 reads=[self.a_ident, self.a_tab], writes=[pS])
            if fcol is None:
                k.op(k.act, lambda e: e.activation(out=PT[:], in_=pS[:, 0:256], func=AF.Exp, scale=0.125),
                     reads=[pS], writes=[PT])
            else:
                k.op(k.act, lambda e: e.activation(out=pexp[:], in_=pS[:, 0:256], func=AF.Exp, scale=0.125),
                     reads=[pS], writes=[pexp])
                k.op(k.dve, lambda e: e.tensor_scalar(out=PT[:], in0=pexp[:], scalar1=self.flags2[:, fcol:fcol + 1],
                                                      scalar2=None, op0=ALU.mult), reads=[pexp, self.flags2], writes=[PT])

        def stageB(i):
            p = parts[i]
            PT = self.a_PT[i % NS]
            j, vb, pO, pD, qsl = p["j"], p["vb"], p["pO"], p["pD"], p["qsl"]
            for hh in range(2):
                k.op(k.pe, lambda e, hh=hh: e.matmul(
                    pO[:, 0:128], V[:, vb, hh * 128:(hh + 1) * 128], PT[:, hh * 128:(hh + 1) * 128],
                    start=(j == 0 and hh == 0), stop=(j == nparts - 1 and hh == 1)), reads=[self.a_Vb[vb // 4], PT], writes=[pO])
            for hh in range(2):
                k.op(k.pe, lambda e, hh=hh: e.matmul(
                    pD[:, 0:128], self.a_onesz[:, hh * 128:(hh + 1) * 128], PT[:, hh * 128:(hh + 1) * 128],
                    start=(j == 0 and hh == 0), stop=(j == nparts - 1 and hh == 1)), reads=[self.a_onesz, PT], writes=[pD])
            if j == nparts - 1:
                for (acc, pp) in ((accO, pO), (accD, pD)):
                    if first:
                        k.op(k.dve, lambda e, acc=acc, pp=pp: e.tensor_copy(out=acc[:, qsl], in_=pp[:, 0:128]),
                             reads=[pp], writes=[acc])
                    else:
                        k.op(k.dve, lambda e, acc=acc, pp=pp: e.tensor_tensor(
                            out=acc[:, qsl], in0=acc[:, qsl], in1=pp[:, 0:128], op=ALU.add),
                            reads=[pp, acc], writes=[acc])

        npt = len(parts)
        for i in range(npt + LAG):
            if i < npt:
                stageA(i)
            if i >= LAG:
                stageB(i - LAG)

    def attn_finish(self, s_, hp):
        k = self.k
        SEG = self.SEG
        accO, accD, OTn = self.a_accO, self.a_accD, self.a_OTn
        k.op(k.dve, lambda e: e.tensor_scalar(out=accD[:, 0:SEG], in0=accD[:, 0:SEG], scalar1=self.es[:, hp:hp + 1],
                                              scalar2=None, op0=ALU.add), reads=[accD, self.es], writes=[accD])
        k.op(k.dve, lambda e: e.reciprocal(out=accD[:, 0:SEG], in_=accD[:, 0:SEG]), reads=[accD], writes=[accD])
        k.op(k.dve, lambda e: e.tensor_tensor(out=OTn[:, 0:SEG], in0=accO[:, 0:SEG], in1=accD[:, 0:SEG], op=ALU.mult),
             reads=[accO, accD], writes=[OTn])
        k.dma(k.sp, self.OT_d.ap()[hp * 128:(hp + 1) * 128, s_ * SEG:(s_ + 1) * SEG], OTn[:, 0:SEG], OTn,
              reads=[OTn], writes=[self.r_ot[hp][s_]])


    def pass_ssd(self, layer, src, dst):
        k = self.k
        j0 = layer // 4
        NT, SEG = self.NT, self.SEG
        A = self.warena
        WIN = 5184
        self.load_weights_cast(None, self.din["d_w_in"][j0], KC, WIN, 0)
        o = KC * WIN
        def carve(n):
            nonlocal o
            v = A[:, o:o + n]
            o += n
            return v
        dcv = Buf(carve(2 * 5 * 24).bitcast(F32).rearrange("p (a b) -> p a b", a=5), "dcv")
        ident = Buf(carve(128), "ident")
        dtb = Buf(carve(128).bitcast(F32), "dtb")
        aneg = Buf(carve(128).bitcast(F32), "aneg")
        dta_s = [Buf(carve(256).bitcast(F32), f"dtas{i}") for i in range(2)]
        tst = [Buf(carve(512), f"tst{i}") for i in range(3)]
        assert o <= self.WCOLS
        k.dma(k.sp, dcv[:], self.din["d_cv"], dcv, writes=[dcv])
        k.dma(k.pool, ident[:], self.din["d_consts"][:, 4, :], ident, writes=[ident])
        k.dma(k.sp, dtb[:], self.din["d_dtb_rep"], dtb, writes=[dtb])
        k.dma(k.sp, aneg[:], self.din["d_alog_rep"], aneg, writes=[aneg])
        k.op(k.act, lambda e: e.activation(out=aneg[:], in_=aneg[:], func=AF.Exp), reads=[aneg], writes=[aneg])
        k.op(k.dve, lambda e: e.tensor_scalar(out=aneg[:], in0=aneg[:], scalar1=-1.0, scalar2=None, op0=ALU.mult),
             reads=[aneg], writes=[aneg])
        xs = [self.hmid[m] if m < 20 else self.hmid[20 + m % 2] for m in range(24)]
        pxb = self.p_x[:].bitcast(BF16)
        winsd = self.windows(2, 1)
        self.front_a(src, winsd[0][0], winsd[0][1])
        for wid, (c0, W, o0, o1, fl, fr) in enumerate(winsd):
            self.front_b(W, layer * 2, fl, fr, 2, 1)
            if wid + 1 < len(winsd):
                self.front_a(src, winsd[wid + 1][0], winsd[wid + 1][1])
            Wo = W - 3
            for m in range(24):
                pa = self.p_a[m % 2]
                for kk in range(KC):
                    oo = kk * WIN + DIN + m * 128
                    k.op(k.pe, lambda e, kk=kk, oo=oo, pa=pa: e.matmul(
                        pa[:, 0:W], A[:, oo:oo + 128], self.hT[kk][:, 0:W], start=(kk == 0), stop=(kk == KC - 1)),
                        reads=[self.hT[kk], self.wbuf], writes=[pa])
                g = self.gbuf[m % 3]
                k.op(k.act, lambda e, m=m, pa=pa, g=g: e.activation(
                    out=g[:, 0:Wo], in_=pa[:, 2:W - 1], func=AF.Identity, bias=dcv[:, 4, m:m + 1], scale=dcv[:, 2, m:m + 1]),
                    reads=[pa, dcv], writes=[g])
                for (tap, a0) in ((0, 0), (1, 1), (3, 3)):
                    k.op(k.dve, lambda e, m=m, pa=pa, g=g, tap=tap, a0=a0: e.scalar_tensor_tensor(
                        out=g[:, 0:Wo], in0=pa[:, a0:a0 + Wo], scalar=dcv[:, tap, m:m + 1], in1=g[:, 0:Wo],
                        op0=ALU.mult, op1=ALU.add), reads=[pa, dcv, g], writes=[g])
                k.op(k.act, lambda e, m=m, g=g: e.activation(out=xs[m][:, 0:Wo], in_=g[:, 0:Wo], func=AF.Silu),
                     reads=[g], writes=[xs[m]])
                if m >= 16:
                    dt_ = self.BT_d if m < 20 else self.CT_d
                    mm = (m - 16) % 4
                    k.dma(k.sp, dt_.ap()[mm * 128:(mm + 1) * 128, o0:o1], xs[m][:, 0:Wo], xs[m], reads=[xs[m]])
            for ib, a in enumerate(range(o0, o1, 128)):
                b = min(o1, a + 128)
                nb = b - a
                la, lb = a - o0, b - o0
                ha, hb = a - c0, b - c0
                for q4 in range(5):
                    for i in range(4):
                        m = q4 * 4 + i
                        k.op(k.pe, lambda e, m=m, i=i, la=la, lb=lb, nb=nb: e.transpose(
                            pxb[0:nb, i * 128:(i + 1) * 128], xs[m][:, la:lb], ident[:]),
                            reads=[xs[m], ident], writes=[self.p_x])
                    st = tst[q4 % 3]
                    k.op(k.act if q4 % 2 == 0 else k.dve,
                         (lambda e, st=st, nb=nb: e.activation(out=st[0:nb, :], in_=pxb[0:nb, 0:512], func=AF.Copy)) if q4 % 2 == 0
                         else (lambda e, st=st, nb=nb: e.tensor_copy(out=st[0:nb, :], in_=pxb[0:nb, 0:512])),
                         reads=[self.p_x], writes=[st])
                    k.dma(k.sp, self.X_d.ap()[a:b, q4 * 512:(q4 + 1) * 512], st[0:nb, :], st, reads=[st])
                for q4 in range(4):
                    pz = self.p_u[q4 % 2]
                    for kk in range(KC):
                        oo = kk * WIN + q4 * 512
                        k.op(k.pe, lambda e, kk=kk, oo=oo, pz=pz, ha=ha, hb=hb, nb=nb: e.matmul(
                            pz[0:nb, 0:512], self.hT[kk][:, ha:hb], A[:, oo:oo + 512], start=(kk == 0), stop=(kk == KC - 1)),
                            reads=[self.hT[kk], self.wbuf], writes=[pz])
                    st = tst[(q4 + 2) % 3]
                    k.op(k.act, lambda e, st=st, pz=pz, nb=nb: e.activation(out=st[0:nb, :], in_=pz[0:nb, 0:512], func=AF.Silu),
                         reads=[pz], writes=[st])
                    k.dma(k.sp, self.Z_d.ap()[a:b, q4 * 512:(q4 + 1) * 512], st[0:nb, :], st, reads=[st])
                pd = self.p_o[ib % 2]
                for kk in range(KC):
                    oo = kk * WIN + DIN + 3072
                    k.op(k.pe, lambda e, kk=kk, oo=oo, pd=pd, ha=ha, hb=hb, nb=nb: e.matmul(
                        pd[0:nb, 0:64], self.hT[kk][:, ha:hb], A[:, oo:oo + 64], start=(kk == 0), stop=(kk == KC - 1)),
                        reads=[self.hT[kk], self.wbuf], writes=[pd])
                ds = dta_s[ib % 2]
                k.op(k.dve, lambda e, ds=ds, pd=pd, nb=nb: e.tensor_tensor(out=ds[0:nb, 0:64], in0=pd[0:nb, 0:64], in1=dtb[0:nb, :], op=ALU.add),
                     reads=[pd, dtb], writes=[ds])
                k.op(k.act, lambda e, ds=ds, nb=nb: e.activation(out=ds[0:nb, 0:64], in_=ds[0:nb, 0:64], func=AF.Exp), reads=[ds], writes=[ds])
                k.op(k.act, lambda e, ds=ds, nb=nb: e.activation(out=ds[0:nb, 0:64], in_=ds[0:nb, 0:64], func=AF.Ln, bias=1.0), reads=[ds], writes=[ds])
                k.op(k.dve, lambda e, ds=ds, nb=nb: e.tensor_tensor(out=ds[0:nb, 64:128], in0=ds[0:nb, 0:64], in1=aneg[0:nb, :], op=ALU.mult),
                     reads=[ds, aneg], writes=[ds])
                k.dma(k.sp, self.DTA_d.ap()[a:b, :], ds[0:nb, :], ds, reads=[ds])
        k.barrier()
        o = 0
        cst = Buf(carve(4 * 256).bitcast(F32).rearrange("p (a b) -> p a b", a=4), "cst")
        ident = Buf(carve(128), "ident2")
        ones32 = Buf(carve(256).bitcast(F32), "ones32")
        skp = Buf(carve(64).bitcast(F32), "skp")
        ngr = Buf(carve(2 * DIN).bitcast(F32), "ngr")
        NB2 = 2
        xtokS = [Buf(carve(2560), f"xtok{i}") for i in range(NB2)]
        BTtS = [Buf(carve(512).rearrange("p (g t) -> p g t", g=4), f"BTt{i}") for i in range(NB2)]
        CTtS = [Buf(carve(512).rearrange("p (g t) -> p g t", g=4), f"CTt{i}") for i in range(NB2)]
        dtaS = [Buf(carve(256).bitcast(F32), f"dta{i}") for i in range(NB2)]
        ztS = [Buf(carve(DIN), f"zt{i}") for i in range(NB2)]
        yaccS = [Buf(carve(2 * DIN).bitcast(F32), f"yacc{i}") for i in range(NB2)]
        hst = [Buf(carve(2 * 512).bitcast(F32), f"hst{g}") for g in range(4)]
        hbf = [Buf(carve(512), f"hbf{g}") for g in range(4)]
        xdt = Buf(carve(DIN), "xdt")
        xsk = Buf(carve(DIN), "xsk")
        xdd = [Buf(carve(512), f"xdd{i}") for i in range(2)]
        NR = 3
        Ue = [Buf(carve(1024).bitcast(F32), f"Ue{i}") for i in range(NR)]
        Dm = [Buf(carve(1024).bitcast(F32), f"Dm{i}") for i in range(NR)]
        MT = [Buf(carve(512), f"MT{i}") for i in range(NR)]
        CBs = [Buf(carve(256).bitcast(F32), f"CB{g}") for g in range(4)]
        acsS = [Buf(carve(64).bitcast(F32), f"acs{i}") for i in range(2)]
        eaS = [Buf(carve(64).bitcast(F32), f"ea{i}") for i in range(2)]
        aclS = [Buf(carve(64).bitcast(F32), f"acl{i}") for i in range(2)]
        cdr = Buf(carve(64).bitcast(F32), "cdr")
        dec = Buf(carve(64).bitcast(F32), "dec")
        tmpg = [Buf(carve(1024).bitcast(F32), f"tmpg{i}") for i in range(2)]
        junk = Buf(carve(DIN), "junk")
        ssq = Buf(carve(4).bitcast(F32), "ssq")
        yn = Buf(carve(DIN), "yn")
        yst = [Buf(carve(512).rearrange("p (a t) -> p a t", a=4), f"yst{i}") for i in range(2)]
        assert o <= self.WCOLS, o
        k.dma(k.sp, cst[:], self.din["d_consts"][:, 0:4, :], cst, writes=[cst])
        k.dma(k.pool, ident[:], self.din["d_consts"][:, 4, :], ident, writes=[ident])
        k.op(k.dve, lambda e: e.memset(ones32[:], 1.0), writes=[ones32])
        k.dma(k.sp, skp[:], self.din["d_skip_rep"], skp, writes=[skp])
        k.dma(k.sp, ngr[:], self.din["d_normg_rep"], ngr, writes=[ngr])
        NCH = SEG // 128
        pyb = self.p_x[:].bitcast(BF16)

        def bc(ap2, n):
            return ap2.unsqueeze(2).broadcast_to([128, ap2.shape[1], n])

        def v3(ap2):
            return ap2.rearrange("p (e q) -> p e q", q=64)

        def q4v(ap2):
            return ap2.rearrange("p (a l) -> p a l", a=4)

        for d in range(2):
            order = list(range(NSEG * NCH))
            if d == 1:
                order = order[::-1]

            def emit_loads(ci):
                c = order[ci]
                t0 = c * 128
                sl = ci % NB2
                k.dma(k.sp, xtokS[sl][:], self.X_d.ap()[t0:t0 + 128, :], xtokS[sl], writes=[xtokS[sl]])
                k.dma(k.sp, BTtS[sl][:], self.BT_d.ap()[:, t0:t0 + 128].rearrange("(g n) t -> n g t", g=4), BTtS[sl], writes=[BTtS[sl]])
                k.dma(k.sp, CTtS[sl][:], self.CT_d.ap()[:, t0:t0 + 128].rearrange("(g n) t -> n g t", g=4), CTtS[sl], writes=[CTtS[sl]])
                k.dma(k.sp, dtaS[sl][:], self.DTA_d.ap()[t0:t0 + 128, :], dtaS[sl], writes=[dtaS[sl]])
                if d == 1:
                    k.dma(k.sp, yaccS[sl][:], self.Y_d.ap()[t0:t0 + 128, :], yaccS[sl], writes=[yaccS[sl]])
                    k.dma(k.sp, ztS[sl][:], self.Z_d.ap()[t0:t0 + 128, :], ztS[sl], writes=[ztS[sl]])

            emit_loads(0)
            for ci, c in enumerate(order):
                t0 = c * 128
                sg, cc = c // NCH, c % NCH
                sl = ci % NB2
                xtok, BTt, CTt, dta, zt, yacc = xtokS[sl], BTtS[sl], CTtS[sl], dtaS[sl], ztS[sl], yaccS[sl]
                acs, ea, acl = acsS[ci % 2], eaS[ci % 2], aclS[ci % 2]
                if ci + 1 < len(order):
                    emit_loads(ci + 1)
                start_seq = (cc == 0) if d == 0 else (cc == NCH - 1)
                if start_seq:
                    joined = (sg == 1) if d == 0 else (sg == 0)
                    for g in range(4):
                        if joined:
                            k.op(k.dve, lambda e, g=g: e.tensor_scalar(out=hst[g][:], in0=hst[g][:], scalar1=self.flags[:, 1:2],
                                                                      scalar2=None, op0=ALU.mult), reads=[hst[g], self.flags], writes=[hst[g]])
                        else:
                            k.op(k.dve, lambda e, g=g: e.memset(hst[g][:], 0.0), writes=[hst[g]])
                        k.op(k.act, lambda e, g=g: e.activation(out=hbf[g][:], in_=hst[g][:], func=AF.Copy), reads=[hst[g]], writes=[hbf[g]])
                adt = dta[:, 64 + d * 32:64 + (d + 1) * 32]
                dtv = dta[:, d * 32:(d + 1) * 32]
                tri = cst[:, d, :]
                msk = cst[:, 2 + d, :]
                last = 127 if d == 0 else 0
                pcs = self.p_ss
                k.op(k.pe, lambda e, tri=tri, adt=adt: e.matmul(pcs[:, 0:32], tri, adt, start=True, stop=True),
                     reads=[cst, dta], writes=[pcs])
                k.op(k.act, lambda e, acs=acs: e.activation(out=acs[:], in_=pcs[:, 0:32], func=AF.Copy), reads=[pcs], writes=[acs])
                k.op(k.act, lambda e, ea=ea: e.activation(out=ea[:], in_=pcs[:, 0:32], func=AF.Exp), reads=[pcs], writes=[ea])
                for g in range(4):
                    k.op(k.pe, lambda e, g=g, BTt=BTt, CTt=CTt: e.matmul(self.p_o[g % 2][:, 0:128], BTt[:, g, :], CTt[:, g, :], start=True, stop=True),
                         reads=[BTt, CTt], writes=[self.p_o[g % 2]])
                    k.op(k.act, lambda e, g=g: e.activation(out=CBs[g][:], in_=self.p_o[g % 2][:, 0:128], func=AF.Copy),
                         reads=[self.p_o[g % 2]], writes=[CBs[g]])

                def stA(qd):
                    e0 = qd * 4
                    ue, pac = Ue[qd % NR], self.p_a[qd % 2]
                    k.op(k.dve, lambda e: e.tensor_tensor(
                        out=q4v(ue[:]), in0=tri.unsqueeze(1).broadcast_to([128, 4, 128]),
                        in1=bc(adt[:, e0:e0 + 4], 128), op=ALU.mult), reads=[cst, dta], writes=[ue])
                    k.op(k.pe, lambda e: e.matmul(pac[:, 0:512], ones32[:], ue[:], start=True, stop=True),
                         reads=[ones32, ue], writes=[pac])

                def stB1(qd):
                    e0 = qd * 4
                    dm, pac = Dm[qd % NR], self.p_a[qd % 2]
                    for i in range(4):
                        k.op(k.dve, lambda e, i=i: e.scalar_tensor_tensor(
                            out=dm[:, i * 128:(i + 1) * 128], in0=pac[:, i * 128:(i + 1) * 128], scalar=acs[:, e0 + i:e0 + i + 1],
                            in1=msk, op0=ALU.subtract, op1=ALU.min), reads=[pac, acs, cst], writes=[dm])
                    pl = q4v(pac[:, 0:512])[:, :, last:last + 1]
                    k.op(k.act, lambda e: e.activation(out=dm[:], in_=dm[:], func=AF.Exp), reads=[dm], writes=[dm])
                    k.op(k.act, lambda e: e.activation(out=acl[:, e0:e0 + 4].unsqueeze(2), in_=pl, func=AF.Copy),
                         reads=[pac], writes=[acl])

                def stB2(qd):
                    g = qd // 2
                    e0 = qd * 4
                    dm, mt = Dm[qd % NR], MT[qd % NR]
                    pdg = self.p_u[g % 2]
                    k.op(k.dve, lambda e: e.tensor_tensor(
                        out=q4v(mt[:]), in0=q4v(dm[:]), in1=CBs[g][:].unsqueeze(1).broadcast_to([128, 4, 128]), op=ALU.mult),
                        reads=[dm, CBs[g]], writes=[mt])
                    for i in range(4):
                        eh = e0 + i
                        k.op(k.pe, lambda e, i=i, eh=eh: e.matmul(
                            pdg[:, (eh % 8) * 64:(eh % 8) * 64 + 64], mt[:, i * 128:(i + 1) * 128], xdt[:, eh * 64:(eh + 1) * 64],
                            start=True, stop=True), reads=[mt, xdt], writes=[pdg])

                def stC(g):
                    pdg = self.p_u[g % 2]
                    pof = self.p_o[g % 2]
                    k.op(k.pe, lambda e: e.matmul(pof[:, 0:512], CTt[:, g, :], hbf[g][:], start=True, stop=True),
                         reads=[CTt, hbf[g]], writes=[pof])
                    tg = tmpg[g % 2]
                    ysl = yacc[:, g * 512:(g + 1) * 512]
                    k.op(k.dve, lambda e: e.tensor_tensor(out=v3(tg[:]), in0=v3(pof[:, 0:512]),
                                                          in1=bc(ea[:, g * 8:(g + 1) * 8], 64), op=ALU.mult), reads=[pof, ea], writes=[tg])
                    if d == 0:
                        k.op(k.dve, lambda e: e.tensor_tensor(out=ysl, in0=tg[:], in1=pdg[:, 0:512], op=ALU.add),
                             reads=[tg, pdg], writes=[yacc])
                    else:
                        k.op(k.dve, lambda e: e.tensor_tensor(out=ysl, in0=ysl, in1=tg[:], op=ALU.add),
                             reads=[tg, yacc], writes=[yacc])
                        k.op(k.dve, lambda e: e.tensor_tensor(out=ysl, in0=ysl, in1=pdg[:, 0:512], op=ALU.add),
                             reads=[pdg, yacc], writes=[yacc])
                    g8 = slice(g * 8, (g + 1) * 8)
                    k.op(k.dve, lambda e: e.tensor_tensor(out=dec[:, g8], in0=acl[:, g8], in1=acs[:, g8], op=ALU.subtract),
                         reads=[acl, acs], writes=[dec])
                    k.op(k.act, lambda e: e.activation(out=dec[:, g8], in_=dec[:, g8], func=AF.Exp), reads=[dec], writes=[dec])
                    k.op(k.act, lambda e: e.activation(out=cdr[:, g8], in_=acl[:, g8], func=AF.Exp), reads=[acl], writes=[cdr])
                    xd = xdd[g % 2]
                    k.op(k.dve, lambda e: e.tensor_tensor(out=v3(xd[:]), in0=v3(xdt[:, g * 512:(g + 1) * 512]),
                                                          in1=bc(dec[:, g8], 64), op=ALU.mult), reads=[xdt, dec], writes=[xd])
                    pst = self.p_x
                    k.op(k.pe, lambda e: e.matmul(pst[:, 0:512], xtok[:, DIN + g * 128:DIN + (g + 1) * 128], xd[:],
                                                  start=True, stop=True), reads=[xtok, xd], writes=[pst])
                    k.op(k.dve, lambda e: e.tensor_tensor(out=v3(hst[g][:]), in0=v3(hst[g][:]), in1=bc(cdr[:, g8], 64), op=ALU.mult),
                         reads=[hst[g], cdr, hbf[g]], writes=[hst[g]])
                    k.op(k.dve, lambda e: e.tensor_tensor(out=hst[g][:], in0=hst[g][:], in1=pst[:, 0:512], op=ALU.add),
                         reads=[hst[g], pst], writes=[hst[g]])
                    k.op(k.act, lambda e: e.activation(out=hbf[g][:], in_=hst[g][:], func=AF.Copy), reads=[hst[g]], writes=[hbf[g]])

                stA(0)
                stA(1)
                k.op(k.dve, lambda e, dtv=dtv, xtok=xtok: e.tensor_tensor(out=v3(xdt[:]), in0=v3(xtok[:, 0:DIN]), in1=bc(dtv, 64), op=ALU.mult),
                     reads=[xtok, dta], writes=[xdt])
                for it in range(2, 8 + 3):
                    if it - 2 < 8:
                        stB1(it - 2)
                    if it < 8:
                        stA(it)
                    if 0 <= it - 3 < 8:
                        stB2(it - 3)
                        if (it - 3) % 2 == 1:
                            stC((it - 3) // 2)
                if d == 0:
                    k.dma(k.sp, self.Y_d.ap()[t0:t0 + 128, :], yacc[:], yacc, reads=[yacc])
                else:
                    k.op(k.dve, lambda e, xtok=xtok: e.tensor_tensor(out=v3(xsk[:]), in0=v3(xtok[:, 0:DIN]), in1=bc(skp[:], 64), op=ALU.mult),
                         reads=[xtok, skp], writes=[xsk])
                    k.op(k.dve, lambda e, yacc=yacc: e.tensor_tensor(out=yacc[:], in0=yacc[:], in1=xsk[:], op=ALU.add), reads=[yacc, xsk], writes=[yacc])
                    k.op(k.dve, lambda e, yacc=yacc, zt=zt: e.tensor_tensor(out=yacc[:], in0=yacc[:], in1=zt[:], op=ALU.mult), reads=[yacc, zt], writes=[yacc])
                    k.op(k.act, lambda e, yacc=yacc: e.activation(out=junk[:], in_=yacc[:], func=AF.Square, accum_out=ssq[:, 0:1]),
                         reads=[yacc], writes=[junk, ssq])
                    k.op(k.act, lambda e: e.activation(out=ssq[:, 0:1], in_=ssq[:, 0:1], func=AF.Sqrt, bias=EPS, scale=1.0 / DIN),
                         reads=[ssq], writes=[ssq])
                    k.op(k.dve, lambda e: e.reciprocal(out=ssq[:, 0:1], in_=ssq[:, 0:1]), reads=[ssq], writes=[ssq])
                    k.op(k.dve, lambda e, yacc=yacc: e.scalar_tensor_tensor(out=yn[:], in0=yacc[:], scalar=ssq[:, 0:1], in1=ngr[:],
                                                                            op0=ALU.mult, op1=ALU.mult), reads=[yacc, ssq, ngr], writes=[yn])
                    for q4 in range(4):
                        for i in range(4):
                            kk = q4 * 4 + i
                            k.op(k.pe, lambda e, kk=kk, i=i: e.transpose(pyb[:, i * 128:(i + 1) * 128], yn[:, kk * 128:(kk + 1) * 128], ident[:]),
                                 reads=[yn, ident], writes=[self.p_x])
                        st = yst[q4 % 2]
                        k.op(k.act, lambda e, st=st: e.activation(out=st[:].rearrange("p a t -> p (a t)"), in_=pyb[:, 0:512], func=AF.Copy),
                             reads=[self.p_x], writes=[st])
                        k.dma(k.sp, self.YT_d.ap()[q4 * 512:(q4 + 1) * 512, t0:t0 + 128].rearrange("(a p) t -> p a t", p=128),
                              st[:], st, reads=[st])
            k.barrier()
        self.load_weights_cast(None, self.din["d_w_out"][j0], 16, D, 0)
        for (c0, W, o0, o1, fl, fr) in self.windows(0):
            srcap = self.aps[src][:, c0 + PAD:c0 + PAD + W].rearrange("(k p) w -> p k w", p=128)
            k.dma(k.sp, self.xt[:, :, 0:W], srcap, self.xt, reads=self.regions(src, c0 + PAD, c0 + PAD + W), writes=[self.xt])
            for m in range(16):
                k.dma(k.sp, self.hmid[m][:, 0:W], self.YT_d.ap()[m * 128:(m + 1) * 128, c0:c0 + W], self.hmid[m],
                      writes=[self.hmid[m]])
            self.back_out(dst, 0, 16, self.hmid, W, o0, o1, False, h=0)

    def pass_conv(self, layer, src, dst):
        k = self.k
        j = layer // 4
        in_off = 0
        out_off = KC * 3 * D
        self.load_weights_cast(None, self.din["c_w_in"][j], KC, 3 * D, in_off)
        self.load_weights_cast(None, self.din["c_w_out"][j], KC, D, out_off)
        cv = self.cvec
        k.dma(k.sp, cv[:, 0:3, 0:KC], self.din["c_cv"], cv, writes=[cv])
        for (c0, W, o0, o1, fl, fr) in self.windows(1):
            self.front(src, c0, W, layer * 2 + 0, fl, fr)
            Wo = W - 2
            for m in range(KC):
                pb, pc, px = (self.p_x, self.p_ss)[m % 2], self.p_a[m % 2], self.p_u[m % 2]
                for (pp, coff) in ((pc, D + m * 128), (px, 2 * D + m * 128), (pb, m * 128)):
                    for kk in range(KC):
                        o = in_off + kk * 3 * D + coff
                        k.op(k.pe, lambda e, kk=kk, o=o, pp=pp: e.matmul(
                            pp[:, 0:W], self.warena[:, o:o + 128], self.hT[kk][:, 0:W],
                            start=(kk == 0), stop=(kk == KC - 1)),
                            reads=[self.hT[kk], self.wbuf], writes=[pp], signal=(kk == KC - 1))
                g = self.gbuf[m % 3]
                g2 = self.xo[m % 3]
                k.op(k.act, lambda e, pc=pc, g=g: e.activation(out=g[:, 0:W], in_=pc[:, 0:W], func=AF.Copy),
                     reads=[pc], writes=[g])
                k.op(k.dve, lambda e, px=px, g=g: e.tensor_tensor(out=g[:, 0:W], in0=g[:, 0:W], in1=px[:, 0:W],
                                                                  op=ALU.mult), reads=[g, px], writes=[g])
                k.op(k.dve, lambda e, m=m, g=g, g2=g2: e.tensor_scalar(
                    out=g2[:, 0:Wo], in0=g[:, 1:W - 1], scalar1=cv[:, 1, m:m + 1], scalar2=None, op0=ALU.mult),
                    reads=[g, cv], writes=[g2])
                k.op(k.dve, lambda e, m=m, g=g, g2=g2: e.scalar_tensor_tensor(
                    out=g2[:, 0:Wo], in0=g[:, 0:W - 2], scalar=cv[:, 0, m:m + 1], in1=g2[:, 0:Wo],
                    op0=ALU.mult, op1=ALU.add), reads=[g, cv, g2], writes=[g2])
                k.op(k.dve, lambda e, m=m, g=g, g2=g2: e.scalar_tensor_tensor(
                    out=g2[:, 0:Wo], in0=g[:, 2:W], scalar=cv[:, 2, m:m + 1], in1=g2[:, 0:Wo],
                    op0=ALU.mult, op1=ALU.add), reads=[g, cv, g2], writes=[g2])
                k.op(k.dve, lambda e, m=m, pb=pb, g2=g2: e.tensor_tensor(
                    out=self.hmid[m][:, 1:W - 1], in0=g2[:, 0:Wo], in1=pb[:, 1:W - 1], op=ALU.mult),
                    reads=[g2, pb], writes=[self.hmid[m]])
            self.back_out(dst, out_off, KC, self.hmid, W, o0, o1, False)


_PROG_CACHE = {}


def core_tokens(x_prompt, x_sample, c):
    if c < 4:
        return np.concatenate([x_prompt[c], x_sample[c]], axis=0), 1.0
    b = 4 + 3 * (c - 4)
    return np.concatenate([x_sample[b], x_sample[b + 1], x_sample[b + 2]], axis=0), 0.0


def host_layout(inputs):
    f = lambda a: np.asarray(a, np.float32)
    out = {}
    for nm in ("c_w_in", "c_w_out", "ffn_w_up", "ffn_w_down", "a_w_qkv", "a_w_o", "b_w_qkv", "b_w_o"):
        out[nm] = np.ascontiguousarray(f(inputs[nm]))
    ng = np.concatenate([f(inputs["norm_g"]).reshape(8, D), f(inputs["final_g"]).reshape(1, D)], 0)
    out["normg_h"] = np.ascontiguousarray(ng.reshape(9, KC, 128).transpose(2, 0, 1))
    cw = f(inputs["ffn_conv_w"]).reshape(4, 3, MC, 128)
    cb = f(inputs["ffn_conv_b"]).reshape(4, 1, MC, 128)
    out["ffn_cv"] = np.ascontiguousarray(np.concatenate([cw, cb], 1).transpose(0, 3, 1, 2))
    out["relb_rep"] = np.ascontiguousarray(np.broadcast_to(f(inputs["rel_bias"]).reshape(1, 512), (128, 512)))
    sk = f(inputs["b_sink"])[0].reshape(8, 2)
    out["sink_h"] = np.ascontiguousarray(np.repeat(sk.T, 64, axis=0))
    out["oh_mats"] = _OH_MATS
    for nm in ("d_w_in", "d_w_out"):
        out[nm] = np.ascontiguousarray(f(inputs[nm]))
    dw = f(inputs["d_conv_w"])[0].reshape(4, 24, 128)
    db = f(inputs["d_conv_b"])[0].reshape(1, 24, 128)
    out["d_cv"] = np.ascontiguousarray(np.concatenate([dw, db], 0).transpose(2, 0, 1))
    ii = np.arange(128)
    U = (ii[:, None] <= ii[None, :]).astype(np.float32)
    L = (ii[:, None] >= ii[None, :]).astype(np.float32)
    out["d_consts"] = np.ascontiguousarray(np.stack([U, L, (U - 1.0) * 1e4, (L - 1.0) * 1e4, np.eye(128, dtype=np.float32)], 1))
    rep = lambda v: np.ascontiguousarray(np.broadcast_to(v.reshape(1, -1), (128, v.size)))
    out["d_dtb_rep"] = rep(f(inputs["d_dt_bias"])[0])
    out["d_alog_rep"] = rep(f(inputs["d_a_log"])[0])
    out["d_skip_rep"] = rep(f(inputs["d_skip"])[0])
    out["d_normg_rep"] = rep(f(inputs["d_norm_g"])[0])
    out["c_cv"] = np.ascontiguousarray(f(inputs["c_conv_w"])[0].reshape(3, KC, 128).transpose(2, 0, 1))
    return out


def run(inputs, seg, layers=(0, 1, 2, 3), do_final=True):
    key = (seg, tuple(layers), do_final)
    prog = Prog(seg, layers, do_final)
    NT = 3 * seg
    xp = np.asarray(inputs["x_prompt"], np.float32)
    xs = np.asarray(inputs["x_sample"], np.float32)
    in_maps = []
    for c in range(8):
        tok, J = core_tokens(xp, xs, c)
        xT = np.zeros((D, NT + 2 * PAD), np.float32)
        xT[:, PAD:NT + PAD] = tok.T
        fl = np.zeros((128, 8), np.float32)
        fl[:, 1] = J
        fl[:, 2] = J
        f2 = np.ones((128, 12), np.float32)
        for sgi, (fL, fR) in enumerate(((0.0, J), (J, 0.0), (0.0, 0.0))):
            f2[0:64, sgi * 4 + 0] = fL
            f2[:, sgi * 4 + 1] = fL
            f2[64:128, sgi * 4 + 2] = fR
            f2[:, sgi * 4 + 3] = fR
        m = {"xT": xT, "flags": fl, "flags2": f2}
        m.update(host_layout(inputs))
        m = {kk: v for kk, v in m.items() if kk in prog.din}
        assert set(m) == set(prog.din), (set(prog.din) - set(m))
        in_maps.append(m)
    res = run_bass_kernel_spmd(prog.k.nc, in_maps, core_ids=list(range(8)))
    yp = np.zeros_like(xp)
    ys = np.zeros_like(xs)
    for c in range(8):
        y = np.asarray(res.results[c]["yT"]).T
        if c < 4:
            yp[c] = y[:2 * seg]
            ys[c] = y[2 * seg:]
        else:
            b = 4 + 3 * (c - 4)
            for i in range(3):
                ys[b + i] = y[i * seg:(i + 1) * seg]
    return yp, ys


def kernel(**inputs):
    return run(inputs, 4096)
```

```python
import numpy as np
import ml_dtypes
import concourse.bass as bass
import concourse.mybir as mybir
from concourse.bass_utils import run_bass_kernel_spmd

F32 = mybir.dt.float32
BF16 = mybir.dt.bfloat16
AF = mybir.ActivationFunctionType
ALU = mybir.AluOpType

D = 1024
KC = 8
DFF = 2816
MC = 22
EPS = 1e-6
NSEG = 3
PAD = 2
DIN = 2048
import os
DBG_ONLY = os.environ.get("DBG_ONLY", "")


ATT_CFG = {"A": [(1, 64), (4, 64), (16, 64)], "B": [(1, 128)]}
ATT_CI = {"A": [0, 1, 2], "B": [3]}
ALL_CFG = [(1, 64), (4, 64), (16, 64), (1, 128)]


def t5_bucket_np(rel):
    half, exact = 16, 8
    n = np.abs(rel)
    log_ratio = np.log(np.maximum(n, 1) / exact) / np.log(1024 / exact)
    large = np.minimum(exact + (log_ratio * (half - exact)).astype(np.int64), half - 1)
    return np.where(rel > 0, half, 0) + np.where(n < exact, n, large)


def oh_struct():
    keys, mats = [], []
    for ci, (r, half) in enumerate(ALL_CFG):
        nparts = 2 * half // 128 + 1
        for j in range(nparts):
            rel = 128 * j - half + np.arange(128)[:, None] - np.arange(128)[None, :]
            valid = np.abs(rel) <= half
            bk = t5_bucket_np(rel * r)
            for b in range(32):
                mk = valid & (bk == b)
                if mk.any():
                    keys.append((ci, j, b))
                    mats.append(mk.astype(np.float32))
            keys.append((ci, j, 32))
            mats.append((~valid).astype(np.float32))
    return keys, np.stack(mats)


OH_KEYS, _OH_MATS = oh_struct()


class Sem:
    def __init__(self, h, i):
        self.h = h
        self.i = i


class Eng:
    def __init__(self, name, h, sem):
        self.name = name
        self.h = h
        self.sem = sem
        self.cnt = 0
        self.seen = {}


class Buf:
    def __init__(self, ap, name=""):
        self.ap = ap
        self.name = name
        self.w = None
        self.r = {}
        self.dsem = None
        self.dcnt = 0
        self.psum = False

    def __getitem__(self, idx):
        return self.ap[idx]


class K:
    def __init__(self):
        self.nc = bass.Bass("TRN2", target_bir_lowering=False)
        nc = self.nc
        self.nsem = 0
        self.pe = Eng("pe", nc.tensor, self.newsem("pe"))
        self.act = Eng("act", nc.scalar, self.newsem("act"))
        self.dve = Eng("dve", nc.vector, self.newsem("dve"))
        self.pool = Eng("pool", nc.gpsimd, self.newsem("pool"))
        self.sp = Eng("sp", nc.sync, self.newsem("sp"))
        self.out_tokens = []
        self.nalloc = 0
        self.dma_latest = {}

    def newsem(self, name):
        s = Sem(self.nc.alloc_semaphore(f"s_{name}_{self.nsem}"), self.nsem)
        self.nsem += 1
        return s

    def sb(self, name, shape, dt):
        self.nalloc += 1
        t = self.nc.alloc_sbuf_tensor(f"{name}_{self.nalloc}", list(shape), dt)
        return t

    def ps(self, name):
        self.nalloc += 1
        t = self.nc.alloc_psum_tensor(f"{name}_{self.nalloc}", [128, 512], F32)
        b = Buf(t[:], name)
        b.psum = True
        return b

    def _wait(self, eng, deps):
        for (sem, val) in deps:
            if sem is eng.sem and eng.name == "pe":
                continue
            if eng.seen.get(sem.i, 0) < val:
                eng.h.wait_ge(sem.h, val)
                eng.seen[sem.i] = val

    @staticmethod
    def _deps(reads, writes):
        deps = []
        for b in reads:
            if b.w is not None:
                deps.append(b.w)
            if b.psum:
                deps.extend(b.r.values())
        for b in writes:
            if b.w is not None:
                deps.append(b.w)
            deps.extend(b.r.values())
        return deps

    @staticmethod
    def _mark(tok, reads, writes):
        for b in reads:
            b.r[tok[0].i] = tok
        for b in writes:
            b.w = tok
            b.r = {}

    def op(self, eng, fn, reads=(), writes=(), signal=True):
        self._wait(eng, self._deps(reads, writes))
        ins = fn(eng.h)
        tok = (eng.sem, eng.cnt + 1)
        ins.then_inc(eng.sem.h, 1)
        eng.cnt += 1
        self._mark(tok, reads, writes)
        return tok

    def dma(self, q, out, in_, owner, reads=(), writes=(), is_output=False):
        if owner.dsem is None:
            owner.dsem = self.newsem("d" + owner.name)
        self._wait(q, [d for d in self._deps(reads, writes) if d[0] is not owner.dsem])
        ins = q.h.dma_start(out=out, in_=in_)
        ins.then_inc(owner.dsem.h, 16)
        owner.dcnt += 16
        tok = (owner.dsem, owner.dcnt)
        self.dma_latest[owner.dsem.i] = tok
        self._mark(tok, reads, writes)
        if is_output:
            self.out_tokens.append(tok)
        return tok

    def barrier(self):
        engs = [self.pe, self.act, self.dve, self.pool, self.sp]
        toks = [(e.sem, e.cnt) for e in engs if e.cnt] + list(self.dma_latest.values())
        for e in engs:
            self._wait(e, toks)

    def finish(self):
        self._wait(self.sp, self.out_tokens)
        for e in (self.pe, self.act, self.dve, self.pool):
            if e.cnt:
                self._wait(self.sp, [(e.sem, e.cnt)])


class Prog:
    def __init__(self, seg, layers=(0, 1, 2, 3), do_final=True):
        self.SEG = seg
        self.NT = NSEG * seg
        self.layers = layers
        self.do_final = do_final
        self.k = K()
        k = self.k
        nc = k.nc
        NT = self.NT
        self.din = {}

        def inp(name, shape):
            self.din[name] = nc.dram_tensor(name, list(shape), F32, kind="ExternalInput").ap()
            return self.din[name]

        self.xin = inp("xT", [D, NT + 2 * PAD])
        self.flags_d = inp("flags", [128, 8])
        inp("c_w_in", [1, D, 3 * D]); inp("c_w_out", [1, D, D])
        inp("ffn_w_up", [4, D, 2 * DFF]); inp("ffn_cv", [4, 128, 4, MC])
        inp("ffn_w_down", [4, DFF, D]); inp("normg_h", [128, 9, KC]); inp("c_cv", [128, 3, KC])
        inp("a_w_qkv", [1, D, 9 * D]); inp("a_w_o", [1, D, D]); inp("b_w_qkv", [1, D, 1536]); inp("b_w_o", [1, D, D])
        inp("relb_rep", [128, 512]); inp("sink_h", [128, 8]); inp("oh_mats", [len(OH_KEYS), 128, 128]); inp("flags2", [128, 12])
        self.QT_d = nc.dram_tensor("QT_d", [3, D, NT], BF16, kind="Internal")
        self.KT_d = nc.dram_tensor("KT_d", [3, D, NT], BF16, kind="Internal")
        self.V_d = nc.dram_tensor("V_d", [3, NT, D], BF16, kind="Internal")
        self.OT_d = nc.dram_tensor("OT_d", [D, NT], BF16, kind="Internal")
        nb5 = (NT + 511) // 512
        self.r_qt = [[[Buf(None, "rq") for _ in range(nb5)] for _ in range(KC)] for _ in range(3)]
        self.r_kt = [[[Buf(None, "rk") for _ in range(nb5)] for _ in range(KC)] for _ in range(3)]
        self.r_v = [[[Buf(None, "rv") for _ in range(NT // 128)] for _ in range(2)] for _ in range(3)]
        self.r_ot = [[Buf(None, "ro") for _ in range(NSEG)] for _ in range(KC)]
        inp("d_w_in", [1, D, 5184]); inp("d_w_out", [1, DIN, D]); inp("d_cv", [128, 5, 24]); inp("d_consts", [128, 5, 128])
        inp("d_dtb_rep", [128, 64]); inp("d_alog_rep", [128, 64]); inp("d_skip_rep", [128, 32]); inp("d_normg_rep", [128, DIN])
        self.X_d = nc.dram_tensor("X_d", [NT, 2560], BF16, kind="Internal")
        self.BT_d = nc.dram_tensor("BT_d", [512, NT], BF16, kind="Internal")
        self.CT_d = nc.dram_tensor("CT_d", [512, NT], BF16, kind="Internal")
        self.Z_d = nc.dram_tensor("Z_d", [NT, DIN], BF16, kind="Internal")
        self.DTA_d = nc.dram_tensor("DTA_d", [NT, 128], F32, kind="Internal")
        self.Y_d = nc.dram_tensor("Y_d", [NT, DIN], F32, kind="Internal")
        self.YT_d = nc.dram_tensor("YT_d", [DIN, NT], BF16, kind="Internal")
        self.yT = nc.dram_tensor("yT", [D, NT], F32, kind="ExternalOutput").ap()
        self.xA = nc.dram_tensor("xA", [D, NT + 2 * PAD], F32, kind="Internal").ap()
        self.xB = nc.dram_tensor("xB", [D, NT + 2 * PAD], F32, kind="Internal").ap()
        self.nreg = (NT + 2 * PAD + 511) // 512
        self.regs = {}
        for nm in ("xin", "xA", "xB", "yT"):
            self.regs[nm] = [[Buf(None, f"{nm}{n}_{i}") for i in range(self.nreg)] for n in range(KC)]
        self.aps = {"xin": self.xin, "xA": self.xA, "xB": self.xB, "yT": self.yT}

        self.ones = Buf(k.sb("ones", [128, 128], BF16)[:], "ones")
        self.flags = Buf(k.sb("flags", [128, 8], F32)[:], "flags")
        self.zer = Buf(k.sb("zer", [128, 8], F32)[:], "zer")
        self.xt = Buf(k.sb("xt", [128, KC, 512], F32)[:], "xt")
        self.hT = [Buf(None, f"hT{i}") for i in range(KC)]
        hT_t = k.sb("hT", [128, KC, 512], BF16)
        for i in range(KC):
            self.hT[i].ap = hT_t[:, i, :]
        self.hT_t = hT_t
        self.sq = [Buf(k.sb("sq", [128, 512], BF16)[:], f"sq{i}") for i in range(2)]
        self.rstd = Buf(k.sb("rstd", [128, 512], F32)[:], "rstd")
        self.gbuf = [Buf(k.sb("gb", [128, 512], F32)[:], f"gb{i}") for i in range(3)]
        self.xo = [Buf(k.sb("xo", [128, 512], F32)[:], f"xo{i}") for i in range(3)]
        self.normg = Buf(k.sb("normg", [128, 9, KC], F32)[:], "normg")
        self.p_ss = k.ps("pss")
        self.p_a = [k.ps("pa0"), k.ps("pa1")]
        self.p_u = [k.ps("pu0"), k.ps("pu1")]
        self.p_o = [k.ps("po0"), k.ps("po1")]
        self.p_x = k.ps("px")
        self.WCOLS = KC * 2 * DFF + MC * D
        self.warena = k.sb("warena", [128, self.WCOLS], BF16)
        self.wbuf = Buf(self.warena[:], "warena")
        self.hmid_t = k.sb("hmid", [128, MC, 512], BF16)
        self.hmid = [Buf(self.hmid_t[:, i, :], f"hm{i}") for i in range(MC)]
        self.cvec = Buf(k.sb("cvec", [128, 4, MC], F32)[:], "cvec")

        self.qs = [Buf(k.sb("qs", [128, 512], BF16)[:], f"qs{i}") for i in range(4)]
        self.xr = [Buf(k.sb("xr", [128, 512], F32)[:], f"xr{i}") for i in range(2)]
        self.flags2 = Buf(k.sb("flags2", [128, 12], F32)[:], "flags2")
        self.es = Buf(k.sb("es", [128, 8], F32)[:], "es")
        self.erb = Buf(k.sb("erb", [128, 512], F32)[:], "erb")
        A = self.warena
        SEGc = self.SEG
        Hm = 1024
        o = 0
        def carve(n):
            nonlocal o
            v = A[:, o:o + n]
            o += n
            return v
        self.a_KT = Buf(carve(SEGc + 2 * Hm), "aKT")
        self.a_QT = Buf(carve(2 * SEGc).rearrange("p (a t) -> p a t", a=2), "aQT")
        self.a_V = Buf(carve(48 * 256).rearrange("p (b c) -> p b c", c=256), "aV")
        self.a_Vb = [Buf(self.a_V.ap, f"aVb{i}") for i in range(12)]
        self.a_tab = Buf(carve(3 * 8 * 2 * 256), "atab")
        self.a_accO = Buf(carve(2 * SEGc).bitcast(F32), "accO")
        self.a_accD = Buf(carve(2 * SEGc).bitcast(F32), "accD")
        self.a_OTn = Buf(carve(SEGc), "OTn")
        self.a_pexp = [Buf(carve(512).bitcast(F32), f"pexp{i}") for i in range(4)]
        self.a_PT = [Buf(carve(256), f"PT{i}") for i in range(4)]
        self.a_oh = Buf(carve(32 * 128).rearrange("p (b c) -> p b c", c=128), "aoh")
        self.a_ttmp = Buf(carve(256).bitcast(F32), "ttmp")
        self.a_onesz = Buf(carve(256), "onesz")
        self.a_ident = Buf(carve(128), "aident")
        self.a_ttmp2 = Buf(carve(256).bitcast(F32), "ttmp2")
        assert o <= self.WCOLS, o
        self.rr = 0
        with nc.allow_non_contiguous_dma(reason="few tiny halo/pad column transfers"):
            self.build()

    def regions(self, nm, c0, c1, n=None):
        out = []
        for nn in (range(KC) if n is None else [n]):
            out.extend(self.regs[nm][nn][c0 // 512:(c1 - 1) // 512 + 1])
        return out

    def build(self):
        k = self.k
        nc = k.nc
        k.op(k.dve, lambda e: e.memset(self.ones[:], 1.0), writes=[self.ones])
        k.op(k.dve, lambda e: e.memset(self.zer[:], 0.0), writes=[self.zer])
        k.dma(k.sp, self.flags[:], self.flags_d, self.flags, writes=[self.flags])
        k.dma(k.sp, self.flags2[:], self.din["flags2"], self.flags2, writes=[self.flags2])
        k.dma(k.sp, self.erb[:], self.din["relb_rep"], self.erb, writes=[self.erb])
        k.op(k.act, lambda e: e.activation(out=self.erb[:], in_=self.erb[:], func=AF.Identity, scale=8.0), reads=[self.erb], writes=[self.erb])
        k.dma(k.sp, self.normg[:], self.din["normg_h"], self.normg, writes=[self.normg])
        NT = self.NT
        for nm in ("xA", "xB"):
            for col in (0, 1, NT + PAD, NT + PAD + 1):
                dst = self.aps[nm][:, col:col + 1].rearrange("(k p) o -> p k o", p=128)
                k.dma(k.sp, dst, self.zer[:, 0:8].rearrange("p (k o) -> p k o", o=1), self.zer,
                      reads=[self.zer], writes=self.regions(nm, col, col + 1))
        cur = "xin"
        names = ["xB", "xA"]
        nlay = len(self.layers)
        for li, layer in enumerate(self.layers):
            kind = layer % 4
            dst = names[0]
            if DBG_ONLY == "ffn":
                kind = -1
            if kind == 2:
                self.pass_conv(layer, cur, dst)
            elif kind in (0, 1):
                self.pass_attention("A" if kind == 0 else "B", layer, cur, dst)
            elif kind == 3:
                self.pass_ssd(layer, cur, dst)
            if kind in (0, 1, 2, 3):
                cur = dst
                names = names[::-1]
                dst = names[0]
            last = (li == nlay - 1) and self.do_final
            if DBG_ONLY == "mixer":
                self.pass_copy_final(cur)
                continue
            self.pass_ffn(layer, cur, "yT" if last else dst, final=last)
            cur = dst
            names = names[::-1]
        k.finish()

    def windows(self, hl=1, hr=None):
        if hr is None:
            hr = hl
        SEG = self.SEG
        res = []
        for s in range(NSEG):
            a = s * SEG
            b = a + SEG
            o0 = a
            while o0 < b:
                o1 = min(b, o0 + 512 - hl - hr)
                c0 = o0 - hl
                W = (o1 + hr) - c0
                fl = (2 * s) if o0 == a else None
                fr = (2 * s + 1) if o1 == b else None
                res.append((c0, W, o0, o1, fl, fr))
                o0 = o1
        return res

    def load_weights_cast(self, dst_ap_fn, src, nk, ncols, wbuf_off):
        k = self.k
        step = 2048
        for kk in range(nk):
            for c in range(0, ncols, step):
                cc = min(step, ncols - c)
                o = wbuf_off + kk * ncols + c
                k.dma(k.pool, self.warena[:, o:o + cc], src[kk * 128:(kk + 1) * 128, c:c + cc],
                      self.wbuf, writes=[self.wbuf])

    def front(self, src, c0, W, gidx, fl, fr, hl=1, hr=1):
        self.front_a(src, c0, W)
        self.front_b(W, gidx, fl, fr, hl, hr)

    def front_a(self, src, c0, W):
        k = self.k
        xt, hT = self.xt, self.hT
        srcap = self.aps[src][:, c0 + PAD:c0 + PAD + W].rearrange("(k p) w -> p k w", p=128)
        k.dma(k.sp, xt[:, :, 0:W], srcap, xt, reads=self.regions(src, c0 + PAD, c0 + PAD + W), writes=[xt])
        for kk in range(KC):
            sq = self.sq[kk % 2]
            k.op(k.act, lambda e, kk=kk, sq=sq: e.activation(out=sq[:, 0:W], in_=xt[:, kk, 0:W], func=AF.Square),
                 reads=[xt], writes=[sq])
            k.op(k.pe, lambda e, kk=kk, sq=sq: e.matmul(self.p_ss[:, 0:W], self.ones[:], sq[:, 0:W],
                                                       start=(kk == 0), stop=(kk == KC - 1)),
                 reads=[sq, self.ones], writes=[self.p_ss], signal=(kk == KC - 1))
        rstd = self.rstd
        k.op(k.act, lambda e: e.activation(out=rstd[:, 0:W], in_=self.p_ss[:, 0:W], func=AF.Sqrt,
                                           bias=EPS, scale=1.0 / D), reads=[self.p_ss], writes=[rstd])
        k.op(k.dve, lambda e: e.reciprocal(out=rstd[:, 0:W], in_=rstd[:, 0:W]), reads=[rstd], writes=[rstd])

    def front_b(self, W, gidx, fl, fr, hl=1, hr=1):
        k = self.k
        xt, hT, rstd = self.xt, self.hT, self.rstd
        for kk in range(KC):
            k.op(k.dve, lambda e, kk=kk: e.scalar_tensor_tensor(
                out=hT[kk][:, 0:W], in0=xt[:, kk, 0:W], scalar=self.normg[:, gidx, kk:kk + 1],
                in1=rstd[:, 0:W], op0=ALU.mult, op1=ALU.mult),
                reads=[xt, rstd, self.normg], writes=[hT[kk]])
        for (f, col, wd) in ((fl, 0, hl), (fr, W - hr, hr)):
            if f is not None and wd > 0:
                k.op(k.dve, lambda e, f=f, col=col, wd=wd: e.tensor_scalar(
                    out=self.hT_t[:, :, col:col + wd], in0=self.hT_t[:, :, col:col + wd],
                    scalar1=self.flags[:, f:f + 1], scalar2=None, op0=ALU.mult),
                    reads=self.hT + [self.flags], writes=self.hT)

    def back_out(self, dst, wo_off, nmk, act_bufs, W, o0, o1, final, h=1, xres=None):
        k = self.k
        Wo = W - 2 * h
        for n in range(KC):
            po = self.p_o[n % 2]
            for m in range(nmk):
                o = wo_off + m * D + n * 128
                k.op(k.pe, lambda e, m=m, o=o, po=po: e.matmul(po[:, 0:Wo], self.warena[:, o:o + 128],
                                                              act_bufs[m][:, h:W - h],
                                                              start=(m == 0), stop=(m == nmk - 1)),
                     reads=[act_bufs[m], self.wbuf], writes=[po], signal=(m == nmk - 1))
            if not final:
                xo = self.xo[self.rr % 3]
                self.rr += 1
                if xres is None:
                    k.op(k.dve, lambda e, n=n, po=po, xo=xo: e.tensor_tensor(
                        out=xo[:, 0:Wo], in0=po[:, 0:Wo], in1=self.xt[:, n, h:W - h], op=ALU.add),
                        reads=[po, self.xt], writes=[xo])
                else:
                    xr = self.xr[n % 2]
                    k.dma(k.sp, xr[:, 0:Wo], self.aps[xres][n * 128:(n + 1) * 128, o0 + PAD:o1 + PAD], xr,
                          reads=self.regions(xres, o0 + PAD, o1 + PAD, n), writes=[xr])
                    k.op(k.dve, lambda e, n=n, po=po, xo=xo, xr=xr: e.tensor_tensor(
                        out=xo[:, 0:Wo], in0=po[:, 0:Wo], in1=xr[:, 0:Wo], op=ALU.add),
                        reads=[po, xr], writes=[xo])
                k.dma(k.sp, self.aps[dst][n * 128:(n + 1) * 128, o0 + PAD:o1 + PAD], xo[:, 0:Wo], xo,
                      reads=[xo], writes=self.regions(dst, o0 + PAD, o1 + PAD, n))
            else:
                k.op(k.dve, lambda e, n=n, po=po: e.tensor_tensor(
                    out=self.xt[:, n, h:W - h], in0=po[:, 0:Wo], in1=self.xt[:, n, h:W - h], op=ALU.add),
                    reads=[po, self.xt], writes=[self.xt])
        if final:
            xt = self.xt
            for kk in range(KC):
                sq = self.sq[kk % 2]
                k.op(k.act, lambda e, kk=kk, sq=sq: e.activation(out=sq[:, 0:Wo], in_=xt[:, kk, h:W - h], func=AF.Square),
                     reads=[xt], writes=[sq])
                k.op(k.pe, lambda e, kk=kk, sq=sq: e.matmul(self.p_ss[:, 0:Wo], self.ones[:], sq[:, 0:Wo],
                                                           start=(kk == 0), stop=(kk == KC - 1)),
                     reads=[sq, self.ones], writes=[self.p_ss], signal=(kk == KC - 1))
            rstd = self.rstd
            k.op(k.act, lambda e: e.activation(out=rstd[:, 0:Wo], in_=self.p_ss[:, 0:Wo], func=AF.Sqrt,
                                               bias=EPS, scale=1.0 / D), reads=[self.p_ss], writes=[rstd])
            k.op(k.dve, lambda e: e.reciprocal(out=rstd[:, 0:Wo], in_=rstd[:, 0:Wo]), reads=[rstd], writes=[rstd])
            for kk in range(KC):
                xo = self.xo[self.rr % 3]
                self.rr += 1
                k.op(k.dve, lambda e, kk=kk, xo=xo: e.scalar_tensor_tensor(
                    out=xo[:, 0:Wo], in0=xt[:, kk, h:W - h], scalar=self.normg[:, 8, kk:kk + 1],
                    in1=rstd[:, 0:Wo], op0=ALU.mult, op1=ALU.mult),
                    reads=[xt, rstd, self.normg], writes=[xo])
                k.dma(k.sp, self.yT[kk * 128:(kk + 1) * 128, o0:o1], xo[:, 0:Wo], xo,
                      reads=[xo], writes=self.regions("yT", o0, o1, kk), is_output=True)

    def pass_copy_final(self, src):
        k = self.k
        for c in range(0, self.NT, 512):
            srcap = self.aps[src][:, c + PAD:c + PAD + 512].rearrange("(k p) w -> p k w", p=128)
            k.dma(k.sp, self.xt[:, :, 0:512], srcap, self.xt, reads=self.regions(src, c + PAD, c + PAD + 512), writes=[self.xt])
            k.dma(k.sp, self.yT[:, c:c + 512].rearrange("(k p) w -> p k w", p=128), self.xt[:, :, 0:512], self.xt,
                  reads=[self.xt], writes=self.regions("yT", c, c + 512), is_output=True)

    def pass_ffn(self, layer, src, dst, final=False):
        k = self.k
        up_off = 0
        dn_off = KC * 2 * DFF
        self.load_weights_cast(None, self.din["ffn_w_up"][layer], KC, 2 * DFF, up_off)
        self.load_weights_cast(None, self.din["ffn_w_down"][layer], MC, D, dn_off)
        cv = self.cvec
        k.dma(k.sp, cv[:], self.din["ffn_cv"][layer], cv, writes=[cv])
        wins = self.windows(1)
        pre = not final
        if pre:
            self.front(src, wins[0][0], wins[0][1], layer * 2 + 1, wins[0][4], wins[0][5])
        for wi, (c0, W, o0, o1, fl, fr) in enumerate(wins):
            if not pre:
                self.front(src, c0, W, layer * 2 + 1, fl, fr)
            Wo = W - 2
            PA = [self.p_a[0], self.p_a[1], self.p_o[0], self.p_x]
            PU = [self.p_u[0], self.p_u[1], self.p_o[1], self.p_ss]
            for m in range(MC):
                pa = PA[m % 4]
                pu = PU[m % 4]
                for (pp, coff) in ((pa, m * 128), (pu, DFF + m * 128)):
                    for kk in range(KC):
                        o = up_off + kk * 2 * DFF + coff
                        k.op(k.pe, lambda e, kk=kk, o=o, pp=pp: e.matmul(
                            pp[:, 0:W], self.warena[:, o:o + 128], self.hT[kk][:, 0:W],
                            start=(kk == 0), stop=(kk == KC - 1)),
                            reads=[self.hT[kk], self.wbuf], writes=[pp], signal=(kk == KC - 1))
                g = self.gbuf[m % 3]
                k.op(k.act, lambda e, m=m, pa=pa, g=g: e.activation(
                    out=g[:, 0:Wo], in_=pa[:, 1:W - 1], func=AF.Identity,
                    bias=cv[:, 3, m:m + 1], scale=cv[:, 1, m:m + 1]), reads=[pa, cv], writes=[g])
                k.op(k.dve, lambda e, m=m, pa=pa, g=g: e.scalar_tensor_tensor(
                    out=g[:, 0:Wo], in0=pa[:, 0:W - 2], scalar=cv[:, 0, m:m + 1], in1=g[:, 0:Wo],
                    op0=ALU.mult, op1=ALU.add), reads=[pa, cv, g], writes=[g])
                k.op(k.dve, lambda e, m=m, pa=pa, g=g: e.scalar_tensor_tensor(
                    out=g[:, 0:Wo], in0=pa[:, 2:W], scalar=cv[:, 2, m:m + 1], in1=g[:, 0:Wo],
                    op0=ALU.mult, op1=ALU.add), reads=[pa, cv, g], writes=[g])
                k.op(k.act, lambda e, g=g: e.activation(out=g[:, 0:Wo], in_=g[:, 0:Wo], func=AF.Silu),
                     reads=[g], writes=[g])
                k.op(k.dve, lambda e, m=m, pu=pu, g=g: e.tensor_tensor(
                    out=self.hmid[m][:, 1:W - 1], in0=g[:, 0:Wo], in1=pu[:, 1:W - 1], op=ALU.mult),
                    reads=[g, pu], writes=[self.hmid[m]])
            if pre and wi + 1 < len(wins):
                nw = wins[wi + 1]
                self.front(src, nw[0], nw[1], layer * 2 + 1, nw[4], nw[5])
            self.back_out(dst, dn_off, MC, self.hmid, W, o0, o1, final, xres=(src if pre else None))

    def pass_attention(self, kind, layer, src, dst):
        k = self.k
        cfgs = ATT_CFG[kind]
        j0 = layer // 4
        NT, SEG = self.NT, self.SEG
        gidx = layer * 2
        for gi in range(len(cfgs)):
            if kind == "A":
                self.load_weights_cast(None, self.din["a_w_qkv"][j0][:, gi * 3 * D:(gi + 1) * 3 * D], KC, 3 * D, 0)
            else:
                wq = self.din["b_w_qkv"][j0]
                for kk in range(KC):
                    base = kk * 3 * D
                    rows = slice(kk * 128, (kk + 1) * 128)
                    k.dma(k.pool, self.warena[:, base:base + D], wq[rows, 0:D], self.wbuf, writes=[self.wbuf])
                    for part, scol in ((1, D), (2, D + 256)):
                        for cd in range(4):
                            dstv = self.warena[:, base + part * D:base + (part + 1) * D].rearrange(
                                "p (g c e) -> p g c e", g=4, c=4)[:, :, cd, :]
                            srcv = wq[rows, scol:scol + 256].rearrange("p (g e) -> p g e", g=4)
                            k.dma(k.pool, dstv, srcv, self.wbuf, writes=[self.wbuf])
            wins0 = self.windows(0)
            self.front_a(src, wins0[0][0], wins0[0][1])
            for wi0, (c0, W, o0, o1, fl, fr) in enumerate(wins0):
                self.front_b(W, gidx, None, None, 0, 0)
                if wi0 + 1 < len(wins0):
                    self.front_a(src, wins0[wi0 + 1][0], wins0[wi0 + 1][1])
                cb = c0 // 512
                for m in range(16):
                    pp = self.p_a[m % 2]
                    for kk in range(KC):
                        o = kk * 3 * D + m * 128
                        k.op(k.pe, lambda e, kk=kk, o=o, pp=pp: e.matmul(
                            pp[:, 0:W], self.warena[:, o:o + 128], self.hT[kk][:, 0:W],
                            start=(kk == 0), stop=(kk == KC - 1)), reads=[self.hT[kk], self.wbuf], writes=[pp])
                    q = self.qs[m % 4]
                    if m % 2 == 0:
                        k.op(k.act, lambda e, pp=pp, q=q: e.activation(out=q[:, 0:W], in_=pp[:, 0:W], func=AF.Copy),
                             reads=[pp], writes=[q])
                    else:
                        k.op(k.dve, lambda e, pp=pp, q=q: e.tensor_copy(out=q[:, 0:W], in_=pp[:, 0:W]),
                             reads=[pp], writes=[q])
                    dt_, rg = (self.QT_d, self.r_qt) if m < 8 else (self.KT_d, self.r_kt)
                    mm = m % 8
                    k.dma(k.sp, dt_.ap()[gi, mm * 128:(mm + 1) * 128, c0:c0 + W], q[:, 0:W], q,
                          reads=[q], writes=[rg[gi][mm][cb]])
                for tb in range(W // 128):
                    for hf in range(2):
                        pv = self.p_u[(tb * 2 + hf) % 2]
                        for kk in range(KC):
                            o = kk * 3 * D + 2 * D + hf * 512
                            k.op(k.pe, lambda e, kk=kk, o=o, pv=pv, tb=tb: e.matmul(
                                pv[:, 0:512], self.hT[kk][:, tb * 128:(tb + 1) * 128], self.warena[:, o:o + 512],
                                start=(kk == 0), stop=(kk == KC - 1)), reads=[self.hT[kk], self.wbuf], writes=[pv])
                        q = self.qs[(tb * 2 + hf) % 4]
                        if hf == 0:
                            k.op(k.act, lambda e, pv=pv, q=q: e.activation(out=q[:, 0:512], in_=pv[:, 0:512], func=AF.Copy),
                                 reads=[pv], writes=[q])
                        else:
                            k.op(k.dve, lambda e, pv=pv, q=q: e.tensor_copy(out=q[:, 0:512], in_=pv[:, 0:512]),
                                 reads=[pv], writes=[q])
                        t0 = c0 + tb * 128
                        k.dma(k.sp, self.V_d.ap()[gi, t0:t0 + 128, hf * 512:(hf + 1) * 512], q[:, 0:512], q,
                              reads=[q], writes=[self.r_v[gi][hf][t0 // 128]])
        k.barrier()
        self.attn_setup(kind, j0)
        for s_ in range(NSEG):
            for hp in range(KC):
                for gi, (r, half) in enumerate(cfgs):
                    self.attn_unit(kind, s_, hp, gi, r, half, first=(gi == 0))
                self.attn_finish(s_, hp)
        k.barrier()
        self.load_weights_cast(None, self.din["a_w_o" if kind == "A" else "b_w_o"][j0], KC, D, 0)
        for (c0, W, o0, o1, fl, fr) in self.windows(0):
            srcap = self.aps[src][:, c0 + PAD:c0 + PAD + W].rearrange("(k p) w -> p k w", p=128)
            k.dma(k.sp, self.xt[:, :, 0:W], srcap, self.xt, reads=self.regions(src, c0 + PAD, c0 + PAD + W), writes=[self.xt])
            for m in range(KC):
                k.dma(k.sp, self.hmid[m][:, 0:W], self.OT_d.ap()[m * 128:(m + 1) * 128, c0:c0 + W], self.hmid[m],
                      reads=[self.r_ot[m][c0 // SEG]], writes=[self.hmid[m]])
            self.back_out(dst, 0, KC, self.hmid, W, o0, o1, False, h=0)

    def attn_setup(self, kind, j0):
        k = self.k
        cfgs = ATT_CFG[kind]
        onesz = self.a_onesz
        k.op(k.dve, lambda e: e.memset(onesz[:], 0.0), writes=[onesz])
        k.op(k.dve, lambda e: e.memset(onesz[:, 0:64], 1.0), writes=[onesz])
        k.op(k.dve, lambda e: e.memset(onesz[:, 192:256], 1.0), writes=[onesz])
        k.dma(k.pool, self.a_ident[:], self.din["d_consts"][:, 4, :], self.a_ident, writes=[self.a_ident])
        k.op(k.dve, lambda e: e.memset(self.a_QT[:], 0.0), writes=[self.a_QT])
        k.op(k.dve, lambda e: e.memset(self.a_V[:], 0.0), writes=list(self.a_Vb))
        if kind == "B":
            k.dma(k.sp, self.es[:], self.din["sink_h"], self.es, writes=[self.es])
            k.op(k.act, lambda e: e.activation(out=self.es[:], in_=self.es[:], func=AF.Exp), reads=[self.es], writes=[self.es])
        else:
            k.op(k.dve, lambda e: e.memset(self.es[:], 0.0), writes=[self.es])
        oh, ttmp, tab = self.a_oh, self.a_ttmp, self.a_tab
        for gi, ci in enumerate(ATT_CI[kind]):
            r, half = ALL_CFG[ci]
            nparts = 2 * half // 128 + 1
            for j in range(nparts):
                idx = [i for i, kk in enumerate(OH_KEYS) if kk[0] == ci and kk[1] == j]
                bks = [OH_KEYS[i][2] for i in idx]
                nb = len(idx)
                k.dma(k.pool, oh[:, 0:nb, :], self.din["oh_mats"][idx[0]:idx[0] + nb].rearrange("b p c -> p b c"),
                      oh, writes=[oh])
                tts = (ttmp, self.a_ttmp2)
                for h0 in range(0, 16, 2):
                    for ii, b in enumerate(bks):
                        for h in (h0, h0 + 1):
                            tt = tts[h % 2]
                            hp, hh = h // 2, h % 2
                            to = ((gi * 8 + hp) * nparts + j) * 256 + hh * 128
                            tslot = tab[:, to:to + 128]
                            sc = self.erb[:, b * 16 + h:b * 16 + h + 1] if b < 32 else -30000.0
                            last = (ii == nb - 1)
                            outap = tslot if last else tt[:]
                            if ii == 0:
                                k.op(k.dve, lambda e, ii=ii, sc=sc, outap=outap: e.tensor_scalar(
                                    out=outap, in0=oh[:, ii, :], scalar1=sc, scalar2=None, op0=ALU.mult),
                                    reads=[oh, self.erb], writes=[tab if last else tt])
                            else:
                                k.op(k.dve, lambda e, ii=ii, sc=sc, outap=outap, tt=tt: e.scalar_tensor_tensor(
                                    out=outap, in0=oh[:, ii, :], scalar=sc, in1=tt[:], op0=ALU.mult, op1=ALU.add),
                                    reads=[oh, self.erb, tt], writes=[tab if last else tt])

    def attn_unit(self, kind, s_, hp, gi, r, half, first):
        k = self.k
        NT, SEG = self.NT, self.SEG
        H = half * r
        n = SEG // r
        nqb = n // 128
        nparts = 2 * half // 128 + 1
        nbk = nqb + nparts - 1
        KT, QT, V = self.a_KT, self.a_QT, self.a_V
        rows = slice(hp * 128, (hp + 1) * 128)
        base = s_ * SEG
        ktd, qtd = self.KT_d.ap()[gi], self.QT_d.ap()[gi]

        def kreg(a, b):
            return self.r_kt[gi][hp][a // 512:(b - 1) // 512 + 1]
        vt = self.V_d
        hf = hp // 4
        for rho in range(r):
            for m in range(nbk):
                tstart = base + rho + r * (128 * m - half)
                p0 = 0 if tstart >= 0 else (-tstart + r - 1) // r
                p1 = min(128, (NT - 1 - tstart) // r + 1)
                if p1 <= p0:
                    continue
                off = (gi * NT + tstart + r * p0) * D + hp * 128
                srcv = bass.AP(vt, off, [[r * D, p1 - p0], [64, 2], [1, 64]])
                dstv = V[p0:p1, rho * nbk + m, :].rearrange("p (a e) -> p a e", a=4)[:, 0:4:3, :]
                ta, tl = tstart + r * p0, tstart + r * (p1 - 1)
                vg = self.a_Vb[(rho * nbk + m) // 4]
                k.dma(k.sp, dstv, srcv, vg, reads=self.r_v[gi][hf][ta // 128:tl // 128 + 1], writes=[vg])
        la = max(0, base - H)
        k.dma(k.sp, KT[:, 0:H], ktd[rows, la:la + H], KT, reads=kreg(la, la + H), writes=[KT])
        k.dma(k.sp, KT[:, H:H + SEG], ktd[rows, base:base + SEG], KT, reads=kreg(base, base + SEG), writes=[KT])
        ra = min(NT - H, base + SEG)
        k.dma(k.sp, KT[:, H + SEG:H + SEG + H], ktd[rows, ra:ra + H], KT, reads=kreg(ra, ra + H), writes=[KT])
        qreg = self.r_qt[gi][hp][base // 512:(base + SEG - 1) // 512 + 1]
        k.dma(k.sp, QT[0:64, 0, 0:SEG], qtd[hp * 128:hp * 128 + 64, base:base + SEG], QT, reads=qreg, writes=[QT])
        k.dma(k.sp, QT[64:128, 1, 0:SEG], qtd[hp * 128 + 64:hp * 128 + 128, base:base + SEG], QT, reads=qreg, writes=[QT])
        accO, accD = self.a_accO, self.a_accD
        parts = []
        for rho in range(r):
            for qb in range(nqb):
                q0 = rho + r * 128 * qb
                qsl = slice(q0, q0 + 127 * r + 1, r)
                bi = rho * nqb + qb
                for j in range(nparts):
                    m = qb + j
                    lo = 128 * m - half
                    fcol = None
                    if lo < 0:
                        fcol = s_ * 4 + (0 if lo == -64 else 1)
                    elif lo + 128 > n:
                        fcol = s_ * 4 + (2 if lo + 128 - n == 64 else 3)
                    parts.append(dict(qsl=qsl, j=j, kc0=H + rho + r * lo, fcol=fcol, vb=rho * nbk + m,
                                      pO=self.p_o[bi % 2], pD=self.p_u[bi % 2]))
        SB = [self.p_a[0], self.p_a[1], self.p_ss, self.p_x]
        NS, LAG = 4, 3

        def stageA(i):
            p = parts[i]
            pS, pexp, PT = SB[i % NS], self.a_pexp[i % NS], self.a_PT[i % NS]
            kc0, qsl, j, fcol = p["kc0"], p["qsl"], p["j"], p["fcol"]
            to = ((gi * 8 + hp) * nparts + j) * 256
            tabv = self.a_tab[:, to:to + 256]
            k.op(k.pe, lambda e: e.matmul(pS[:, 0:256], KT[:, kc0:kc0 + 127 * r + 1:r], QT[:, :, qsl], start=True, stop=False),
                 reads=[KT, QT], writes=[pS])
            k.op(k.pe, lambda e: e.matmul(pS[:, 0:256], self.a_ident[:], tabv, start=False, stop=True),
                 reads=[self.a_ident, self.a_tab], writes=[pS])
            if fcol is None:
                k.op(k.act, lambda e: e.activation(out=PT[:], in_=pS[:, 0:256], func=AF.Exp, scale=0.125),
                     reads=[pS], writes=[PT])
            else:
                k.op(k.act, lambda e: e.activation(out=pexp[:], in_=pS[:, 0:256], func=AF.Exp, scale=0.125),
                     reads=[pS], writes=[pexp])
                k.op(k.dve, lambda e: e.tensor_scalar(out=PT[:], in0=pexp[:], scalar1=self.flags2[:, fcol:fcol + 1],
                                                      scalar2=None, op0=ALU.mult), reads=[pexp, self.flags2], writes=[PT])

        def stageB(i):
            p = parts[i]
            PT = self.a_PT[i % NS]
            j, vb, pO, pD, qsl = p["j"], p["vb"], p["pO"], p["pD"], p["qsl"]
            for hh in range(2):
                k.op(k.pe, lambda e, hh=hh: e.matmul(
                    pO[:, 0:128], V[:, vb, hh * 128:(hh + 1) * 128], PT[:, hh * 128:(hh + 1) * 128],
                    start=(j == 0 and hh == 0), stop=(j == nparts - 1 and hh == 1)), reads=[self.a_Vb[vb // 4], PT], writes=[pO])
            for hh in range(2):
                k.op(k.pe, lambda e, hh=hh: e.matmul(
                    pD[:, 0:128], self.a_onesz[:, hh * 128:(hh + 1) * 128], PT[:, hh * 128:(hh + 1) * 128],
                    start=(j == 0 and hh == 0), stop=(j == nparts - 1 and hh == 1)), reads=[self.a_onesz, PT], writes=[pD])
            if j == nparts - 1:
                for (acc, pp) in ((accO, pO), (accD, pD)):
                    if first:
                        k.op(k.dve, lambda e, acc=acc, pp=pp: e.tensor_copy(out=acc[:, qsl], in_=pp[:, 0:128]),
                             reads=[pp], writes=[acc])
                    else:
                        k.op(k.dve, lambda e, acc=acc, pp=pp: e.tensor_tensor(
                            out=acc[:, qsl], in0=acc[:, qsl], in1=pp[:, 0:128], op=ALU.add),
                            reads=[pp, acc], writes=[acc])

        npt = len(parts)
        for i in range(npt + LAG):
            if i < npt:
                stageA(i)
            if i >= LAG:
                stageB(i - LAG)

    def attn_finish(self, s_, hp):
        k = self.k
        SEG = self.SEG
        accO, accD, OTn = self.a_accO, self.a_accD, self.a_OTn
        k.op(k.dve, lambda e: e.tensor_scalar(out=accD[:, 0:SEG], in0=accD[:, 0:SEG], scalar1=self.es[:, hp:hp + 1],
                                              scalar2=None, op0=ALU.add), reads=[accD, self.es], writes=[accD])
        k.op(k.dve, lambda e: e.reciprocal(out=accD[:, 0:SEG], in_=accD[:, 0:SEG]), reads=[accD], writes=[accD])
        k.op(k.dve, lambda e: e.tensor_tensor(out=OTn[:, 0:SEG], in0=accO[:, 0:SEG], in1=accD[:, 0:SEG], op=ALU.mult),
             reads=[accO, accD], writes=[OTn])
        k.dma(k.sp, self.OT_d.ap()[hp * 128:(hp + 1) * 128, s_ * SEG:(s_ + 1) * SEG], OTn[:, 0:SEG], OTn,
              reads=[OTn], writes=[self.r_ot[hp][s_]])


    def pass_ssd(self, layer, src, dst):
        k = self.k
        j0 = layer // 4
        NT, SEG = self.NT, self.SEG
        A = self.warena
        WIN = 5184
        self.load_weights_cast(None, self.din["d_w_in"][j0], KC, WIN, 0)
        o = KC * WIN
        def carve(n):
            nonlocal o
            v = A[:, o:o + n]
            o += n
            return v
        dcv = Buf(carve(2 * 5 * 24).bitcast(F32).rearrange("p (a b) -> p a b", a=5), "dcv")
        ident = Buf(carve(128), "ident")
        dtb = Buf(carve(128).bitcast(F32), "dtb")
        aneg = Buf(carve(128).bitcast(F32), "aneg")
        dta_s = [Buf(carve(256).bitcast(F32), f"dtas{i}") for i in range(2)]
        tst = [Buf(carve(512), f"tst{i}") for i in range(3)]
        assert o <= self.WCOLS
        k.dma(k.sp, dcv[:], self.din["d_cv"], dcv, writes=[dcv])
        k.dma(k.pool, ident[:], self.din["d_consts"][:, 4, :], ident, writes=[ident])
        k.dma(k.sp, dtb[:], self.din["d_dtb_rep"], dtb, writes=[dtb])
        k.dma(k.sp, aneg[:], self.din["d_alog_rep"], aneg, writes=[aneg])
        k.op(k.act, lambda e: e.activation(out=aneg[:], in_=aneg[:], func=AF.Exp), reads=[aneg], writes=[aneg])
        k.op(k.dve, lambda e: e.tensor_scalar(out=aneg[:], in0=aneg[:], scalar1=-1.0, scalar2=None, op0=ALU.mult),
             reads=[aneg], writes=[aneg])
        xs = [self.hmid[m] if m < 20 else self.hmid[20 + m % 2] for m in range(24)]
        pxb = self.p_x[:].bitcast(BF16)
        winsd = self.windows(2, 1)
        self.front_a(src, winsd[0][0], winsd[0][1])
        for wid, (c0, W, o0, o1, fl, fr) in enumerate(winsd):
            self.front_b(W, layer * 2, fl, fr, 2, 1)
            if wid + 1 < len(winsd):
                self.front_a(src, winsd[wid + 1][0], winsd[wid + 1][1])
            Wo = W - 3
            for m in range(24):
                pa = self.p_a[m % 2]
                for kk in range(KC):
                    oo = kk * WIN + DIN + m * 128
                    k.op(k.pe, lambda e, kk=kk, oo=oo, pa=pa: e.matmul(
                        pa[:, 0:W], A[:, oo:oo + 128], self.hT[kk][:, 0:W], start=(kk == 0), stop=(kk == KC - 1)),
                        reads=[self.hT[kk], self.wbuf], writes=[pa])
                g = self.gbuf[m % 3]
                k.op(k.act, lambda e, m=m, pa=pa, g=g: e.activation(
                    out=g[:, 0:Wo], in_=pa[:, 2:W - 1], func=AF.Identity, bias=dcv[:, 4, m:m + 1], scale=dcv[:, 2, m:m + 1]),
                    reads=[pa, dcv], writes=[g])
                for (tap, a0) in ((0, 0), (1, 1), (3, 3)):
                    k.op(k.dve, lambda e, m=m, pa=pa, g=g, tap=tap, a0=a0: e.scalar_tensor_tensor(
                        out=g[:, 0:Wo], in0=pa[:, a0:a0 + Wo], scalar=dcv[:, tap, m:m + 1], in1=g[:, 0:Wo],
                        op0=ALU.mult, op1=ALU.add), reads=[pa, dcv, g], writes=[g])
                k.op(k.act, lambda e, m=m, g=g: e.activation(out=xs[m][:, 0:Wo], in_=g[:, 0:Wo], func=AF.Silu),
                     reads=[g], writes=[xs[m]])
                if m >= 16:
                    dt_ = self.BT_d if m < 20 else self.CT_d
                    mm = (m - 16) % 4
                    k.dma(k.sp, dt_.ap()[mm * 128:(mm + 1) * 128, o0:o1], xs[m][:, 0:Wo], xs[m], reads=[xs[m]])
            for ib, a in enumerate(range(o0, o1, 128)):
                b = min(o1, a + 128)
                nb = b - a
                la, lb = a - o0, b - o0
                ha, hb = a - c0, b - c0
                for q4 in range(5):
                    for i in range(4):
                        m = q4 * 4 + i
                        k.op(k.pe, lambda e, m=m, i=i, la=la, lb=lb, nb=nb: e.transpose(
                            pxb[0:nb, i * 128:(i + 1) * 128], xs[m][:, la:lb], ident[:]),
                            reads=[xs[m], ident], writes=[self.p_x])
                    st = tst[q4 % 3]
                    k.op(k.act if q4 % 2 == 0 else k.dve,
                         (lambda e, st=st, nb=nb: e.activation(out=st[0:nb, :], in_=pxb[0:nb, 0:512], func=AF.Copy)) if q4 % 2 == 0
                         else (lambda e, st=st, nb=nb: e.tensor_copy(out=st[0:nb, :], in_=pxb[0:nb, 0:512])),
                         reads=[self.p_x], writes=[st])
                    k.dma(k.sp, self.X_d.ap()[a:b, q4 * 512:(q4 + 1) * 512], st[0:nb, :], st, reads=[st])
                for q4 in range(4):
                    pz = self.p_u[q4 % 2]
                    for kk in range(KC):
                        oo = kk * WIN + q4 * 512
                        k.op(k.pe, lambda e, kk=kk, oo=oo, pz=pz, ha=ha, hb=hb, nb=nb: e.matmul(
                            pz[0:nb, 0:512], self.hT[kk][:, ha:hb], A[:, oo:oo + 512], start=(kk == 0), stop=(kk == KC - 1)),
                            reads=[self.hT[kk], self.wbuf], writes=[pz])
                    st = tst[(q4 + 2) % 3]
                    k.op(k.act, lambda e, st=st, pz=pz, nb=nb: e.activation(out=st[0:nb, :], in_=pz[0:nb, 0:512], func=AF.Silu),
                         reads=[pz], writes=[st])
                    k.dma(k.sp, self.Z_d.ap()[a:b, q4 * 512:(q4 + 1) * 512], st[0:nb, :], st, reads=[st])
                pd = self.p_o[ib % 2]
                for kk in range(KC):
                    oo = kk * WIN + DIN + 3072
                    k.op(k.pe, lambda e, kk=kk, oo=oo, pd=pd, ha=ha, hb=hb, nb=nb: e.matmul(
                        pd[0:nb, 0:64], self.hT[kk][:, ha:hb], A[:, oo:oo + 64], start=(kk == 0), stop=(kk == KC - 1)),
                        reads=[self.hT[kk], self.wbuf], writes=[pd])
                ds = dta_s[ib % 2]
                k.op(k.dve, lambda e, ds=ds, pd=pd, nb=nb: e.tensor_tensor(out=ds[0:nb, 0:64], in0=pd[0:nb, 0:64], in1=dtb[0:nb, :], op=ALU.add),
                     reads=[pd, dtb], writes=[ds])
                k.op(k.act, lambda e, ds=ds, nb=nb: e.activation(out=ds[0:nb, 0:64], in_=ds[0:nb, 0:64], func=AF.Exp), reads=[ds], writes=[ds])
                k.op(k.act, lambda e, ds=ds, nb=nb: e.activation(out=ds[0:nb, 0:64], in_=ds[0:nb, 0:64], func=AF.Ln, bias=1.0), reads=[ds], writes=[ds])
                k.op(k.dve, lambda e, ds=ds, nb=nb: e.tensor_tensor(out=ds[0:nb, 64:128], in0=ds[0:nb, 0:64], in1=aneg[0:nb, :], op=ALU.mult),
                     reads=[ds, aneg], writes=[ds])
                k.dma(k.sp, self.DTA_d.ap()[a:b, :], ds[0:nb, :], ds, reads=[ds])
        k.barrier()
        o = 0
        cst = Buf(carve(4 * 256).bitcast(F32).rearrange("p (a b) -> p a b", a=4), "cst")
        ident = Buf(carve(128), "ident2")
        ones32 = Buf(carve(256).bitcast(F32), "ones32")
        skp = Buf(carve(64).bitcast(F32), "skp")
        ngr = Buf(carve(2 * DIN).bitcast(F32), "ngr")
        NB2 = 2
        xtokS = [Buf(carve(2560), f"xtok{i}") for i in range(NB2)]
        BTtS = [Buf(carve(512).rearrange("p (g t) -> p g t", g=4), f"BTt{i}") for i in range(NB2)]
        CTtS = [Buf(carve(512).rearrange("p (g t) -> p g t", g=4), f"CTt{i}") for i in range(NB2)]
        dtaS = [Buf(carve(256).bitcast(F32), f"dta{i}") for i in range(NB2)]
        ztS = [Buf(carve(DIN), f"zt{i}") for i in range(NB2)]
        yaccS = [Buf(carve(2 * DIN).bitcast(F32), f"yacc{i}") for i in range(NB2)]
        hst = [Buf(carve(2 * 512).bitcast(F32), f"hst{g}") for g in range(4)]
        hbf = [Buf(carve(512), f"hbf{g}") for g in range(4)]
        xdt = Buf(carve(DIN), "xdt")
        xsk = Buf(carve(DIN), "xsk")
        xdd = [Buf(carve(512), f"xdd{i}") for i in range(2)]
        NR = 3
        Ue = [Buf(carve(1024).bitcast(F32), f"Ue{i}") for i in range(NR)]
        Dm = [Buf(carve(1024).bitcast(F32), f"Dm{i}") for i in range(NR)]
        MT = [Buf(carve(512), f"MT{i}") for i in range(NR)]
        CBs = [Buf(carve(256).bitcast(F32), f"CB{g}") for g in range(4)]
        acsS = [Buf(carve(64).bitcast(F32), f"acs{i}") for i in range(2)]
        eaS = [Buf(carve(64).bitcast(F32), f"ea{i}") for i in range(2)]
        aclS = [Buf(carve(64).bitcast(F32), f"acl{i}") for i in range(2)]
        cdr = Buf(carve(64).bitcast(F32), "cdr")
        dec = Buf(carve(64).bitcast(F32), "dec")
        tmpg = [Buf(carve(1024).bitcast(F32), f"tmpg{i}") for i in range(2)]
        junk = Buf(carve(DIN), "junk")
        ssq = Buf(carve(4).bitcast(F32), "ssq")
        yn = Buf(carve(DIN), "yn")
        yst = [Buf(carve(512).rearrange("p (a t) -> p a t", a=4), f"yst{i}") for i in range(2)]
        assert o <= self.WCOLS, o
        k.dma(k.sp, cst[:], self.din["d_consts"][:, 0:4, :], cst, writes=[cst])
        k.dma(k.pool, ident[:], self.din["d_consts"][:, 4, :], ident, writes=[ident])
        k.op(k.dve, lambda e: e.memset(ones32[:], 1.0), writes=[ones32])
        k.dma(k.sp, skp[:], self.din["d_skip_rep"], skp, writes=[skp])
        k.dma(k.sp, ngr[:], self.din["d_normg_rep"], ngr, writes=[ngr])
        NCH = SEG // 128
        pyb = self.p_x[:].bitcast(BF16)

        def bc(ap2, n):
            return ap2.unsqueeze(2).broadcast_to([128, ap2.shape[1], n])

        def v3(ap2):
            return ap2.rearrange("p (e q) -> p e q", q=64)

        def q4v(ap2):
            return ap2.rearrange("p (a l) -> p a l", a=4)

        for d in range(2):
            order = list(range(NSEG * NCH))
            if d == 1:
                order = order[::-1]

            def emit_loads(ci):
                c = order[ci]
                t0 = c * 128
                sl = ci % NB2
                k.dma(k.sp, xtokS[sl][:], self.X_d.ap()[t0:t0 + 128, :], xtokS[sl], writes=[xtokS[sl]])
                k.dma(k.sp, BTtS[sl][:], self.BT_d.ap()[:, t0:t0 + 128].rearrange("(g n) t -> n g t", g=4), BTtS[sl], writes=[BTtS[sl]])
                k.dma(k.sp, CTtS[sl][:], self.CT_d.ap()[:, t0:t0 + 128].rearrange("(g n) t -> n g t", g=4), CTtS[sl], writes=[CTtS[sl]])
                k.dma(k.sp, dtaS[sl][:], self.DTA_d.ap()[t0:t0 + 128, :], dtaS[sl], writes=[dtaS[sl]])
                if d == 1:
                    k.dma(k.sp, yaccS[sl][:], self.Y_d.ap()[t0:t0 + 128, :], yaccS[sl], writes=[yaccS[sl]])
                    k.dma(k.sp, ztS[sl][:], self.Z_d.ap()[t0:t0 + 128, :], ztS[sl], writes=[ztS[sl]])

            emit_loads(0)
            for ci, c in enumerate(order):
                t0 = c * 128
                sg, cc = c // NCH, c % NCH
                sl = ci % NB2
                xtok, BTt, CTt, dta, zt, yacc = xtokS[sl], BTtS[sl], CTtS[sl], dtaS[sl], ztS[sl], yaccS[sl]
                acs, ea, acl = acsS[ci % 2], eaS[ci % 2], aclS[ci % 2]
                if ci + 1 < len(order):
                    emit_loads(ci + 1)
                start_seq = (cc == 0) if d == 0 else (cc == NCH - 1)
                if start_seq:
                    joined = (sg == 1) if d == 0 else (sg == 0)
                    for g in range(4):
                        if joined:
                            k.op(k.dve, lambda e, g=g: e.tensor_scalar(out=hst[g][:], in0=hst[g][:], scalar1=self.flags[:, 1:2],
                                                                      scalar2=None, op0=ALU.mult), reads=[hst[g], self.flags], writes=[hst[g]])
                        else:
                            k.op(k.dve, lambda e, g=g: e.memset(hst[g][:], 0.0), writes=[hst[g]])
                        k.op(k.act, lambda e, g=g: e.activation(out=hbf[g][:], in_=hst[g][:], func=AF.Copy), reads=[hst[g]], writes=[hbf[g]])
                adt = dta[:, 64 + d * 32:64 + (d + 1) * 32]
                dtv = dta[:, d * 32:(d + 1) * 32]
                tri = cst[:, d, :]
                msk = cst[:, 2 + d, :]
                last = 127 if d == 0 else 0
                pcs = self.p_ss
                k.op(k.pe, lambda e, tri=tri, adt=adt: e.matmul(pcs[:, 0:32], tri, adt, start=True, stop=True),
                     reads=[cst, dta], writes=[pcs])
                k.op(k.act, lambda e, acs=acs: e.activation(out=acs[:], in_=pcs[:, 0:32], func=AF.Copy), reads=[pcs], writes=[acs])
                k.op(k.act, lambda e, ea=ea: e.activation(out=ea[:], in_=pcs[:, 0:32], func=AF.Exp), reads=[pcs], writes=[ea])
                for g in range(4):
                    k.op(k.pe, lambda e, g=g, BTt=BTt, CTt=CTt: e.matmul(self.p_o[g % 2][:, 0:128], BTt[:, g, :], CTt[:, g, :], start=True, stop=True),
                         reads=[BTt, CTt], writes=[self.p_o[g % 2]])
                    k.op(k.act, lambda e, g=g: e.activation(out=CBs[g][:], in_=self.p_o[g % 2][:, 0:128], func=AF.Copy),
                         reads=[self.p_o[g % 2]], writes=[CBs[g]])

                def stA(qd):
                    e0 = qd * 4
                    ue, pac = Ue[qd % NR], self.p_a[qd % 2]
                    k.op(k.dve, lambda e: e.tensor_tensor(
                        out=q4v(ue[:]), in0=tri.unsqueeze(1).broadcast_to([128, 4, 128]),
                        in1=bc(adt[:, e0:e0 + 4], 128), op=ALU.mult), reads=[cst, dta], writes=[ue])
                    k.op(k.pe, lambda e: e.matmul(pac[:, 0:512], ones32[:], ue[:], start=True, stop=True),
                         reads=[ones32, ue], writes=[pac])

                def stB1(qd):
                    e0 = qd * 4
                    dm, pac = Dm[qd % NR], self.p_a[qd % 2]
                    for i in range(4):
                        k.op(k.dve, lambda e, i=i: e.scalar_tensor_tensor(
                            out=dm[:, i * 128:(i + 1) * 128], in0=pac[:, i * 128:(i + 1) * 128], scalar=acs[:, e0 + i:e0 + i + 1],
                            in1=msk, op0=ALU.subtract, op1=ALU.min), reads=[pac, acs, cst], writes=[dm])
                    pl = q4v(pac[:, 0:512])[:, :, last:last + 1]
                    k.op(k.act, lambda e: e.activation(out=dm[:], in_=dm[:], func=AF.Exp), reads=[dm], writes=[dm])
                    k.op(k.act, lambda e: e.activation(out=acl[:, e0:e0 + 4].unsqueeze(2), in_=pl, func=AF.Copy),
                         reads=[pac], writes=[acl])

                def stB2(qd):
                    g = qd // 2
                    e0 = qd * 4
                    dm, mt = Dm[qd % NR], MT[qd % NR]
                    pdg = self.p_u[g % 2]
                    k.op(k.dve, lambda e: e.tensor_tensor(
                        out=q4v(mt[:]), in0=q4v(dm[:]), in1=CBs[g][:].unsqueeze(1).broadcast_to([128, 4, 128]), op=ALU.mult),
                        reads=[dm, CBs[g]], writes=[mt])
                    for i in range(4):
                        eh = e0 + i
                        k.op(k.pe, lambda e, i=i, eh=eh: e.matmul(
                            pdg[:, (eh % 8) * 64:(eh % 8) * 64 + 64], mt[:, i * 128:(i + 1) * 128], xdt[:, eh * 64:(eh + 1) * 64],
                            start=True, stop=True), reads=[mt, xdt], writes=[pdg])

                def stC(g):
                    pdg = self.p_u[g % 2]
                    pof = self.p_o[g % 2]
                    k.op(k.pe, lambda e: e.matmul(pof[:, 0:512], CTt[:, g, :], hbf[g][:], start=True, stop=True),
                         reads=[CTt, hbf[g]], writes=[pof])
                    tg = tmpg[g % 2]
                    ysl = yacc[:, g * 512:(g + 1) * 512]
                    k.op(k.dve, lambda e: e.tensor_tensor(out=v3(tg[:]), in0=v3(pof[:, 0:512]),
                                                          in1=bc(ea[:, g * 8:(g + 1) * 8], 64), op=ALU.mult), reads=[pof, ea], writes=[tg])
                    if d == 0:
                        k.op(k.dve, lambda e: e.tensor_tensor(out=ysl, in0=tg[:], in1=pdg[:, 0:512], op=ALU.add),
                             reads=[tg, pdg], writes=[yacc])
                    else:
                        k.op(k.dve, lambda e: e.tensor_tensor(out=ysl, in0=ysl, in1=tg[:], op=ALU.add),
                             reads=[tg, yacc], writes=[yacc])
                        k.op(k.dve, lambda e: e.tensor_tensor(out=ysl, in0=ysl, in1=pdg[:, 0:512], op=ALU.add),
                             reads=[pdg, yacc], writes=[yacc])
                    g8 = slice(g * 8, (g + 1) * 8)
                    k.op(k.dve, lambda e: e.tensor_tensor(out=dec[:, g8], in0=acl[:, g8], in1=acs[:, g8], op=ALU.subtract),
                         reads=[acl, acs], writes=[dec])
                    k.op(k.act, lambda e: e.activation(out=dec[:, g8], in_=dec[:, g8], func=AF.Exp), reads=[dec], writes=[dec])
                    k.op(k.act, lambda e: e.activation(out=cdr[:, g8], in_=acl[:, g8], func=AF.Exp), reads=[acl], writes=[cdr])
                    xd = xdd[g % 2]
                    k.op(k.dve, lambda e: e.tensor_tensor(out=v3(xd[:]), in0=v3(xdt[:, g * 512:(g + 1) * 512]),
                                                          in1=bc(dec[:, g8], 64), op=ALU.mult), reads=[xdt, dec], writes=[xd])
                    pst = self.p_x
                    k.op(k.pe, lambda e: e.matmul(pst[:, 0:512], xtok[:, DIN + g * 128:DIN + (g + 1) * 128], xd[:],
                                                  start=True, stop=True), reads=[xtok, xd], writes=[pst])
                    k.op(k.dve, lambda e: e.tensor_tensor(out=v3(hst[g][:]), in0=v3(hst[g][:]), in1=bc(cdr[:, g8], 64), op=ALU.mult),
                         reads=[hst[g], cdr, hbf[g]], writes=[hst[g]])
                    k.op(k.dve, lambda e: e.tensor_tensor(out=hst[g][:], in0=hst[g][:], in1=pst[:, 0:512], op=ALU.add),
                         reads=[hst[g], pst], writes=[hst[g]])
                    k.op(k.act, lambda e: e.activation(out=hbf[g][:], in_=hst[g][:], func=AF.Copy), reads=[hst[g]], writes=[hbf[g]])

                stA(0)
                stA(1)
                k.op(k.dve, lambda e, dtv=dtv, xtok=xtok: e.tensor_tensor(out=v3(xdt[:]), in0=v3(xtok[:, 0:DIN]), in1=bc(dtv, 64), op=ALU.mult),
                     reads=[xtok, dta], writes=[xdt])
                for it in range(2, 8 + 3):
                    if it - 2 < 8:
                        stB1(it - 2)
                    if it < 8:
                        stA(it)
                    if 0 <= it - 3 < 8:
                        stB2(it - 3)
                        if (it - 3) % 2 == 1:
                            stC((it - 3) // 2)
                if d == 0:
                    k.dma(k.sp, self.Y_d.ap()[t0:t0 + 128, :], yacc[:], yacc, reads=[yacc])
                else:
                    k.op(k.dve, lambda e, xtok=xtok: e.tensor_tensor(out=v3(xsk[:]), in0=v3(xtok[:, 0:DIN]), in1=bc(skp[:], 64), op=ALU.mult),
                         reads=[xtok, skp], writes=[xsk])
                    k.op(k.dve, lambda e, yacc=yacc: e.tensor_tensor(out=yacc[:], in0=yacc[:], in1=xsk[:], op=ALU.add), reads=[yacc, xsk], writes=[yacc])
                    k.op(k.dve, lambda e, yacc=yacc, zt=zt: e.tensor_tensor(out=yacc[:], in0=yacc[:], in1=zt[:], op=ALU.mult), reads=[yacc, zt], writes=[yacc])
                    k.op(k.act, lambda e, yacc=yacc: e.activation(out=junk[:], in_=yacc[:], func=AF.Square, accum_out=ssq[:, 0:1]),
                         reads=[yacc], writes=[junk, ssq])
                    k.op(k.act, lambda e: e.activation(out=ssq[:, 0:1], in_=ssq[:, 0:1], func=AF.Sqrt, bias=EPS, scale=1.0 / DIN),
                         reads=[ssq], writes=[ssq])
                    k.op(k.dve, lambda e: e.reciprocal(out=ssq[:, 0:1], in_=ssq[:, 0:1]), reads=[ssq], writes=[ssq])
                    k.op(k.dve, lambda e, yacc=yacc: e.scalar_tensor_tensor(out=yn[:], in0=yacc[:], scalar=ssq[:, 0:1], in1=ngr[:],
                                                                            op0=ALU.mult, op1=ALU.mult), reads=[yacc, ssq, ngr], writes=[yn])
                    for q4 in range(4):
                        for i in range(4):
                            kk = q4 * 4 + i
                            k.op(k.pe, lambda e, kk=kk, i=i: e.transpose(pyb[:, i * 128:(i + 1) * 128], yn[:, kk * 128:(kk + 1) * 128], ident[:]),
                                 reads=[yn, ident], writes=[self.p_x])
                        st = yst[q4 % 2]
                        k.op(k.act, lambda e, st=st: e.activation(out=st[:].rearrange("p a t -> p (a t)"), in_=pyb[:, 0:512], func=AF.Copy),
                             reads=[self.p_x], writes=[st])
                        k.dma(k.sp, self.YT_d.ap()[q4 * 512:(q4 + 1) * 512, t0:t0 + 128].rearrange("(a p) t -> p a t", p=128),
                              st[:], st, reads=[st])
            k.barrier()
        self.load_weights_cast(None, self.din["d_w_out"][j0], 16, D, 0)
        for (c0, W, o0, o1, fl, fr) in self.windows(0):
            srcap = self.aps[src][:, c0 + PAD:c0 + PAD + W].rearrange("(k p) w -> p k w", p=128)
            k.dma(k.sp, self.xt[:, :, 0:W], srcap, self.xt, reads=self.regions(src, c0 + PAD, c0 + PAD + W), writes=[self.xt])
            for m in range(16):
                k.dma(k.sp, self.hmid[m][:, 0:W], self.YT_d.ap()[m * 128:(m + 1) * 128, c0:c0 + W], self.hmid[m],
                      writes=[self.hmid[m]])
            self.back_out(dst, 0, 16, self.hmid, W, o0, o1, False, h=0)

    def pass_conv(self, layer, src, dst):
        k = self.k
        j = layer // 4
        in_off = 0
        out_off = KC * 3 * D
        self.load_weights_cast(None, self.din["c_w_in"][j], KC, 3 * D, in_off)
        self.load_weights_cast(None, self.din["c_w_out"][j], KC, D, out_off)
        cv = self.cvec
        k.dma(k.sp, cv[:, 0:3, 0:KC], self.din["c_cv"], cv, writes=[cv])
        for (c0, W, o0, o1, fl, fr) in self.windows(1):
            self.front(src, c0, W, layer * 2 + 0, fl, fr)
            Wo = W - 2
            for m in range(KC):
                pb, pc, px = (self.p_x, self.p_ss)[m % 2], self.p_a[m % 2], self.p_u[m % 2]
                for (pp, coff) in ((pc, D + m * 128), (px, 2 * D + m * 128), (pb, m * 128)):
                    for kk in range(KC):
                        o = in_off + kk * 3 * D + coff
                        k.op(k.pe, lambda e, kk=kk, o=o, pp=pp: e.matmul(
                            pp[:, 0:W], self.warena[:, o:o + 128], self.hT[kk][:, 0:W],
                            start=(kk == 0), stop=(kk == KC - 1)),
                            reads=[self.hT[kk], self.wbuf], writes=[pp], signal=(kk == KC - 1))
                g = self.gbuf[m % 3]
                g2 = self.xo[m % 3]
                k.op(k.act, lambda e, pc=pc, g=g: e.activation(out=g[:, 0:W], in_=pc[:, 0:W], func=AF.Copy),
                     reads=[pc], writes=[g])
                k.op(k.dve, lambda e, px=px, g=g: e.tensor_tensor(out=g[:, 0:W], in0=g[:, 0:W], in1=px[:, 0:W],
                                                                  op=ALU.mult), reads=[g, px], writes=[g])
                k.op(k.dve, lambda e, m=m, g=g, g2=g2: e.tensor_scalar(
                    out=g2[:, 0:Wo], in0=g[:, 1:W - 1], scalar1=cv[:, 1, m:m + 1], scalar2=None, op0=ALU.mult),
                    reads=[g, cv], writes=[g2])
                k.op(k.dve, lambda e, m=m, g=g, g2=g2: e.scalar_tensor_tensor(
                    out=g2[:, 0:Wo], in0=g[:, 0:W - 2], scalar=cv[:, 0, m:m + 1], in1=g2[:, 0:Wo],
                    op0=ALU.mult, op1=ALU.add), reads=[g, cv, g2], writes=[g2])
                k.op(k.dve, lambda e, m=m, g=g, g2=g2: e.scalar_tensor_tensor(
                    out=g2[:, 0:Wo], in0=g[:, 2:W], scalar=cv[:, 2, m:m + 1], in1=g2[:, 0:Wo],
                    op0=ALU.mult, op1=ALU.add), reads=[g, cv, g2], writes=[g2])
                k.op(k.dve, lambda e, m=m, pb=pb, g2=g2: e.tensor_tensor(
                    out=self.hmid[m][:, 1:W - 1], in0=g2[:, 0:Wo], in1=pb[:, 1:W - 1], op=ALU.mult),
                    reads=[g2, pb], writes=[self.hmid[m]])
            self.back_out(dst, out_off, KC, self.hmid, W, o0, o1, False)


_PROG_CACHE = {}


def core_tokens(x_prompt, x_sample, c):
    if c < 4:
        return np.concatenate([x_prompt[c], x_sample[c]], axis=0), 1.0
    b = 4 + 3 * (c - 4)
    return np.concatenate([x_sample[b], x_sample[b + 1], x_sample[b + 2]], axis=0), 0.0


def host_layout(inputs):
    f = lambda a: np.asarray(a, np.float32)
    out = {}
    for nm in ("c_w_in", "c_w_out", "ffn_w_up", "ffn_w_down", "a_w_qkv", "a_w_o", "b_w_qkv", "b_w_o"):
        out[nm] = np.ascontiguousarray(f(inputs[nm]))
    ng = np.concatenate([f(inputs["norm_g"]).reshape(8, D), f(inputs["final_g"]).reshape(1, D)], 0)
    out["normg_h"] = np.ascontiguousarray(ng.reshape(9, KC, 128).transpose(2, 0, 1))
    cw = f(inputs["ffn_conv_w"]).reshape(4, 3, MC, 128)
    cb = f(inputs["ffn_conv_b"]).reshape(4, 1, MC, 128)
    out["ffn_cv"] = np.ascontiguousarray(np.concatenate([cw, cb], 1).transpose(0, 3, 1, 2))
    out["relb_rep"] = np.ascontiguousarray(np.broadcast_to(f(inputs["rel_bias"]).reshape(1, 512), (128, 512)))
    sk = f(inputs["b_sink"])[0].reshape(8, 2)
    out["sink_h"] = np.ascontiguousarray(np.repeat(sk.T, 64, axis=0))
    out["oh_mats"] = _OH_MATS
    for nm in ("d_w_in", "d_w_out"):
        out[nm] = np.ascontiguousarray(f(inputs[nm]))
    dw = f(inputs["d_conv_w"])[0].reshape(4, 24, 128)
    db = f(inputs["d_conv_b"])[0].reshape(1, 24, 128)
    out["d_cv"] = np.ascontiguousarray(np.concatenate([dw, db], 0).transpose(2, 0, 1))
    ii = np.arange(128)
    U = (ii[:, None] <= ii[None, :]).astype(np.float32)
    L = (ii[:, None] >= ii[None, :]).astype(np.float32)
    out["d_consts"] = np.ascontiguousarray(np.stack([U, L, (U - 1.0) * 1e4, (L - 1.0) * 1e4, np.eye(128, dtype=np.float32)], 1))
    rep = lambda v: np.ascontiguousarray(np.broadcast_to(v.reshape(1, -1), (128, v.size)))
    out["d_dtb_rep"] = rep(f(inputs["d_dt_bias"])[0])
    out["d_alog_rep"] = rep(f(inputs["d_a_log"])[0])
    out["d_skip_rep"] = rep(f(inputs["d_skip"])[0])
    out["d_normg_rep"] = rep(f(inputs["d_norm_g"])[0])
    out["c_cv"] = np.ascontiguousarray(f(inputs["c_conv_w"])[0].reshape(3, KC, 128).transpose(2, 0, 1))
    return out


def run(inputs, seg, layers=(0, 1, 2, 3), do_final=True):
    key = (seg, tuple(layers), do_final)
    prog = Prog(seg, layers, do_final)
    NT = 3 * seg
    xp = np.asarray(inputs["x_prompt"], np.float32)
    xs = np.asarray(inputs["x_sample"], np.float32)
    in_maps = []
    for c in range(8):
        tok, J = core_tokens(xp, xs, c)
        xT = np.zeros((D, NT + 2 * PAD), np.float32)
        xT[:, PAD:NT + PAD] = tok.T
        fl = np.zeros((128, 8), np.float32)
        fl[:, 1] = J
        fl[:, 2] = J
        f2 = np.ones((128, 12), np.float32)
        for sgi, (fL, fR) in enumerate(((0.0, J), (J, 0.0), (0.0, 0.0))):
            f2[0:64, sgi * 4 + 0] = fL
            f2[:, sgi * 4 + 1] = fL
            f2[64:128, sgi * 4 + 2] = fR
            f2[:, sgi * 4 + 3] = fR
        m = {"xT": xT, "flags": fl, "flags2": f2}
        m.update(host_layout(inputs))
        m = {kk: v for kk, v in m.items() if kk in prog.din}
        assert set(m) == set(prog.din), (set(prog.din) - set(m))
        in_maps.append(m)
    res = run_bass_kernel_spmd(prog.k.nc, in_maps, core_ids=list(range(8)))
    yp = np.zeros_like(xp)
    ys = np.zeros_like(xs)
    for c in range(8):
        y = np.asarray(res.results[c]["yT"]).T
        if c < 4:
            yp[c] = y[:2 * seg]
            ys[c] = y[2 * seg:]
        else:
            b = 4 + 3 * (c - 4)
            for i in range(3):
                ys[b + i] = y[i * seg:(i + 1) * seg]
    return yp, ys


def kernel(**inputs):
    return run(inputs, 4096)
```
